# Optimizing a Trainium2 kernel written in Bass

```python
import jax, jax.numpy as jnp
from jax import lax
import numpy as np

D_MODEL = 2048
BATCH = 4
SEQ = 4096
DEPTH = 2

RW_HEAD = 64
RW_WIDTH = D_MODEL
RW_HEADS = RW_WIDTH // RW_HEAD
RW_DECAY_LORA = 96
RW_AAA_LORA = 96
RW_MV_LORA = 64
RW_GATE_LORA = 256
RW_GN_EPS = RW_HEAD * 1e-5
GM_WIDTH = D_MODEL
GM_CHUNK = 128
GM_GROUP = 128
GM_GROUPS = GM_WIDTH // GM_GROUP
NSA_HEADS = 16
NSA_KV_GROUPS = 4
NSA_HPG = NSA_HEADS // NSA_KV_GROUPS
NSA_DK = 192
NSA_DV = 128
NSA_WIDTH = NSA_HEADS * NSA_DV
CMP_BLK = 32
CMP_STRIDE = 16
SEL_BLK = 64
N_SEL = 16
WIN = 512
Q_BLK = 32
D_FF = 5632
N_BRANCH = 3
BRANCH_WIDTH = D_MODEL
ALPHA = (2 * DEPTH) ** 0.25
BETA = (8 * DEPTH) ** -0.25
LN_EPS = 1e-5
NEG = -1e30
FORCED = 1e6

RW_COLS = 3 * RW_WIDTH + RW_DECAY_LORA + RW_AAA_LORA + RW_GATE_LORA
GM_COLS = 2 * GM_WIDTH
NSA_Q_COLS = NSA_HEADS * NSA_DK
NSA_GK = NSA_KV_GROUPS * NSA_DK
NSA_GV = NSA_KV_GROUPS * NSA_DV
NSA_KV_COLS = 3 * (NSA_GK + NSA_GV)
NSA_G_COLS = 3 * NSA_HEADS
NSA_COLS = NSA_Q_COLS + NSA_KV_COLS + NSA_G_COLS
GATE_COLS = N_BRANCH * D_MODEL
OFF_GM = RW_COLS
OFF_NSA = OFF_GM + GM_COLS
OFF_GATE = OFF_NSA + NSA_COLS
C_IN = OFF_GATE + GATE_COLS

kernel_name = 'hybrid_rwkv7_gmlp_nsa_macaron_deepnorm'


def layer_norm(x, g, b):
    xf = x.astype(jnp.float32)
    mu = jnp.mean(xf, -1, keepdims=True)
    var = jnp.mean(jnp.square(xf - mu), -1, keepdims=True)
    return ((xf - mu) * lax.rsqrt(var + LN_EPS) * g + b).astype(x.dtype)


def masked_softmax(s, mask):
    s = jnp.where(mask, s.astype(jnp.float32), NEG)
    e = jnp.where(mask, jnp.exp(s - jnp.max(s, -1, keepdims=True)), 0.0)
    return e / jnp.maximum(jnp.sum(e, -1, keepdims=True), 1e-30)


def swiglu(x, wg, wu, wd):
    return (jax.nn.silu(x @ wg) * (x @ wu)) @ wd


def wkv7_scan(r, w, k, v, a, b):
    B, T, H, N = r.shape

    def step(S, inp):
        r_t, w_t, k_t, v_t, a_t, b_t = inp
        sa = jnp.einsum('bhij,bhj->bhi', S, a_t)
        S = S * w_t[:, :, None, :] + sa[..., None] * b_t[:, :, None, :] + v_t[..., None] * k_t[:, :, None, :]
        return S, jnp.einsum('bhij,bhj->bhi', S, r_t)

    xs = [jnp.moveaxis(z, 1, 0) for z in (r, w, k, v, a, b)]
    _, y = lax.scan(step, jnp.zeros((B, H, N, N), jnp.float32), xs)
    return jnp.moveaxis(y, 0, 1)


def rwkv7_mix(p, v_first, vres, mu, w0, w2, a0, a2, g2, k_k, k_a, r_k, gn_g, gn_b):
    B, T, _ = p.shape
    prev = jnp.pad(p, ((0, 0), (1, 0), (0, 0)))[:, :-1]
    p = p + (prev - p) * mu
    cut = [RW_WIDTH, 2 * RW_WIDTH, 3 * RW_WIDTH, 3 * RW_WIDTH + RW_DECAY_LORA,
           3 * RW_WIDTH + RW_DECAY_LORA + RW_AAA_LORA]
    r, k, v, wl, al, gl = jnp.split(p, cut, axis=-1)
    w = -jax.nn.softplus(-(w0 + jnp.tanh(wl) @ w2)) - 0.5
    a = jax.nn.sigmoid(a0 + al @ a2)
    g = jax.nn.sigmoid(gl) @ g2
    if vres is None:
        v_first = v
    else:
        v0, v1, v2 = vres
        v = v + (v_first - v) * jax.nn.sigmoid(v0 + (v @ v1) @ v2)

    def hs(z):
        return z.reshape(B, T, RW_HEADS, RW_HEAD).astype(jnp.float32)

    kk = hs(k * k_k)
    kk = kk * lax.rsqrt(jnp.maximum(jnp.sum(jnp.square(kk), -1, keepdims=True), 1e-24))
    k = k * (1.0 + (a - 1.0) * k_a)
    rh, kh, vh, ah = hs(r), hs(k), hs(v), hs(a)
    decay = jnp.exp(-jnp.exp(hs(w)))
    y = wkv7_scan(rh, decay, kh, vh, -kk, kk * ah)
    m = jnp.mean(y, -1, keepdims=True)
    var = jnp.mean(jnp.square(y - m), -1, keepdims=True)
    y = ((y - m) * lax.rsqrt(var + RW_GN_EPS)).reshape(B, T, RW_WIDTH) * gn_g + gn_b
    bonus = jnp.sum(rh * kh * r_k, -1, keepdims=True) * vh
    y = (y + bonus.reshape(B, T, RW_WIDTH)) * g
    return y.astype(p.dtype), v_first


def gmlp_mix(p, ln_g, ln_b, ws, bs):
    B, T, _ = p.shape
    u, v = jnp.split(jax.nn.gelu(p), 2, axis=-1)
    v = layer_norm(v, ln_g, ln_b).reshape(B, T // GM_CHUNK, GM_CHUNK, GM_GROUPS, GM_GROUP)
    causal = jnp.tril(jnp.ones((GM_CHUNK, GM_CHUNK), ws.dtype))
    s = jnp.einsum('gts,bcsgd->bctgd', ws * causal, v) + bs.T[None, None, :, :, None]
    return u * s.reshape(B, T, GM_WIDTH)


def compress(z, pos, w1, w2):
    B, T, G, d = z.shape
    n_c = (T - CMP_BLK) // CMP_STRIDE + 1
    idx = np.arange(n_c)[:, None] * CMP_STRIDE + np.arange(CMP_BLK)[None, :]
    blocks = z[:, idx] + pos[None, None, :, None, :]
    flat = blocks.transpose(0, 3, 1, 2, 4).reshape(B, G, n_c, CMP_BLK * d)
    return jax.nn.gelu(flat @ w1) @ w2


def nsa_mix(p, pos_k, pos_v, phi_k1, phi_k2, phi_v1, phi_v2):
    B, T, _ = p.shape
    G, h = NSA_KV_GROUPS, NSA_HPG
    q = p[..., :NSA_Q_COLS].reshape(B, T, G, h, NSA_DK).transpose(0, 2, 3, 1, 4)
    kv = p[..., NSA_Q_COLS:NSA_Q_COLS + NSA_KV_COLS]
    cut = [NSA_GK, NSA_GK + NSA_GV, 2 * NSA_GK + NSA_GV, 2 * NSA_GK + 2 * NSA_GV, 3 * NSA_GK + 2 * NSA_GV]
    kc, vc, ks, vs, kw, vw = jnp.split(kv, cut, axis=-1)
    gates = jax.nn.sigmoid(p[..., NSA_Q_COLS + NSA_KV_COLS:]).reshape(B, T, G, h, 3).transpose(0, 2, 3, 1, 4)

    k_cmp = compress(kc.reshape(B, T, G, NSA_DK), pos_k, phi_k1, phi_k2)
    v_cmp = compress(vc.reshape(B, T, G, NSA_DV), pos_v, phi_v1, phi_v2)
    n_c = k_cmp.shape[2]
    n_s = T // SEL_BLK
    k_sel = min(N_SEL, n_s)
    ks_blk = ks.reshape(B, T, G, NSA_DK).transpose(0, 2, 1, 3).reshape(B, G, n_s, SEL_BLK, NSA_DK)
    vs_blk = vs.reshape(B, T, G, NSA_DV).transpose(0, 2, 1, 3).reshape(B, G, n_s, SEL_BLK, NSA_DV)
    pad = ((0, 0), (0, 0), (WIN, 0), (0, 0))
    kw_pad = jnp.pad(kw.reshape(B, T, G, NSA_DK).transpose(0, 2, 1, 3), pad)
    vw_pad = jnp.pad(vw.reshape(B, T, G, NSA_DV).transpose(0, 2, 1, 3), pad)

    cmp_start = jnp.arange(n_c) * CMP_STRIDE
    cmp_end = cmp_start + CMP_BLK - 1
    blk = jnp.arange(n_s)
    sel_start = blk * SEL_BLK
    overlap = ((cmp_start[:, None] < sel_start[None, :] + SEL_BLK)
               & (cmp_end[:, None] >= sel_start[None, :])).astype(jnp.float32)
    bi = jnp.arange(B)[:, None, None, None]
    gi = jnp.arange(G)[None, :, None, None]
    scale = NSA_DK ** -0.5

    def block(i):
        t0 = i * Q_BLK
        tq = t0 + jnp.arange(Q_BLK)
        qb = lax.dynamic_slice_in_dim(q, t0, Q_BLK, axis=3)
        gb = lax.dynamic_slice_in_dim(gates, t0, Q_BLK, axis=3)
        s_c = jnp.einsum('bghqd,bgnd->bghqn', qb, k_cmp) * scale
        p_c = masked_softmax(s_c, cmp_end[None, :] <= tq[:, None])
        o_c = jnp.einsum('bghqn,bgnd->bghqd', p_c, v_cmp)
        imp = jnp.einsum('bghqn,ns->bgqs', p_c, overlap)
        valid = sel_start[None, :] <= tq[:, None]
        cur = (tq // SEL_BLK)[:, None]
        forced = valid & ((blk[None, :] == 0) | (blk[None, :] == cur) | (blk[None, :] == cur - 1))
        score = jnp.where(forced, FORCED, jnp.where(valid, imp, NEG))
        top_val, top_idx = lax.top_k(score, k_sel)
        sel_ok = top_val > 0.5 * NEG
        kg = ks_blk[bi, gi, top_idx]
        vg = vs_blk[bi, gi, top_idx]
        s_s = jnp.einsum('bghqd,bgqkld->bghqkl', qb, kg) * scale
        kpos = top_idx[..., None] * SEL_BLK + jnp.arange(SEL_BLK)
        m_s = sel_ok[..., None] & (kpos <= tq[:, None, None])
        p_s = masked_softmax(s_s.reshape(B, G, h, Q_BLK, k_sel * SEL_BLK),
                             m_s.reshape(B, G, 1, Q_BLK, k_sel * SEL_BLK)).reshape(s_s.shape)
        o_s = jnp.einsum('bghqkl,bgqkld->bghqd', p_s, vg)
        kwb = lax.dynamic_slice_in_dim(kw_pad, t0, Q_BLK + WIN, axis=2)
        vwb = lax.dynamic_slice_in_dim(vw_pad, t0, Q_BLK + WIN, axis=2)
        kpos_w = t0 - WIN + jnp.arange(Q_BLK + WIN)
        m_w = ((kpos_w[None, :] <= tq[:, None]) & (kpos_w[None, :] > tq[:, None] - WIN)
               & (kpos_w[None, :] >= 0))
        s_w = jnp.einsum('bghqd,bgkd->bghqk', qb, kwb) * scale
        o_w = jnp.einsum('bghqk,bgkd->bghqd', masked_softmax(s_w, m_w), vwb)
        return gb[..., 0:1] * o_c + gb[..., 1:2] * o_s + gb[..., 2:3] * o_w

    out = lax.map(block, jnp.arange(T // Q_BLK))
    return out.transpose(1, 0, 4, 2, 3, 5).reshape(B, T, NSA_WIDTH).astype(p.dtype)


def setup_inputs(seed: int = 0) -> dict:
    key = jax.random.key(seed)
    ks = iter(jax.random.split(key, 48))
    L, D = DEPTH, D_MODEL

    def nrm(shape, scale):
        return jax.random.normal(next(ks), shape, jnp.float32) * scale

    ramp = (jnp.arange(RW_WIDTH, dtype=jnp.float32) / (RW_WIDTH - 1)) ** 0.85
    return {
        'x': nrm((BATCH, SEQ, D), 1.0),
        'w_in': nrm((L, D, C_IN), D ** -0.5),
        'rw_mu': jax.random.uniform(next(ks), (L, RW_COLS), jnp.float32),
        'rw_w0': -6.0 + 5.0 * ramp + nrm((L, RW_WIDTH), 0.1),
        'rw_w2': nrm((L, RW_DECAY_LORA, RW_WIDTH), 0.5 * RW_DECAY_LORA ** -0.5),
        'rw_a0': nrm((L, RW_WIDTH), 0.1),
        'rw_a2': nrm((L, RW_AAA_LORA, RW_WIDTH), RW_AAA_LORA ** -0.5),
        'rw_g2': nrm((L, RW_GATE_LORA, RW_WIDTH), RW_GATE_LORA ** -0.5),
        'rw_v0': nrm((L - 1, RW_WIDTH), 0.1),
        'rw_v1': nrm((L - 1, RW_WIDTH, RW_MV_LORA), RW_WIDTH ** -0.5),
        'rw_v2': nrm((L - 1, RW_MV_LORA, RW_WIDTH), RW_MV_LORA ** -0.5),
        'rw_k_k': 0.85 + nrm((L, RW_WIDTH), 0.05),
        'rw_k_a': 1.0 + nrm((L, RW_WIDTH), 0.05),
        'rw_r_k': nrm((L, RW_HEADS, RW_HEAD), 0.1),
        'rw_gn_g': 1.0 + nrm((L, RW_WIDTH), 0.02),
        'rw_gn_b': nrm((L, RW_WIDTH), 0.02),
        'gm_ln_g': 1.0 + nrm((L, GM_WIDTH), 0.02),
        'gm_ln_b': nrm((L, GM_WIDTH), 0.02),
        'gm_ws': nrm((L, GM_GROUPS, GM_CHUNK, GM_CHUNK), GM_CHUNK ** -0.5),
        'gm_bs': 1.0 + nrm((L, GM_GROUPS, GM_CHUNK), 0.02),
        'nsa_pos_k': nrm((L, CMP_BLK, NSA_DK), 0.02),
        'nsa_pos_v': nrm((L, CMP_BLK, NSA_DV), 0.02),
        'nsa_phi_k1': nrm((L, CMP_BLK * NSA_DK, NSA_DK), (CMP_BLK * NSA_DK) ** -0.5),
        'nsa_phi_k2': nrm((L, NSA_DK, NSA_DK), NSA_DK ** -0.5),
        'nsa_phi_v1': nrm((L, CMP_BLK * NSA_DV, NSA_DV), (CMP_BLK * NSA_DV) ** -0.5),
        'nsa_phi_v2': nrm((L, NSA_DV, NSA_DV), NSA_DV ** -0.5),
        'w_br': nrm((L, N_BRANCH, BRANCH_WIDTH, D), BRANCH_WIDTH ** -0.5),
        'w_o': nrm((L, D, D), BETA * D ** -0.5),
        'ffn1_wg': nrm((L, D, D_FF), D ** -0.5),
        'ffn1_wu': nrm((L, D, D_FF), D ** -0.5),
        'ffn1_wd': nrm((L, D_FF, D), BETA * D_FF ** -0.5),
        'ffn2_wg': nrm((L, D, D_FF), D ** -0.5),
        'ffn2_wu': nrm((L, D, D_FF), D ** -0.5),
        'ffn2_wd': nrm((L, D_FF, D), BETA * D_FF ** -0.5),
        'ln_g': 1.0 + nrm((L, 3, D), 0.02),
        'ln_b': nrm((L, 3, D), 0.02),
    }


def reference(x, w_in, rw_mu, rw_w0, rw_w2, rw_a0, rw_a2, rw_g2, rw_v0, rw_v1, rw_v2,
              rw_k_k, rw_k_a, rw_r_k, rw_gn_g, rw_gn_b, gm_ln_g, gm_ln_b, gm_ws, gm_bs,
              nsa_pos_k, nsa_pos_v, nsa_phi_k1, nsa_phi_k2, nsa_phi_v1, nsa_phi_v2,
              w_br, w_o, ffn1_wg, ffn1_wu, ffn1_wd, ffn2_wg, ffn2_wu, ffn2_wd, ln_g, ln_b):
    B, T, D = x.shape
    v_first = None
    for l in range(DEPTH):
        x = layer_norm(ALPHA * x + 0.5 * swiglu(x, ffn1_wg[l], ffn1_wu[l], ffn1_wd[l]), ln_g[l, 0], ln_b[l, 0])
        wl = w_in[l]
        vres = None if l == 0 else (rw_v0[l - 1], rw_v1[l - 1], rw_v2[l - 1])
        y_rw, v_first = rwkv7_mix(x @ wl[:, :OFF_GM], v_first, vres, rw_mu[l], rw_w0[l], rw_w2[l],
                                  rw_a0[l], rw_a2[l], rw_g2[l], rw_k_k[l], rw_k_a[l], rw_r_k[l],
                                  rw_gn_g[l], rw_gn_b[l])
        y_gm = gmlp_mix(x @ wl[:, OFF_GM:OFF_NSA], gm_ln_g[l], gm_ln_b[l], gm_ws[l], gm_bs[l])
        y_ns = nsa_mix(x @ wl[:, OFF_NSA:OFF_GATE], nsa_pos_k[l], nsa_pos_v[l], nsa_phi_k1[l],
                       nsa_phi_k2[l], nsa_phi_v1[l], nsa_phi_v2[l])
        gate = jax.nn.sigmoid(x @ wl[:, OFF_GATE:]).reshape(B, T, N_BRANCH, D)
        merged = (gate[:, :, 0] * (y_rw @ w_br[l, 0]) + gate[:, :, 1] * (y_gm @ w_br[l, 1])
                  + gate[:, :, 2] * (y_ns @ w_br[l, 2]))
        x = layer_norm(ALPHA * x + merged @ w_o[l], ln_g[l, 1], ln_b[l, 1])
        x = layer_norm(ALPHA * x + 0.5 * swiglu(x, ffn2_wg[l], ffn2_wu[l], ffn2_wd[l]), ln_g[l, 2], ln_b[l, 2])
    return x
```

```python
import numpy as np
from contextlib import ExitStack
import concourse.bass as bass
import concourse.mybir as mybir
from concourse.bass_utils import run_bass_kernel_spmd

F32 = mybir.dt.float32
BF16 = mybir.dt.bfloat16
I32 = mybir.dt.int32
U8 = mybir.dt.uint8
AF = mybir.ActivationFunctionType
ALU = mybir.AluOpType
AX = mybir.AxisListType

_DS = {F32: 4, BF16: 2, I32: 4, U8: 1, mybir.dt.uint32: 4, mybir.dt.float32r: 4,
       mybir.dt.uint16: 2, mybir.dt.int16: 2}


def _foot(ap):
    name = ap.tensor.name
    es = _DS.get(ap.dtype, 4)
    dims = list(ap.ap)
    if type(ap.tensor).__name__ == 'DRamTensorHandle':
        lo = ap.offset
        hi = lo + sum((c - 1) * abs(s) for s, c in dims) + 1
        return (name, 0, 1, lo * es, hi * es)
    is_psum = 'PSum' in type(ap.tensor).__name__ or 'Psum' in type(ap.tensor).__name__
    pstep, pcnt = dims[0]
    if pstep == 0:
        pstep = None
    free = dims[1:]
    ext = sum((c - 1) * abs(s) for s, c in free) + 1
    if pstep:
        f0 = ap.offset % pstep
        p0 = ap.offset // pstep
    else:
        tsh = ap.tensor.shape
        ps_ = 1
        for d in tsh[1:]:
            ps_ *= d
        tes = _DS.get(ap.tensor.dtype, 4)
        ps_ = ps_ * tes // es
        f0 = ap.offset % ps_
        p0 = ap.offset // ps_
        pcnt = 1
    if is_psum:
        return (name, (p0 // 32) * 32, ((p0 + pcnt + 31) // 32) * 32, 0, 1 << 40)
    return (name, p0, p0 + pcnt, f0 * es, (f0 + ext) * es)


class Ctx:
    NDMA = 24

    def __init__(self, nc):
        self.nc = nc
        self.es = ExitStack()
        self.eng = {'pe': nc.tensor, 'act': nc.scalar, 'dve': nc.vector, 'pool': nc.gpsimd, 'sp': nc.sync}
        self.sem = {}
        self.cnt = {}
        for e in ['pe', 'act', 'dve', 'pool']:
            self.sem[e] = self.es.enter_context(nc.semaphore('s_' + e))
            self.cnt[e] = 0
        for i in range(self.NDMA):
            k = 'd%d' % i
            self.sem[k] = self.es.enter_context(nc.semaphore('s_' + k))
            self.cnt[k] = 0
        self.dma_rr = 0
        self.known = {e: {} for e in ['pe', 'act', 'dve', 'pool', 'sp']}
        self.rec = {}
        self.ninst = 0

    def sb(self, name, shape, dt=F32, stack=None):
        self.uid = getattr(self, 'uid', 0) + 1
        return (stack or self.es).enter_context(self.nc.sbuf_tensor('%s_%d' % (name, self.uid), list(shape), dt))

    def ps(self, name, shape, dt=F32, stack=None):
        self.uid = getattr(self, 'uid', 0) + 1
        return (stack or self.es).enter_context(self.nc.psum_tensor('%s_%d' % (name, self.uid), list(shape), dt))

    def barrier(self):
        deps = {k: c for k, c in self.cnt.items() if c > 0}
        for e in ['pe', 'act', 'dve', 'pool', 'sp']:
            self._waits(e, dict(deps))

    def _deps(self, reads, writes, me):
        deps = {}
        for ap in reads:
            f = _foot(ap)
            for r in self.rec.get(f[0], ()):
                if r[6] and r[0] < f[2] and f[1] < r[1] and r[2] < f[4] and f[3] < r[3]:
                    if r[5] > deps.get(r[4], 0):
                        deps[r[4]] = r[5]
        for ap in writes:
            f = _foot(ap)
            for r in self.rec.get(f[0], ()):
                if r[0] < f[2] and f[1] < r[1] and r[2] < f[4] and f[3] < r[3]:
                    if r[4] == me and not r[6]:
                        continue
                    if r[5] > deps.get(r[4], 0):
                        deps[r[4]] = r[5]
        if me == 'pe':
            deps.pop('pe', None)
        return deps

    def _record(self, reads, writes, me, cnt):
        for ap in writes:
            f = _foot(ap)
            lst = self.rec.setdefault(f[0], [])
            lst[:] = [r for r in lst if not (f[1] <= r[0] and r[1] <= f[2] and f[3] <= r[2] and r[3] <= f[4])]
            lst.append([f[1], f[2], f[3], f[4], me, cnt, True])
        for ap in reads:
            f = _foot(ap)
            lst = self.rec.setdefault(f[0], [])
            lst[:] = [r for r in lst if not (r[4] == me and not r[6] and f[1] <= r[0] and r[1] <= f[2]
                                             and f[3] <= r[2] and r[3] <= f[4])]
            lst.append([f[1], f[2], f[3], f[4], me, cnt, False])

    def _waits(self, issuer, deps):
        e = self.eng[issuer]
        kn = self.known[issuer]
        for k, c in deps.items():
            if kn.get(k, 0) < c:
                e.wait_ge(self.sem[k], c)
                kn[k] = c

    def op(self, engname, fn, reads, writes):
        xr = [a for a in reads if 'PSum' in type(a.tensor).__name__]
        if xr:
            writes = list(writes) + xr
            for a in xr:
                self._ps_guard_read(a)
        deps = self._deps(reads, writes, engname)
        self._waits(engname, deps)
        inst = fn()
        self.cnt[engname] += 1
        inst.then_inc(self.sem[engname], 1)
        self._record(reads, writes, engname, self.cnt[engname])
        self.ninst += 1
        return inst

    def dma(self, out, in_, q='sp', **kw):
        k = 'd%d' % self.dma_rr
        self.dma_rr = (self.dma_rr + 1) % self.NDMA
        deps = self._deps([in_], [out], k)
        if self.cnt[k] > 0:
            deps[k] = max(deps.get(k, 0), self.cnt[k])
        self._waits(q, deps)
        inst = self.eng[q].dma_start(out=out, in_=in_, **kw)
        self.cnt[k] += 16
        inst.then_inc(self.sem[k], 16)
        self._record([in_], [out], k, self.cnt[k])
        self.ninst += 1
        return inst

    def wait_all(self, issuer='sp'):
        deps = {k: c for k, c in self.cnt.items() if c > 0}
        self._waits(issuer, deps)

    def _r(self, out):
        r32 = self.__dict__.get('r32', ())
        if r32 and out.dtype == F32 and out.tensor.name in r32:
            return out.bitcast(mybir.dt.float32r)
        return out

    def _ps_cols(self, ap):
        dims = list(ap.ap)
        es = _DS.get(ap.dtype, 4)
        pstep, pcnt = dims[0]
        ext = sum((c - 1) * abs(st) for st, c in dims[1:]) + 1
        f0 = ap.offset % pstep if pstep else 0
        p0 = ap.offset // pstep if pstep else 0
        return ap.tensor.name, p0, p0 + pcnt, f0 * es, (f0 + ext) * es

    def _ps_guard_write(self, out, start):
        name, p0, p1, b0, b1 = self._ps_cols(out)
        lst = self.__dict__.setdefault('pw', {}).setdefault(name, [])
        if start:
            for r in lst:
                assert not (r[0] < p1 and p0 < r[1] and r[2] < b1 and b0 < r[3]), \
                    'PSUM overwrite of unread matmul result in %s %s' % (name, (p0, p1, b0, b1, r))
            lst.append([p0, p1, b0, b1])

    def _ps_guard_read(self, ap):
        name, p0, p1, b0, b1 = self._ps_cols(ap)
        lst = self.__dict__.setdefault('pw', {}).get(name)
        if lst:
            lst[:] = [r for r in lst if not (r[0] < p1 and p0 < r[1] and r[2] < b1 and b0 < r[3])]

    def mm(self, out, lhsT, rhs, start=True, stop=True):
        self._ps_guard_write(out, start)
        r32 = self.__dict__.get('r32', ())
        if (r32 and lhsT.dtype == F32 and rhs.dtype == F32 and lhsT.tensor.name in r32 and rhs.tensor.name in r32
                and _foot(out)[1] == 0):
            lhsT = lhsT.bitcast(mybir.dt.float32r)
            rhs = rhs.bitcast(mybir.dt.float32r)
        return self.op('pe', lambda: self.nc.tensor.matmul(out, lhsT, rhs, start=start, stop=stop), [lhsT, rhs], [out])

    def tr(self, out, in_, ident):
        self._ps_guard_write(out, True)
        return self.op('pe', lambda: self.nc.tensor.transpose(out, in_, ident), [in_, ident], [out])

    def act(self, out, in_, func, bias=None, scale=None, accum_out=None, eng='act'):
        out = self._r(out)
        kw = {}
        rd = [in_]
        if bias is not None:
            kw['bias'] = bias
            if not isinstance(bias, (int, float)):
                rd.append(bias)
        if scale is not None:
            kw['scale'] = scale
            if not isinstance(scale, (int, float)):
                rd.append(scale)
        wr = [out]
        if accum_out is not None:
            kw['accum_out'] = accum_out
            wr.append(accum_out)
        return self.op('act', lambda: self.nc.scalar.activation(out, in_, func, **kw), rd, wr)

    def tt(self, out, in0, in1, op, eng='dve'):
        out = self._r(out)
        return self.op(eng, lambda: self.eng[eng].tensor_tensor(out, in0, in1, op), [in0, in1], [out])

    def ts(self, out, in0, s1, s2=None, op0=ALU.mult, op1=None, eng='dve', accum_out=None):
        out = self._r(out)
        rd = [in0]
        if not isinstance(s1, (int, float)):
            rd.append(s1)
        if s2 is not None and not isinstance(s2, (int, float)):
            rd.append(s2)
        kw = {}
        wr = [out]
        if accum_out is not None:
            kw['accum_out'] = accum_out
            wr.append(accum_out)
        if op1 is None:
            return self.op(eng, lambda: self.eng[eng].tensor_scalar(out, in0, s1, None, op0, **kw), rd, wr)
        return self.op(eng, lambda: self.eng[eng].tensor_scalar(out, in0, s1, s2, op0, op1, **kw), rd, wr)

    def stt(self, out, in0, scalar, in1, op0, op1, accum_out=None):
        out = self._r(out)
        rd = [in0, in1]
        if not isinstance(scalar, (int, float)):
            rd.append(scalar)
        wr = [out]
        kw = {}
        if accum_out is not None:
            kw['accum_out'] = accum_out
            wr.append(accum_out)
        return self.op('dve', lambda: self.nc.vector.scalar_tensor_tensor(out, in0, scalar, in1, op0, op1, **kw), rd, wr)

    def copy(self, out, in_, eng='dve'):
        out = self._r(out)
        if eng == 'act':
            return self.op('act', lambda: self.nc.scalar.copy(out, in_), [in_], [out])
        return self.op(eng, lambda: self.eng[eng].tensor_copy(out, in_), [in_], [out])

    def memset(self, ap, v, eng='dve'):
        return self.op(eng, lambda: self.eng[eng].memset(ap, v), [], [ap])

    def reduce(self, out, in_, op=ALU.add, axis=AX.X, eng='dve'):
        return self.op(eng, lambda: self.eng[eng].tensor_reduce(out, in_, axis, op), [in_], [out])

    def recip(self, out, in_):
        return self.op('dve', lambda: self.nc.vector.reciprocal(out, in_), [in_], [out])


D = 2048
DFF = 5632
KT = D // 128
FT = DFF // 128
DEPTH = 2
ALPHA = (2 * DEPTH) ** 0.25
LN_EPS = 1e-5
RW_COLS = 6592
OFF_GM = 6592
OFF_NSA = 10688
OFF_GATE = 17648
C_IN = 23792
TT = 512


class Net:
    def __init__(self, T):
        self.T = T
        nc = bass.Bass("TRN2", target_bir_lowering=False)
        self.nc = nc
        self.c = Ctx(nc)
        self.inp = {}
        self.psn = 0

    def din(self, name, shape, dt=F32):
        t = self.nc.dram_tensor(name, list(shape), dt, kind="ExternalInput").ap()
        self.inp[name] = t
        return t

    def dout(self, name, shape, dt=F32):
        return self.nc.dram_tensor(name, list(shape), dt, kind="ExternalOutput").ap()

    def dscr(self, name, shape, dt=F32):
        return self.nc.dram_tensor(name, list(shape), dt, kind="Internal").ap()


def conv_weight(c, cva, cvb, src2d, dst2d, rows, cols, k):
    CH = 4096
    s = src2d.rearrange("(p r) c -> p (r c)", p=128)
    d = dst2d.rearrange("(p r) c -> p (r c)", p=128)
    n = (rows // 128) * cols
    i = 0
    while i < n:
        m = min(CH, n - i)
        a = cva[k % 2]
        b = cvb[k % 2]
        c.dma(a[:, 0:m], s[:, i:i + m], q='sp')
        e = ['act', 'pool', 'dve'][k % 3]
        c.copy(b[:, 0:m], a[:, 0:m], eng=e)
        c.dma(d[:, i:i + m], b[:, 0:m], q='sp')
        i += m
        k += 1
    return k


class PsumPool:
    def __init__(self, c, n=8):
        self.banks = [c.ps("psb%d" % i, [128, 512]) for i in range(n)]
        self.i = 0

    def get(self):
        b = self.banks[self.i % len(self.banks)]
        self.i += 1
        return b


def layer_norm_fm(c, pp, st, z, g, b, ntok, consts):
    zb = st['zb']
    zq = st['zq']
    ones = consts['ones_bf']
    c.copy(zb[:, :, 0:ntok], z[:, :, 0:ntok], eng='act')
    c.act(zq[:, :, 0:ntok], z[:, :, 0:ntok], AF.Square)
    ps1 = pp.get()
    ps2 = pp.get()
    for k in range(KT):
        c.mm(ps1[:, 0:ntok], ones[:, :], zb[:, k, 0:ntok], start=(k == 0), stop=(k == KT - 1))
    for k in range(KT):
        c.mm(ps2[:, 0:ntok], ones[:, :], zq[:, k, 0:ntok], start=(k == 0), stop=(k == KT - 1))
    mean = st['mean']
    rstd = st['rstd']
    tmp = st['tmp512']
    c.act(mean[:, 0:ntok], ps1[:, 0:ntok], AF.Copy, scale=1.0 / D)
    c.tt(tmp[:, 0:ntok], mean[:, 0:ntok], mean[:, 0:ntok], ALU.mult)
    c.stt(rstd[:, 0:ntok], ps2[:, 0:ntok], 1.0 / D, tmp[:, 0:ntok], ALU.mult, ALU.subtract)
    c.ts(rstd[:, 0:ntok], rstd[:, 0:ntok], LN_EPS, None, ALU.add)
    c.act(rstd[:, 0:ntok], rstd[:, 0:ntok], AF.Sqrt)
    c.recip(rstd[:, 0:ntok], rstd[:, 0:ntok])
    for k in range(KT):
        e = 'dve' if k % 2 == 0 else 'pool'
        c.tt(z[:, k, 0:ntok], z[:, k, 0:ntok], mean[:, 0:ntok], ALU.subtract, eng=e)
        c.tt(z[:, k, 0:ntok], z[:, k, 0:ntok], rstd[:, 0:ntok], ALU.mult, eng=e)
        c.ts(z[:, k, 0:ntok], z[:, k, 0:ntok], g[:, k:k + 1], b[:, k:k + 1], ALU.mult, ALU.add, eng='dve')
    c.copy(zb[:, :, 0:ntok], z[:, :, 0:ntok], eng='act')


def ffn_tile(c, pp, st, xs, xb, wg, wu, wd, ntok):
    hT = st['hT']
    FW = 256
    for f0 in range(0, DFF, FW):
        i = (f0 // FW) % 2
        wgs = st['wgs'][i]
        wus = st['wus'][i]
        c.dma(wgs[:], wg[:, f0:f0 + FW].rearrange("(kt p) m -> p kt m", p=128), q='sp')
        c.dma(wus[:], wu[:, f0:f0 + FW].rearrange("(kt p) m -> p kt m", p=128), q='sp')
        for j in range(FW // 128):
            f = f0 // 128 + j
            psg = pp.get()
            psu = pp.get()
            for k in range(KT):
                c.mm(psg[:, 0:ntok], wgs[:, k, j * 128:(j + 1) * 128], xb[:, k, 0:ntok], start=(k == 0), stop=(k == KT - 1))
            for k in range(KT):
                c.mm(psu[:, 0:ntok], wus[:, k, j * 128:(j + 1) * 128], xb[:, k, 0:ntok], start=(k == 0), stop=(k == KT - 1))
            sg = st['sg'][f % 2]
            c.act(sg[:, 0:ntok], psg[:, 0:ntok], AF.Silu)
            c.tt(hT[:, f, 0:ntok], sg[:, 0:ntok], psu[:, 0:ntok], ALU.mult)
    c.ts(xs[:, :, 0:ntok], xs[:, :, 0:ntok], ALPHA, eng='pool')
    for d in range(KT):
        wds = st['wds'][d % 2]
        c.dma(wds[:], wd[:, d * 128:(d + 1) * 128].rearrange("(ft p) m -> p ft m", p=128), q='sp')
        ps = pp.get()
        for f in range(FT):
            c.mm(ps[:, 0:ntok], wds[:, f, :], hT[:, f, 0:ntok], start=(f == 0), stop=(f == FT - 1))
        c.stt(xs[:, d, 0:ntok], ps[:, 0:ntok], 0.5, xs[:, d, 0:ntok], ALU.mult, ALU.add)


def run_slabs(slabs, bufs, load_fn, compute_fn):
    if not slabs:
        return
    load_fn(slabs[0], bufs[0])
    for n, s in enumerate(slabs):
        if n + 1 < len(slabs):
            load_fn(slabs[n + 1], bufs[(n + 1) % len(bufs)])
        compute_fn(s, bufs[n % len(bufs)])


def make_slabs(tiles, maxw=512):
    slabs = []
    cur = None
    for (c0, ms, meta) in tiles:
        if cur is not None and cur[0] + cur[1] == c0 and cur[1] + ms <= maxw:
            cur[1] += ms
            cur[2].append((c0, ms, meta))
        else:
            cur = [c0, ms, [(c0, ms, meta)]]
            slabs.append(cur)
    return slabs


def fm_proj(c, pp, wsl, Wb2d, tiles, xb, ntok, evac, nk=KT, pre=None):
    slabs = make_slabs(tiles)

    def load(s, buf):
        c.dma(buf[:, 0:nk, 0:s[1]], Wb2d[:, s[0]:s[0] + s[1]].rearrange("(kt p) m -> p kt m", p=128), q='sp')

    def comp(s, buf):
        for (c0, ms, meta) in s[2]:
            o = c0 - s[0]
            if pre is not None:
                pre(c0, ms, meta)
            ps = pp.get()
            for k in range(nk):
                c.mm(ps[0:ms, 0:ntok], buf[:, k, o:o + ms], xb[:, k, 0:ntok], start=(k == 0), stop=(k == nk - 1))
            evac(ps, c0, ms, meta)
    run_slabs(slabs, wsl, load, comp)


def ffn_pass(c, pp, T, lng, lnb, consts, xT_src, xT_dst, wg, wu, wd, l, lni):
    with ExitStack() as es:
        st = {}
        st['zb'] = c.sb('zb', [128, KT, TT], BF16, stack=es)
        st['zq'] = c.sb('zq', [128, KT, TT], BF16, stack=es)
        st['mean'] = c.sb('mean', [128, TT], stack=es)
        st['rstd'] = c.sb('rstd', [128, TT], stack=es)
        st['tmp512'] = c.sb('tmp512', [128, TT], stack=es)
        st['xs'] = c.sb('xs', [128, KT, TT], stack=es)
        st['hT'] = c.sb('hT', [128, FT, TT], BF16, stack=es)
        st['wgs'] = [c.sb('wgs%d' % i, [128, KT, 256], BF16, stack=es) for i in range(2)]
        st['wus'] = [c.sb('wus%d' % i, [128, KT, 256], BF16, stack=es) for i in range(2)]
        st['wds'] = [c.sb('wds%d' % i, [128, FT, 128], BF16, stack=es) for i in range(2)]
        st['sg'] = [c.sb('sg%d' % i, [128, TT], stack=es) for i in range(2)]
        for tt in range(T // TT):
            xs = st['xs']
            c.dma(xs[:], xT_src[:, tt * TT:(tt + 1) * TT].rearrange("(k p) t -> p k t", p=128))
            c.copy(st['zb'][:], xs[:], eng='act')
            ffn_tile(c, pp, st, xs, st['zb'], wg, wu, wd, TT)
            layer_norm_fm(c, pp, st, xs, lng[:, l * 3 + lni, :], lnb[:, l * 3 + lni, :], TT, consts)
            c.dma(xT_dst[:, tt * TT:(tt + 1) * TT].rearrange("(k p) t -> p k t", p=128), xs[:], q='pool')


def inproj_pass(c, pp, T, l, W, Wb, S, xT_src):
    win = Wb['w_in'][l]
    tiles = []
    for i in range(48):
        tiles.append((i * 128, 128, ('rw', i)))
    tiles.append((6144, 96, ('rw', 48)))
    tiles.append((6240, 96, ('rw', 49)))
    tiles.append((6336, 128, ('rw', 50)))
    tiles.append((6464, 128, ('rw', 51)))
    for i in range(16):
        tiles.append((OFF_GM + i * 128, 128, ('gmu', i)))
    for i in range(24):
        tiles.append((OFF_NSA + i * 128, 128, ('q', i)))
    for i in range(30):
        tiles.append((OFF_NSA + 3072 + i * 128, 128, ('kv', i)))
    tiles.append((OFF_NSA + 6912, 48, ('ng', 0)))
    for i in range(48):
        tiles.append((OFF_GATE + i * 128, 128, ('gate', i)))
    with ExitStack() as es:
        xs = c.sb('ip_xs', [128, KT, TT], stack=es)
        xb = c.sb('ip_xb', [128, KT, TT], BF16, stack=es)
        wsl = [c.sb('ip_w%d' % i, [128, KT, 512], BF16, stack=es) for i in range(2)]
        wtk = [c.sb('ip_wt%d' % i, [128, KT, 512], BF16, stack=es) for i in range(2)]
        mu = c.sb('ip_mu', [128, 52], stack=es)
        carry = c.sb('ip_carry', [128, 52], stack=es)
        pbuf = [c.sb('ip_pb%d' % i, [128, TT + 1], stack=es) for i in range(2)]
        dbuf = [c.sb('ip_db%d' % i, [128, TT], stack=es) for i in range(2)]
        obuf = [c.sb('ip_ob%d' % i, [128, TT], stack=es) for i in range(3)]
        obb = [c.sb('ip_obb%d' % i, [128, TT], BF16, stack=es) for i in range(3)]
        vbuf = [c.sb('ip_vb%d' % i, [128, 512], stack=es) for i in range(2)]
        c.memset(carry[:], 0.0)
        c.memset(mu[:], 0.0)
        for (c0, ms, meta) in tiles:
            if meta[0] == 'rw':
                c.dma(mu[0:ms, meta[1]:meta[1] + 1], W['rw_mu'][l:l + 1, c0:c0 + ms].rearrange("o m -> m o"), q='sp', allow_slow_non_contiguous=True)
        cnt = [0]
        for tt in range(T // TT):
            ts_ = slice(tt * TT, (tt + 1) * TT)
            c.dma(xs[:], xT_src[:, ts_].rearrange("(k p) t -> p k t", p=128))
            c.copy(xb[:], xs[:], eng='act')

            def evac(ps, c0, ms, meta):
                kind, idx = meta
                n = cnt[0]
                cnt[0] += 1
                if kind == 'rw':
                    pb = pbuf[n % 2]
                    db = dbuf[n % 2]
                    c.copy(pb[0:ms, 0:1], carry[0:ms, idx:idx + 1], eng='dve')
                    c.copy(pb[0:ms, 1:TT + 1], ps[0:ms, 0:TT], eng='act')
                    c.copy(carry[0:ms, idx:idx + 1], pb[0:ms, TT:TT + 1], eng='dve')
                    c.tt(db[0:ms, :], pb[0:ms, 0:TT], pb[0:ms, 1:TT + 1], ALU.subtract)
                    c.stt(db[0:ms, :], db[0:ms, :], mu[0:ms, idx:idx + 1], pb[0:ms, 1:TT + 1], ALU.mult, ALU.add)
                    c.dma(S['pRW'][c0:c0 + ms, ts_], db[0:ms, :], q='pool')
                elif kind == 'gmu':
                    ob = obuf[n % 3]
                    c.act(ob[0:ms, :], ps[0:ms, 0:TT], AF.Gelu)
                    c.dma(S['uT'][idx * 128:idx * 128 + ms, ts_], ob[0:ms, :], q='pool')
                elif kind in ('q', 'kv'):
                    ob = obb[n % 3]
                    c.copy(ob[0:ms, :], ps[0:ms, 0:TT], eng=('act' if n % 2 else 'dve'))
                    dst = S['qT'] if kind == 'q' else S['kvT']
                    c.dma(dst[idx * 128:idx * 128 + ms, ts_], ob[0:ms, :], q='pool')
                elif kind == 'ng':
                    ob = obuf[n % 3]
                    c.act(ob[0:ms, :], ps[0:ms, 0:TT], AF.Sigmoid)
                    c.dma(S['ngT'][0:ms, ts_], ob[0:ms, :], q='pool')
                elif kind == 'gate':
                    ob = obuf[n % 3]
                    c.act(ob[0:ms, :], ps[0:ms, 0:TT], AF.Sigmoid)
                    c.dma(S['gateT'][idx * 128:idx * 128 + ms, ts_], ob[0:ms, :], q='pool')
            fm_proj(c, pp, wsl, win, tiles, xb, TT, evac)
            vslabs = [(OFF_GM + 2048 + j * 512, 512) for j in range(4)]

            def vload(s, buf):
                c.dma(buf[:], win[:, s[0]:s[0] + 512].rearrange("(kt p) m -> p kt m", p=128), q='sp')

            def vcomp(s, buf):
                for tq in range(TT // 128):
                    ps = pp.get()
                    for k in range(KT):
                        c.mm(ps[:, :], xb[:, k, tq * 128:(tq + 1) * 128], buf[:, k, :], start=(k == 0), stop=(k == KT - 1))
                    vb = vbuf[cnt[0] % 2]
                    cnt[0] += 1
                    c.act(vb[:], ps[:], AF.Gelu)
                    j0 = s[0] - OFF_GM - 2048
                    c.dma(S['vtok'][tt * TT + tq * 128:tt * TT + (tq + 1) * 128, j0:j0 + 512], vb[:], q='pool')
            run_slabs(vslabs, wtk, vload, vcomp)


def gmlp_pass(c, pp, T, l, W, S, consts):
    with ExitStack() as es:
        gbc = c.sb('gmt_g', [128, D], stack=es)
        bbc = c.sb('gmt_b', [128, D], stack=es)
        bsb = c.sb('gmt_bs', [128, 16, 128], stack=es)
        wst = c.sb('gm_wsT', [128, 16, 128], BF16, stack=es)
        wraw = [c.sb('gm_wr%d' % i, [128, 128], stack=es) for i in range(2)]
        c.dma(gbc[:], W['gm_ln_g'][l:l + 1, :].partition_broadcast(128).rearrange("p o d -> p (o d)"), q='sp')
        c.dma(bbc[:], W['gm_ln_b'][l:l + 1, :].partition_broadcast(128).rearrange("p o d -> p (o d)"), q='sp')
        c.dma(bsb[:].rearrange("p g t -> p (g t)"), W['gm_bs'][l:l + 1].rearrange("o g t -> o (g t)").partition_broadcast(128).rearrange("p o d -> p (o d)"), q='sp')
        triu = consts['triu']
        for g in range(16):
            wr = wraw[g % 2]
            c.dma(wr[:], W['gm_ws'][l, g])
            ps = pp.get()
            c.tr(ps[:, 0:128], wr[:], consts['ident'][:])
            c.tt(wst[:, g, :], ps[:, 0:128], triu[:], ALU.mult)
        vt = [c.sb('gm_v%d' % i, [128, D], stack=es) for i in range(2)]
        vc = c.sb('gm_vc', [128, D], stack=es)
        vn = c.sb('gm_vn', [128, D], BF16, stack=es)
        junk = c.sb('gm_junk', [128, D], BF16, stack=es)
        stat = c.sb('gm_stat', [128, 8], stack=es)
        ut = [c.sb('gm_u%d' % i, [128, 16, 128], stack=es) for i in range(2)]
        yt = [c.sb('gm_y%d' % i, [128, 16, 128], BF16, stack=es) for i in range(2)]
        tmp = c.sb('gm_tmp', [128, 512], stack=es)
        for ch in range(T // 128):
            v = vt[ch % 2]
            u = ut[ch % 2]
            y = yt[ch % 2]
            c.dma(v[:], S['vtok'][ch * 128:(ch + 1) * 128, :])
            c.dma(u[:], S['uT'][:, ch * 128:(ch + 1) * 128].rearrange("(g p) t -> p g t", p=128))
            c.reduce(stat[:, 0:1], v[:], ALU.add)
            c.ts(stat[:, 1:2], stat[:, 0:1], 1.0 / D)
            c.ts(vc[:], v[:], stat[:, 1:2], None, ALU.subtract)
            c.act(junk[:], vc[:], AF.Square, accum_out=stat[:, 2:3])
            c.ts(stat[:, 3:4], stat[:, 2:3], 1.0 / D, LN_EPS, ALU.mult, ALU.add)
            c.act(stat[:, 3:4], stat[:, 3:4], AF.Sqrt)
            c.recip(stat[:, 4:5], stat[:, 3:4])
            c.stt(vc[:], vc[:], stat[:, 4:5], gbc[:], ALU.mult, ALU.mult)
            c.tt(vn[:], vc[:], bbc[:], ALU.add)
            for g4 in range(4):
                ps = pp.get()
                for j in range(4):
                    g = g4 * 4 + j
                    c.mm(ps[:, j * 128:(j + 1) * 128], vn[:, g * 128:(g + 1) * 128], wst[:, g, :])
                c.tt(tmp[:], ps[:], bsb[:, g4 * 4:(g4 + 1) * 4, :].rearrange("p g t -> p (g t)"), ALU.add)
                c.tt(y[:, g4 * 4:(g4 + 1) * 4, :].rearrange("p g t -> p (g t)"), tmp[:], u[:, g4 * 4:(g4 + 1) * 4, :].rearrange("p g t -> p (g t)"), ALU.mult)
            c.dma(S['ygmT'][:, ch * 128:(ch + 1) * 128].rearrange("(g p) t -> p g t", p=128), y[:], q='pool')


def merge_pass(c, pp, T, l, W, Wb, S, lng, lnb, consts, xT_src, xT_dst, branches):
    with ExitStack() as es:
        st = {}
        st['zb'] = c.sb('zb', [128, KT, TT], BF16, stack=es)
        st['zq'] = c.sb('zq', [128, KT, TT], BF16, stack=es)
        st['mean'] = c.sb('mean', [128, TT], stack=es)
        st['rstd'] = c.sb('rstd', [128, TT], stack=es)
        st['tmp512'] = c.sb('tmp512', [128, TT], stack=es)
        xs = c.sb('xs', [128, KT, TT], stack=es)
        mg = c.sb('mg', [128, KT, TT], stack=es)
        yb = [c.sb('mg_y%d' % i, [128, KT, TT], BF16, stack=es) for i in range(2)]
        wsl = [c.sb('mg_w%d' % i, [128, KT, 512], BF16, stack=es) for i in range(2)]
        gt = [c.sb('mg_g%d' % i, [128, TT], stack=es) for i in range(3)]
        tmp = [c.sb('mg_t%d' % i, [128, TT], stack=es) for i in range(2)]
        ysrc = {0: S['yrwT'], 1: S['ygmT'], 2: S['ynsT']}
        tiles = [(m * 128, 128, m) for m in range(KT)]
        cnt = [0]
        for tt in range(T // TT):
            ts_ = slice(tt * TT, (tt + 1) * TT)
            c.dma(xs[:], xT_src[:, ts_].rearrange("(k p) t -> p k t", p=128))
            first = True
            for bi, i in enumerate(branches):
                y = yb[bi % 2]
                c.dma(y[:], ysrc[i][:, ts_].rearrange("(k p) t -> p k t", p=128))

                gq = []

                def pre(c0, ms, m, i=i, gq=gq):
                    g = gt[cnt[0] % 3]
                    c.dma(g[:], S['gateT'][i * D + m * 128:i * D + (m + 1) * 128, ts_], q='sp')
                    gq.append(g)

                def evac(ps, c0, ms, m, i=i, first=first, gq=gq):
                    n = cnt[0]
                    cnt[0] += 1
                    g = gq.pop(0)
                    if first:
                        c.tt(mg[:, m, :], ps[:, 0:TT], g[:], ALU.mult)
                    else:
                        t_ = tmp[n % 2]
                        c.tt(t_[:], ps[:, 0:TT], g[:], ALU.mult)
                        c.tt(mg[:, m, :], mg[:, m, :], t_[:], ALU.add, eng='pool')
                fm_proj(c, pp, wsl, Wb['w_br'][l, i], tiles, y, TT, evac, pre=pre)
                first = False
            c.copy(st['zb'][:], mg[:], eng='act')

            def evac2(ps, c0, ms, m):
                c.stt(xs[:, m, :], xs[:, m, :], ALPHA, ps[:, 0:TT], ALU.mult, ALU.add)
            fm_proj(c, pp, wsl, Wb['w_o'][l], tiles, st['zb'], TT, evac2)
            layer_norm_fm(c, pp, st, xs, lng[:, l * 3 + 1, :], lnb[:, l * 3 + 1, :], TT, consts)
            c.dma(xT_dst[:, ts_].rearrange("(k p) t -> p k t", p=128), xs[:], q='pool')


WSHAPES = {
    'w_in': [DEPTH, D, C_IN], 'ffn1_wg': [DEPTH, D, DFF], 'ffn1_wu': [DEPTH, D, DFF], 'ffn1_wd': [DEPTH, DFF, D],
    'ffn2_wg': [DEPTH, D, DFF], 'ffn2_wu': [DEPTH, D, DFF], 'ffn2_wd': [DEPTH, DFF, D],
    'w_br': [DEPTH, 3, D, D], 'w_o': [DEPTH, D, D],
    'ln_g': [DEPTH, 3, D], 'ln_b': [DEPTH, 3, D],
    'rw_mu': [DEPTH, RW_COLS], 'rw_w0': [DEPTH, D], 'rw_w2': [DEPTH, 96, D], 'rw_a0': [DEPTH, D], 'rw_a2': [DEPTH, 96, D],
    'rw_g2': [DEPTH, 256, D], 'rw_v0': [DEPTH - 1, D], 'rw_v1': [DEPTH - 1, D, 64], 'rw_v2': [DEPTH - 1, 64, D],
    'rw_k_k': [DEPTH, D], 'rw_k_a': [DEPTH, D], 'rw_r_k': [DEPTH, 32, 64], 'rw_gn_g': [DEPTH, D], 'rw_gn_b': [DEPTH, D],
    'gm_ln_g': [DEPTH, D], 'gm_ln_b': [DEPTH, D], 'gm_ws': [DEPTH, 16, 128, 128], 'gm_bs': [DEPTH, 16, 128],
    'nsa_pos_k': [DEPTH, 32, 192], 'nsa_pos_v': [DEPTH, 32, 128], 'nsa_phi_k1': [DEPTH, 6144, 192], 'nsa_phi_k2': [DEPTH, 192, 192],
    'nsa_phi_v1': [DEPTH, 4096, 128], 'nsa_phi_v2': [DEPTH, 128, 128],
}
BIGW = ['w_in', 'ffn1_wg', 'ffn1_wu', 'ffn1_wd', 'ffn2_wg', 'ffn2_wu', 'ffn2_wd', 'w_br', 'w_o']


def build(T=4096, nlayers=DEPTH, passes=('ffn1', 'inproj', 'gmlp', 'rwkv', 'nsa', 'merge', 'ffn2'), branches=(0, 1, 2),
          use=None, debug=False):
    net = Net(T)
    c = net.c
    nc = net.nc
    x_in = net.din('x', [T, D])
    CONST_SHAPES = {'c_ident': [128, 128], 'c_triu': [128, 128], 'c_bones': [128, 128], 'c_maskqr': [128, 128],
                    'c_masksl': [128, 64], 'c_reset': [128, 512], 'c_ident2': [128, 64], 'c_maskc': [128, 8],
                    'c_esel': [64, 4096], 'c_caus4': [128, 512], 'c_first4': [128, 512]}
    cd = {k: net.din(k, s) for k, s in CONST_SHAPES.items()}
    W = {}
    for k, s in WSHAPES.items():
        if use is None or k in use:
            W[k] = net.din(k, s)
    out = net.dout('out', [T, D])

    mk = net.dout if debug else net.dscr
    xT = [net.dscr('xT%d' % i, [D, T]) for i in range(2)]
    Wb = {}
    for k in BIGW:
        if k in W:
            Wb[k] = net.dscr(k + '_b', WSHAPES[k], BF16)
    S = {
        'pRW': mk('s_pRW', [RW_COLS, T]), 'uT': mk('s_uT', [D, T]), 'vtok': mk('s_vtok', [T, D]),
        'qT': mk('s_qT', [3072, T], BF16), 'kvT': mk('s_kvT', [3840, T], BF16), 'ngT': mk('s_ngT', [48, T]),
        'gateT': mk('s_gateT', [3 * D, T]),
        'yrwT': mk('s_yrwT', [D, T], BF16), 'ygmT': mk('s_ygmT', [D, T], BF16), 'ynsT': mk('s_ynsT', [D, T], BF16),
        'vfirstT': net.dscr('s_vfirstT', [D, T]),
    }
    net.S = S

    ident = c.sb('ident', [128, 128])
    c.dma(ident[:], cd['c_ident'])
    triu = c.sb('triu', [128, 128])
    c.dma(triu[:], cd['c_triu'])
    ones_bf = c.sb('ones_bf', [128, 128], BF16)
    c.memset(ones_bf[:], 1.0)
    consts = {'ident': ident, 'ones_bf': ones_bf, 'triu': triu}
    consts['d_esel'] = cd['c_esel']
    consts['d_caus4'] = cd['c_caus4']
    consts['d_first4'] = cd['c_first4']
    for nm in ['bones', 'maskqr', 'masksl', 'reset', 'ident2', 'maskc']:
        consts[nm] = c.sb('k_' + nm, CONST_SHAPES['c_' + nm])
        c.dma(consts[nm][:], cd['c_' + nm])
    lng = c.sb('lng', [128, DEPTH * 3, KT])
    lnb = c.sb('lnb', [128, DEPTH * 3, KT])
    c.dma(lng[:], W['ln_g'].rearrange("l i (k p) -> p (l i) k", p=128), q='sp', allow_slow_non_contiguous=True)
    c.dma(lnb[:], W['ln_b'].rearrange("l i (k p) -> p (l i) k", p=128), q='sp', allow_slow_non_contiguous=True)
    pp = PsumPool(c)

    with ExitStack() as es:
        cva = [c.sb('cva%d' % i, [128, 4096], F32, stack=es) for i in range(2)]
        cvb = [c.sb('cvb%d' % i, [128, 4096], BF16, stack=es) for i in range(2)]
        k = 0
        for name in Wb:
            for l in range(nlayers):
                s = WSHAPES[name]
                if name == 'w_br':
                    for i in range(3):
                        k = conv_weight(c, cva, cvb, W[name][l, i], Wb[name][l, i], s[2], s[3], k)
                else:
                    k = conv_weight(c, cva, cvb, W[name][l], Wb[name][l], s[1], s[2], k)

    c.barrier()
    with ExitStack() as es:
        xtok = [c.sb('xtok%d' % i, [128, D], F32, stack=es) for i in range(2)]
        xo = [c.sb('xo%d' % i, [128, 4, 128], F32, stack=es) for i in range(2)]
        n = 0
        for t in range(T // 128):
            xt = xtok[t % 2]
            c.dma(xt[:], x_in[t * 128:(t + 1) * 128, :])
            for g4 in range(KT // 4):
                ps = pp.get()
                for j in range(4):
                    k = g4 * 4 + j
                    c.tr(ps[:, j * 128:(j + 1) * 128], xt[:, k * 128:(k + 1) * 128], ident[:])
                o = xo[n % 2]
                n += 1
                c.copy(o[:].rearrange("p a b -> p (a b)"), ps[:], eng=('dve' if n % 2 else 'act'))
                c.dma(xT[0][g4 * 512:(g4 + 1) * 512, t * 128:(t + 1) * 128].rearrange("(a p) t -> p a t", p=128), o[:], q='pool')

    c.barrier()
    cur = 0
    for l in range(nlayers):
        if 'ffn1' in passes:
            ffn_pass(c, pp, T, lng, lnb, consts, xT[cur], xT[1 - cur], Wb['ffn1_wg'][l], Wb['ffn1_wu'][l], Wb['ffn1_wd'][l], l, 0)
            c.barrier()
            cur = 1 - cur
        if 'inproj' in passes:
            inproj_pass(c, pp, T, l, W, Wb, S, xT[cur])
            c.barrier()
        if 'gmlp' in passes:
            gmlp_pass(c, pp, T, l, W, S, consts)
            c.barrier()
        if 'rwkv' in passes:
            rwkv_pass(c, pp, T, l, W, S, consts)
            c.barrier()
        if 'nsa' in passes:
            nsa_pass(c, pp, T, l, W, S, consts)
            c.barrier()
        if 'merge' in passes:
            merge_pass(c, pp, T, l, W, Wb, S, lng, lnb, consts, xT[cur], xT[1 - cur], branches)
            c.barrier()
            cur = 1 - cur
        if 'ffn2' in passes:
            ffn_pass(c, pp, T, lng, lnb, consts, xT[cur], xT[1 - cur], Wb['ffn2_wg'][l], Wb['ffn2_wu'][l], Wb['ffn2_wd'][l], l, 2)
            c.barrier()
            cur = 1 - cur

    c.barrier()
    with ExitStack() as es:
        xf = [c.sb('xf%d' % i, [128, KT, 128], F32, stack=es) for i in range(2)]
        yo = [c.sb('yo%d' % i, [128, D], F32, stack=es) for i in range(2)]
        for t in range(T // 128):
            a = xf[t % 2]
            c.dma(a[:], xT[cur][:, t * 128:(t + 1) * 128].rearrange("(k p) t -> p k t", p=128))
            y = yo[t % 2]
            for g4 in range(KT // 4):
                ps = pp.get()
                for j in range(4):
                    k = g4 * 4 + j
                    c.tr(ps[:, j * 128:(j + 1) * 128], a[:, k, :], ident[:])
                c.copy(y[:, g4 * 512:(g4 + 1) * 512], ps[:], eng=('dve' if g4 % 2 else 'act'))
            c.dma(out[t * 128:(t + 1) * 128, :], y[:], q='pool')
    c.wait_all('sp')
    return net


RW_SBT = 256
RW_NCH = RW_SBT // 64
RW_G = 4
USE_F32R = True
EXPM05 = 0.6065306597126334


def rwkv_pass(c, pp, T, l, W, S, consts, stop=99):
    nc = c.nc
    SBT, NCH, G = RW_SBT, RW_NCH, RW_G
    ident = consts['ident']
    bones = consts['bones']
    mqr = consts['maskqr']
    msl = consts['masksl']
    reset = consts['reset']
    pRW = S['pRW']
    with ExitStack() as es:
        def sb(name, shape, dt=F32):
            return c.sb('rw_' + name, shape, dt, stack=es)
        par = {}
        for nm in ['rw_w0', 'rw_a0', 'rw_k_k', 'rw_k_a', 'rw_gn_g', 'rw_gn_b']:
            par[nm] = sb(nm, [128, 16])
            c.dma(par[nm][:], W[nm][l:l + 1, :].rearrange("o (q p) -> p (o q)", p=128), q='sp', allow_slow_non_contiguous=True)
        par['rw_r_k'] = sb('rk', [128, 16])
        c.dma(par['rw_r_k'][:], W['rw_r_k'][l:l + 1].rearrange("o (q a) n -> (a n) (o q)", a=2), q='sp', allow_slow_non_contiguous=True)
        par['omka'] = sb('omka', [128, 16])
        c.ts(par['omka'][:], par['rw_k_a'][:], -1.0, 1.0, ALU.mult, ALU.add)
        if l > 0:
            par['rw_v0'] = sb('v0', [128, 16])
            c.dma(par['rw_v0'][:], W['rw_v0'][l - 1:l, :].rearrange("o (q p) -> p (o q)", p=128), q='sp', allow_slow_non_contiguous=True)
        w2b = sb('w2b', [96, D], BF16)
        a2b = sb('a2b', [96, D], BF16)
        g2b = sb('g2b', [128, 2, D], BF16)
        if l > 0:
            v1b = sb('v1b', [128, KT, 64], BF16)
            v2b = sb('v2b', [64, D], BF16)
            vvT = sb('vvT', [64, T], BF16)
        with ExitStack() as es2:
            stg = c.sb('rw_stg', [128, 2, D], stack=es2)
            c.dma(stg[0:96, 0, :], W['rw_w2'][l])
            c.copy(w2b[:], stg[0:96, 0, :], eng='act')
            c.dma(stg[0:96, 1, :], W['rw_a2'][l])
            c.copy(a2b[:], stg[0:96, 1, :], eng='act')
            c.dma(stg[:], W['rw_g2'][l].rearrange("(k p) d -> p k d", p=128))
            c.copy(g2b[:], stg[:], eng='act')
            if l > 0:
                c.dma(stg[:, 0, 0:KT * 64].rearrange("p (k e) -> p k e", e=64), W['rw_v1'][l - 1].rearrange("(k p) e -> p k e", p=128))
                c.copy(v1b[:].rearrange("p k e -> p (k e)"), stg[:, 0, 0:KT * 64], eng='act')
                c.dma(stg[0:64, 1, :], W['rw_v2'][l - 1])
                c.copy(v2b[:], stg[0:64, 1, :], eng='act')
                vld = c.sb('rw_vld', [128, KT, 512], stack=es2)
                vlb = c.sb('rw_vlb', [128, KT, 512], BF16, stack=es2)
                for tb in range(T // 512):
                    c.dma(vld[:], pRW[4096:6144, tb * 512:(tb + 1) * 512].rearrange("(k p) t -> p k t", p=128))
                    c.copy(vlb[:], vld[:], eng='act')
                    ps = pp.get()
                    for k in range(KT):
                        c.mm(ps[0:64, :], v1b[:, k, :], vlb[:, k, :], start=(k == 0), stop=(k == KT - 1))
                    c.copy(vvT[:, tb * 512:(tb + 1) * 512], ps[0:64, :], eng='dve')
            c.barrier()
        lw = sb('lw', [96, 2, SBT])
        lg = sb('lg', [128, 2, SBT])
        twl = sb('twl', [96, SBT], BF16)
        alb = sb('alb', [96, SBT], BF16)
        sgl = sb('sgl', [128, 2, SBT], BF16)
        names = ['r', 'k', 'v', 'a', 'ld', 'cs', 'kk', 'kp', 'bv', 't0', 't1', 't2', 'BtT', 'KtT', 'BcT', 'KcT', 'g', 'bonus', 'yT', 'vf', 'sq', 'vr']
        P = [{n: sb('%s%d' % (n, g), [128, SBT]) for n in names} for g in range(G)]
        identr = sb('identr', [128, 128])
        bonesr = sb('bonesr', [128, 128])
        for g in range(G):
            P[g]['AR'] = sb('AR%d' % g, [128, NCH, 128])
            P[g]['QRB'] = sb('QRB%d' % g, [128, NCH, 128])
            P[g]['AKRK'] = sb('AKRK%d' % g, [128, NCH, 128])
            P[g]['Nn'] = sb('Nn%d' % g, [128, NCH, 64])
            P[g]['PwQ'] = sb('PwQ%d' % g, [128, NCH, 128])
            P[g]['PwN'] = sb('PwN%d' % g, [128, NCH, 128])
            P[g]['Tt'] = sb('Tt%d' % g, [128, NCH, 128])
            for n_ in ('PwQ', 'PwN', 'Tt'):
                c.memset(P[g][n_][:], 0.0, eng='pool')
            P[g]['Vtok'] = sb('Vtok%d' % g, [128, NCH, 64])
            P[g]['Bctok'] = sb('Bctok%d' % g, [128, NCH, 64])
            P[g]['Kctok'] = sb('Kctok%d' % g, [128, NCH, 64])
            P[g]['PC'] = sb('PC%d' % g, [128, NCH])
            P[g]['yb'] = sb('yb%d' % g, [128, SBT], BF16)
        St = sb('St', [128, G, 64])
        X0 = sb('X0', [128, G, 64])
        Us = sb('Us', [128, G, 64])
        if USE_F32R:
            c.r32 = set()
            for g in range(G):
                for n in ['AR', 'BtT', 'KtT', 'BcT', 'KcT', 'QRB', 'AKRK', 'Nn', 'PwQ', 'PwN', 'Tt', 'Vtok', 'Bctok', 'Kctok', 'yT', 'sq', 'vr']:
                    c.r32.add(P[g][n].name)
            for t_ in (St, X0, Us, identr, bonesr):
                c.r32.add(t_.name)
        c.copy(identr[:], ident[:], eng='act')
        c.copy(bonesr[:], bones[:], eng='act')
        ident = identr
        bones = bonesr

        def hv(ap):
            return ap.rearrange("p (c t) -> p c t", t=64)

        for pg in range(16 // G):
            c.memset(St[:], 0.0)
            for sbi in range(T // SBT):
                tsl = slice(sbi * SBT, (sbi + 1) * SBT)
                c.dma(lw[:, 0, :], pRW[6144:6240, tsl])
                c.dma(lw[:, 1, :], pRW[6240:6336, tsl])
                c.dma(lg[:], pRW[6336:6592, tsl].rearrange("(k p) t -> p k t", p=128))
                c.act(twl[:], lw[:, 0, :], AF.Tanh)
                c.copy(alb[:], lw[:, 1, :], eng='dve')
                c.act(sgl[:], lg[:], AF.Sigmoid)
                def prep(g):
                    q = pg * G + g
                    t = P[g]
                    rows = slice(q * 128, (q + 1) * 128)
                    c.dma(t['r'][:], pRW[q * 128:(q + 1) * 128, tsl])
                    yield
                    c.dma(t['k'][:], pRW[2048 + q * 128:2048 + (q + 1) * 128, tsl])
                    yield
                    c.dma(t['v'][:], pRW[4096 + q * 128:4096 + (q + 1) * 128, tsl])
                    yield
                    ps = pp.get()
                    c.mm(ps[:, 0:SBT], w2b[:, rows], twl[:])
                    yield
                    c.act(t['ld'][:], ps[:, 0:SBT], AF.Sigmoid, bias=par['rw_w0'][:, q:q + 1])
                    yield
                    c.ts(t['ld'][:], t['ld'][:], -EXPM05)
                    yield
                    c.mm(ps[:, SBT:2 * SBT], a2b[:, rows], alb[:])
                    yield
                    c.act(t['a'][:], ps[:, SBT:2 * SBT], AF.Sigmoid, bias=par['rw_a0'][:, q:q + 1])
                    yield
                    ps = pp.get()
                    for kk_ in range(2):
                        c.mm(ps[:, 0:SBT], g2b[:, kk_, rows], sgl[:, kk_, :], start=(kk_ == 0), stop=(kk_ == 1))
                        yield
                    c.copy(t['g'][:], ps[:, 0:SBT], eng='act')
                    yield
                    if l > 0:
                        c.mm(ps[:, SBT:2 * SBT], v2b[:, rows], vvT[:, tsl])
                        yield
                        c.act(t['t0'][:], ps[:, SBT:2 * SBT], AF.Sigmoid, bias=par['rw_v0'][:, q:q + 1])
                        yield
                        c.dma(t['vf'][:], S['vfirstT'][rows, tsl])
                        yield
                        c.tt(t['t1'][:], t['vf'][:], t['v'][:], ALU.subtract, eng='pool')
                        yield
                        c.tt(t['t1'][:], t['t1'][:], t['t0'][:], ALU.mult, eng='pool')
                        yield
                        c.tt(t['v'][:], t['v'][:], t['t1'][:], ALU.add, eng='pool')
                        yield
                    else:
                        c.dma(S['vfirstT'][rows, tsl], t['v'][:], q='pool')
                        yield
                    if stop <= 1:
                        return
                    c.ts(t['kk'][:], t['k'][:], par['rw_k_k'][:, q:q + 1])
                    yield
                    c.tt(t['sq'][:], t['kk'][:], t['kk'][:], ALU.mult, eng='pool')
                    yield
                    ps = pp.get()
                    c.mm(ps[:, 0:SBT], bones[:], t['sq'][:])
                    yield
                    c.ts(t['t1'][:], ps[:, 0:SBT], 1e-24, None, ALU.max)
                    yield
                    c.act(t['t1'][:], t['t1'][:], AF.Sqrt)
                    yield
                    c.recip(t['t1'][:], t['t1'][:])
                    yield
                    c.tt(t['kk'][:], t['kk'][:], t['t1'][:], ALU.mult)
                    yield
                    c.ts(t['t2'][:], t['a'][:], par['rw_k_a'][:, q:q + 1], par['omka'][:, q:q + 1], ALU.mult, ALU.add)
                    yield
                    c.tt(t['kp'][:], t['k'][:], t['t2'][:], ALU.mult, eng='pool')
                    yield
                    c.tt(t['bv'][:], t['kk'][:], t['a'][:], ALU.mult, eng='pool')
                    yield
                    c.stt(t['sq'][:], t['r'][:], par['rw_r_k'][:, q:q + 1], t['kp'][:], ALU.mult, ALU.mult)
                    yield
                    c.mm(ps[:, SBT:2 * SBT], bones[:], t['sq'][:])
                    yield
                    c.copy(t['vr'][:], t['v'][:], eng='act')
                    yield
                    c.tt(t['bonus'][:], ps[:, SBT:2 * SBT], t['v'][:], ALU.mult)
                    yield
                    if stop <= 2:
                        return
                    c.op('dve', lambda t=t: nc.vector.tensor_tensor_scan(t['cs'][:], reset[:, 0:SBT], t['ld'][:], 0.0, ALU.mult, ALU.add),
                         [reset[:, 0:SBT], t['ld'][:]], [t['cs'][:]])
                    yield
                    c.tt(t['t0'][:], t['cs'][:], t['ld'][:], ALU.subtract, eng='pool')
                    yield
                    c.act(t['t1'][:], t['t0'][:], AF.Exp)
                    yield
                    c.stt(t['AR'][:, :, 0:64], hv(t['kk'][:]), -1.0, hv(t['t1'][:]), ALU.mult, ALU.mult)
                    yield
                    c.act(t['t1'][:], t['cs'][:], AF.Exp)
                    yield
                    c.tt(t['AR'][:, :, 64:128], hv(t['r'][:]), hv(t['t1'][:]), ALU.mult)
                    yield
                    c.act(t['t1'][:], t['cs'][:], AF.Exp, scale=-1.0)
                    yield
                    c.tt(t['BtT'][:], t['bv'][:], t['t1'][:], ALU.mult)
                    yield
                    c.tt(t['KtT'][:], t['kp'][:], t['t1'][:], ALU.mult, eng='pool')
                    yield
                    csv = hv(t['cs'][:])
                    c.tt(hv(t['t0'][:]), csv[:, :, 63:64].broadcast_to([128, NCH, 64]), csv, ALU.subtract)
                    yield
                    c.act(t['t1'][:], t['t0'][:], AF.Exp)
                    yield
                    c.tt(t['BcT'][:], t['bv'][:], t['t1'][:], ALU.mult)
                    yield
                    c.tt(t['KcT'][:], t['kp'][:], t['t1'][:], ALU.mult, eng='pool')
                    yield
                    c.act(t['PC'][:], csv[:, :, 63], AF.Exp)
                    yield
                    if stop <= 3:
                        return
                    ps1 = pp.get()
                    ps2 = pp.get()
                    ps3 = pp.get()
                    ps4 = pp.get()
                    for h in range(2):
                        hs = slice(h * 64, (h + 1) * 64)
                        for ch in range(NCH):
                            cs_ = slice(ch * 64, (ch + 1) * 64)
                            c.mm(ps1[hs, ch * 128:(ch + 1) * 128], t['BtT'][hs, cs_], t['AR'][hs, ch, :])
                            c.mm(ps2[hs, ch * 128:(ch + 1) * 128], t['KtT'][hs, cs_], t['AR'][hs, ch, :])
                            c.mm(ps3[hs, cs_], t['AR'][hs, ch, 0:64], t['BtT'][hs, cs_])
                            c.mm(ps4[hs, cs_], t['vr'][hs, cs_], ident[hs, hs])
                            c.mm(ps4[hs, SBT + ch * 64:SBT + (ch + 1) * 64], t['BcT'][hs, cs_], ident[hs, hs])
                    mq = mqr[:].unsqueeze(1).broadcast_to([128, NCH, 128])
                    c.tt(t['QRB'][:], ps1[:, 0:NCH * 128].rearrange("p (c t) -> p c t", t=128), mq, ALU.mult)
                    c.tt(t['AKRK'][:], ps2[:, 0:NCH * 128].rearrange("p (c t) -> p c t", t=128), mq, ALU.mult)
                    c.tt(t['Nn'][:], hv(ps3[:, 0:SBT]), msl[:].unsqueeze(1).broadcast_to([128, NCH, 64]), ALU.mult)
                    c.copy(t['Vtok'][:], hv(ps4[:, 0:SBT]), eng='act')
                    c.copy(t['Bctok'][:], hv(ps4[:, SBT:2 * SBT]), eng='act')
                    ps5 = pp.get()
                    for h in range(2):
                        hs = slice(h * 64, (h + 1) * 64)
                        for ch in range(NCH):
                            cs_ = slice(ch * 64, (ch + 1) * 64)
                            c.mm(ps5[hs, cs_], t['KcT'][hs, cs_], ident[hs, hs])
                    c.copy(t['Kctok'][:], hv(ps5[:, 0:SBT]), eng='act')
                    if stop <= 4:
                        return
                    yield
                gens = [prep(g) for g in range(G)]
                while gens:
                    nxt = []
                    for gen in gens:
                        try:
                            next(gen)
                            nxt.append(gen)
                        except StopIteration:
                            pass
                    gens = nxt
                if stop > 4:
                    for g in range(G):
                        t = P[g]
                        for h in range(2):
                            hs = slice(h * 64, (h + 1) * 64)
                            c.copy(t['PwQ'][hs, :, hs], t['QRB'][hs, :, 0:64], eng='pool')
                            c.copy(t['PwN'][hs, :, hs], t['Nn'][hs, :, :], eng='pool')
                            c.tt(t['Tt'][hs, :, hs], t['QRB'][hs, :, 0:64], consts['ident2'][hs, :].unsqueeze(1).broadcast_to([64, NCH, 64]), ALU.add)
                    for lev in range(5):
                        bn = {}
                        bq = {}
                        for g in range(G):
                            t = P[g]
                            bn[g] = pp.get()
                            for ch in range(NCH):
                                c.mm(bn[g][:, ch * 128:(ch + 1) * 128], t['PwQ'][:, ch, :], t['PwN'][:, ch, :])
                            if lev < 4:
                                bq[g] = pp.get()
                                for ch in range(NCH):
                                    c.mm(bq[g][:, ch * 128:(ch + 1) * 128], t['PwN'][:, ch, :], t['PwQ'][:, ch, :])
                        for g in range(G):
                            t = P[g]
                            c.copy(t['PwN'][:].rearrange("p c t -> p (c t)"), bn[g][:, :], eng='act')
                            if lev < 4:
                                c.copy(t['PwQ'][:].rearrange("p c t -> p (c t)"), bq[g][:, :], eng='dve')
                        bt = {}
                        for g in range(G):
                            t = P[g]
                            bt[g] = pp.get()
                            for ch in range(NCH):
                                c.mm(bt[g][:, ch * 128:(ch + 1) * 128], t['PwN'][:, ch, :], t['Tt'][:, ch, :])
                        for g in range(G):
                            t = P[g]
                            c.tt(t['Tt'][:].rearrange("p c t -> p (c t)"), t['Tt'][:].rearrange("p c t -> p (c t)"), bt[g][:, :], ALU.add)
                if stop <= 5:
                    continue
                for ch in range(NCH):
                    cs_ = slice(ch * 64, (ch + 1) * 64)
                    psx = pp.get()
                    for g in range(G):
                        t = P[g]
                        for h in range(2):
                            hs = slice(h * 64, (h + 1) * 64)
                            o = psx[hs, g * 64:(g + 1) * 64]
                            c.mm(o, t['AR'][hs, ch, 0:64], St[hs, g, :], start=True, stop=False)
                            c.mm(o, t['AKRK'][hs, ch, 0:64], t['Vtok'][hs, ch, :], start=False, stop=True)
                    c.copy(X0[:].rearrange("p g i -> p (g i)"), psx[:, 0:G * 64], eng='act')
                    psu = pp.get()
                    for g in range(G):
                        t = P[g]
                        c.mm(psu[:, g * 64:(g + 1) * 64], t['Tt'][:, ch, :], X0[:, g, :])
                    c.copy(Us[:].rearrange("p g i -> p (g i)"), psu[:, 0:G * 64], eng='dve')
                    psy = pp.get()
                    for g in range(G):
                        t = P[g]
                        for h in range(2):
                            hs = slice(h * 64, (h + 1) * 64)
                            o = psy[hs, g * 64:(g + 1) * 64]
                            c.mm(o, St[hs, g, :], t['AR'][hs, ch, 64:128], start=True, stop=False)
                            c.mm(o, Us[hs, g, :], t['QRB'][hs, ch, 64:128], start=False, stop=False)
                            c.mm(o, t['Vtok'][hs, ch, :], t['AKRK'][hs, ch, 64:128], start=False, stop=True)
                            o2 = psy[hs, 256 + g * 64:256 + (g + 1) * 64]
                            c.mm(o2, t['Bctok'][hs, ch, :], Us[hs, g, :], start=True, stop=False)
                            c.mm(o2, t['Kctok'][hs, ch, :], t['Vtok'][hs, ch, :], start=False, stop=True)
                    for g in range(G):
                        t = P[g]
                        c.copy(t['yT'][:, cs_], psy[:, g * 64:(g + 1) * 64], eng='act')
                        c.stt(St[:, g, :], St[:, g, :], t['PC'][:, ch:ch + 1], psy[:, 256 + g * 64:256 + (g + 1) * 64], ALU.mult, ALU.add)
                if stop <= 6:
                    continue
                for g in range(G):
                    q = pg * G + g
                    t = P[g]
                    ps = pp.get()
                    c.mm(ps[:, 0:SBT], bones[:], t['yT'][:])
                    c.tt(t['sq'][:], t['yT'][:], t['yT'][:], ALU.mult, eng='pool')
                    c.mm(ps[:, SBT:2 * SBT], bones[:], t['sq'][:])
                    c.act(t['t1'][:], ps[:, 0:SBT], AF.Copy, scale=1.0 / 64)
                    c.tt(t['t2'][:], t['t1'][:], t['t1'][:], ALU.mult)
                    c.stt(t['t2'][:], ps[:, SBT:2 * SBT], 1.0 / 64, t['t2'][:], ALU.mult, ALU.subtract)
                    c.ts(t['t2'][:], t['t2'][:], 64e-5, None, ALU.add)
                    c.act(t['t2'][:], t['t2'][:], AF.Sqrt)
                    c.recip(t['t2'][:], t['t2'][:])
                    c.tt(t['t0'][:], t['yT'][:], t['t1'][:], ALU.subtract)
                    c.tt(t['t0'][:], t['t0'][:], t['t2'][:], ALU.mult)
                    c.ts(t['t0'][:], t['t0'][:], par['rw_gn_g'][:, q:q + 1], par['rw_gn_b'][:, q:q + 1], ALU.mult, ALU.add)
                    c.tt(t['t0'][:], t['t0'][:], t['bonus'][:], ALU.add, eng='pool')
                    c.tt(t['yb'][:], t['t0'][:], t['g'][:], ALU.mult)
                    c.dma(S['yrwT'][q * 128:(q + 1) * 128, tsl], t['yb'][:], q='pool')


def host_consts():
    cc = {}
    cc['c_ident'] = np.eye(128, dtype=np.float32)
    cc['c_triu'] = np.triu(np.ones((128, 128), np.float32))
    bo = np.zeros((128, 128), np.float32)
    bo[:64, :64] = 1
    bo[64:, 64:] = 1
    cc['c_bones'] = bo
    s = np.arange(128)[:, None] % 64
    t = np.arange(64)[None, :]
    cc['c_maskqr'] = np.concatenate([(s < t), (s <= t)], axis=1).astype(np.float32)
    cc['c_masksl'] = (t < s).astype(np.float32)
    r = np.ones((128, 512), np.float32)
    r[:, ::64] = 0
    cc['c_reset'] = r
    cc['c_ident2'] = (s == t).astype(np.float32)
    qq = np.arange(128)[:, None]
    jj = np.arange(8)[None, :]
    cc['c_maskc'] = (qq >= 16 * jj + 15).astype(np.float32)
    cc['c_esel'] = (np.arange(4096)[None, :] // 64 == np.arange(64)[:, None]).astype(np.float32)
    kk = np.arange(128)[:, None]
    q4 = np.tile(np.arange(128), 4)[None, :]
    cc['c_caus4'] = np.where(kk <= q4, 0.0, -30000.0).astype(np.float32)
    cc['c_first4'] = np.where(kk > q4, 0.0, -30000.0).astype(np.float32)
    return cc


NSA_SCALE = 192 ** -0.5
NEGM = -30000.0


def nsa_pass(c, pp, T, l, W, S, consts):
    nc = c.nc
    n_c = T // 16 - 1
    n_s = T // 64
    NQ = T // 128
    ident = consts['ident']
    kvT = S['kvT']
    banks = pp.banks

    class Rot:
        def __init__(self, idx):
            self.idx = idx
            self.i = 0

        def get(self):
            b = banks[self.idx[self.i % len(self.idx)]]
            self.i += 1
            return b
    rs_ = Rot([0, 1, 2, 3])
    rm_ = Rot([6, 7])
    psO = banks[4]
    psL = banks[5]
    with ExitStack() as es:
        def sb(name, shape, dt=F32):
            return c.sb('ns_' + name, shape, dt, stack=es)
        ident_b = sb('identb', [128, 128], BF16)
        c.copy(ident_b[:], ident[:], eng='act')
        ones_b = consts['ones_bf']
        esel = sb('esel', [64, T], BF16)
        caus4 = sb('caus4', [128, 512], BF16)
        first4 = sb('first4', [128, 512], BF16)
        maskc = consts['maskc']
        kcmpT = sb('kcmpT', [96, 2, 4, 256], BF16)
        vcmp = sb('vcmp', [128, 2, 4, 128], BF16)
        c.memset(vcmp[:], 0.0)
        with ExitStack() as es2:
            def sb2(name, shape, dt=F32):
                return c.sb('ns2_' + name, shape, dt, stack=es2)
            stg = sb2('stg', [128, 4096])
            c.dma(stg[0:64, 0:T], consts['d_esel'][:, 0:T])
            c.copy(esel[:], stg[0:64, 0:T], eng='act')
            c.dma(stg[:, 0:512], consts['d_caus4'])
            c.copy(caus4[:], stg[:, 0:512], eng='act')
            c.dma(stg[:, 512:1024], consts['d_first4'])
            c.copy(first4[:], stg[:, 512:1024], eng='act')
            phk1 = sb2('phk1', [96, 64, 192], BF16)
            phv1 = sb2('phv1', [128, 32, 128], BF16)
            phk2 = sb2('phk2', [96, 2, 192], BF16)
            phv2 = sb2('phv2', [128, 128], BF16)
            stk = sb2('stk', [96, 16 * 192])
            for part in range(4):
                c.dma(stk[:].rearrange("p (a e) -> p a e", e=192),
                      W['nsa_phi_k1'][l][part * 1536:(part + 1) * 1536, :].rearrange("(a p) e -> p a e", p=96))
                c.copy(phk1[:, part * 16:(part + 1) * 16, :].rearrange("p a e -> p (a e)"), stk[:], eng=('act' if part % 2 else 'dve'))
            c.dma(stg[:, 0:4096].rearrange("p (a e) -> p a e", e=128), W['nsa_phi_v1'][l].rearrange("(a p) e -> p a e", p=128))
            c.copy(phv1[:].rearrange("p a e -> p (a e)"), stg[:, 0:4096], eng='dve')
            c.dma(stk[:, 0:384].rearrange("p (a e) -> p a e", e=192), W['nsa_phi_k2'][l].rearrange("(a p) e -> p a e", p=96))
            c.copy(phk2[:].rearrange("p a e -> p (a e)"), stk[:, 0:384], eng='act')
            c.dma(stg[:, 0:128], W['nsa_phi_v2'][l])
            c.copy(phv2[:], stg[:, 0:128], eng='act')
            posk = sb2('posk', [96, 2, 32], BF16)
            posv = sb2('posv', [128, 32], BF16)
            for dc in range(2):
                c.dma(stk[:, 400 + dc * 32:432 + dc * 32], W['nsa_pos_k'][l][:, dc * 96:(dc + 1) * 96].rearrange("b p -> p b"), allow_slow_non_contiguous=True)
            c.copy(posk[:].rearrange("p a b -> p (a b)"), stk[:, 400:464], eng='act')
            c.dma(stg[:, 200:232], W['nsa_pos_v'][l].rearrange("b p -> p b"), allow_slow_non_contiguous=True)
            c.copy(posv[:], stg[:, 200:232], eng='act')
            hpk = sb2('hpk', [96, 2])
            hpv = sb2('hpv', [128, 1])
            for et in range(2):
                ps = rm_.get()
                n = 0
                for lq in range(32):
                    for dc in range(2):
                        c.mm(ps[0:96, 0:1], phk1[:, lq * 2 + dc, et * 96:(et + 1) * 96], posk[:, dc, lq:lq + 1], start=(n == 0), stop=(n == 63))
                        n += 1
                c.copy(hpk[:, et:et + 1], ps[0:96, 0:1], eng='dve')
            ps = rm_.get()
            for lq in range(32):
                c.mm(ps[:, 0:1], phv1[:, lq, :], posv[:, lq:lq + 1], start=(lq == 0), stop=(lq == 31))
            c.copy(hpv[:], ps[:, 0:1], eng='dve')
            kc = sb2('kc', [96, 2, T], BF16)
            vc = sb2('vc', [128, T], BF16)
            ghk = sb2('ghk', [96, 2, 256], BF16)
            ghv = sb2('ghv', [128, 256], BF16)
            for g in range(4):
                c.dma(kc[:], kvT[g * 192:(g + 1) * 192, :].rearrange("(a p) t -> p a t", p=96))
                c.dma(vc[:], kvT[768 + g * 128:768 + (g + 1) * 128, :])
                for et in range(2):
                    ps = rm_.get()
                    n = 0
                    for lq in range(32):
                        for dc in range(2):
                            c.mm(ps[0:96, 0:n_c], phk1[:, lq * 2 + dc, et * 96:(et + 1) * 96],
                                 kc[:, dc, lq:lq + 16 * (n_c - 1) + 1:16], start=(n == 0), stop=(n == 63))
                            n += 1
                    c.act(ghk[:, et, 0:n_c], ps[0:96, 0:n_c], AF.Gelu, bias=hpk[:, et:et + 1])
                for e2 in range(2):
                    ps = rm_.get()
                    for ec in range(2):
                        c.mm(ps[0:96, 0:n_c], phk2[:, ec, e2 * 96:(e2 + 1) * 96], ghk[:, ec, 0:n_c], start=(ec == 0), stop=(ec == 1))
                    c.copy(kcmpT[:, e2, g, 0:n_c], ps[0:96, 0:n_c], eng='dve')
                ps = rm_.get()
                for lq in range(32):
                    c.mm(ps[:, 0:n_c], phv1[:, lq, :], vc[:, lq:lq + 16 * (n_c - 1) + 1:16], start=(lq == 0), stop=(lq == 31))
                c.act(ghv[:, 0:n_c], ps[:, 0:n_c], AF.Gelu, bias=hpv[:, 0:1])
                for nb in range((n_c + 127) // 128):
                    w = min(128, n_c - nb * 128)
                    ps = rm_.get()
                    c.mm(ps[0:w, 0:128], ghv[:, nb * 128:nb * 128 + w], phv2[:])
                    c.copy(vcmp[0:w, nb, g, :], ps[0:w, 0:128], eng='dve')
            c.barrier()
        ksT = sb('ksT', [96, 2, T], BF16)
        kwT = sb('kwT', [96, 2, T], BF16)
        vsk = sb('vsk', [128, NQ, 128], BF16)
        vwk = sb('vwk', [128, NQ, 128], BF16)
        vtmp = [sb('vtmp%d' % i, [128, 512], BF16) for i in range(2)]
        Qg = [sb('Qg%d' % i, [96, 4, 2, 128], BF16) for i in range(2)]
        Qd = [sb('Qd%d' % i, [96, 2, 512], BF16) for i in range(2)]
        gbc = [sb('gbc%d' % i, [128, 12, 128]) for i in range(2)]
        yn = sb('yn', [128, 4, 128])
        ynb = [sb('ynb%d' % i, [128, 4, 128], BF16) for i in range(2)]
        Pacc = sb('Pacc', [128, 264])
        ee = [sb('ee%d' % i, [128, 256]) for i in range(4)]
        pb = [sb('pb%d' % i, [128, 256], BF16) for i in range(4)]
        pT = [sb('pT%d' % i, [128, 2, 128], BF16) for i in range(4)]
        st = sb('st', [128, 16])
        imp = sb('imp', [128, 64])
        score = sb('score', [128, 64])
        sc2 = sb('sc2', [128, 64])
        m8 = sb('m8', [128, 16])
        sel = sb('sel', [128, 64])
        sel2 = sb('sel2', [128, 64])
        R = sb('R', [64, 512], BF16)
        R2 = sb('R2', [1, 512], BF16)
        R2w = sb('R2w', [1, 512], BF16)
        negm = sb('negm', [128, 8])
        mpart = sb('mpart', [128, 16])
        PTs = [sb('PTs%d' % i, [128, 512], BF16) for i in range(4)]
        rl = sb('rl', [128, 512])
        ot = sb('ot', [128, 512])
        rl2 = sb('rl2', [128, 512])
        ot2 = sb('ot2', [128, 512])

        def hq(ap):
            return ap.rearrange("p (h q) -> p h q", q=128)

        def dense_gen(i, g, kT, vk, kb0, Rrow, use_sel, gate_j, psO_, psL_, PT_, rl_, ot_):
            Q_ = Qd[i % 2]

            def scores(kb):
                ps = rs_.get()
                mms = [(kT[:, 0, kb * 128:(kb + 1) * 128], Q_[:, 0, :]), (kT[:, 1, kb * 128:(kb + 1) * 128], Q_[:, 1, :]),
                       (ones_b[0:1, :], Rrow[0:1, :])]
                if use_sel:
                    mms.append((esel[0:n_s, kb * 128:(kb + 1) * 128], R[0:n_s, :]))
                if kb == i:
                    mms.append((ident_b[:], caus4[:]))
                if (not use_sel) and i >= 4 and kb == i - 4:
                    mms.append((ident_b[:], first4[:]))
                for n, (a, b) in enumerate(mms):
                    c.mm(ps[:, :], a, b, start=(n == 0), stop=(n == len(mms) - 1))
                return ps
            nxt = scores(kb0)
            yield
            for kb in range(kb0, i + 1):
                ps = nxt
                if kb < i:
                    nxt = scores(kb + 1)
                    yield
                P_ = PT_[kb % 2]
                c.act(P_[:], ps[:, :], AF.Exp, scale=NSA_SCALE)
                yield
                c.mm(psO_[:, :], vk[:, kb, :], P_[:], start=(kb == kb0), stop=(kb == i))
                c.mm(psL_[:, :], ones_b[:], P_[:], start=(kb == kb0), stop=(kb == i))
                yield
            c.ts(rl_[:], psL_[:, :], 1e-30, None, ALU.max)
            yield
            c.recip(rl_[:], rl_[:])
            yield
            c.tt(ot_[:], psO_[:, :], rl_[:], ALU.mult)
            yield
            gv = gbc[i % 2][:].rearrange("p (h j) q -> p h j q", j=3)[:, :, gate_j, :]
            c.tt(hq(ot_[:]), hq(ot_[:]), gv, ALU.mult, eng='pool')
            yield
            c.tt(yn[:], yn[:], hq(ot_[:]), ALU.add, eng='pool')
            yield

        def drive(gens):
            while gens:
                nx = []
                for gen in gens:
                    try:
                        next(gen)
                        nx.append(gen)
                    except StopIteration:
                        pass
                gens = nx

        def rowmax(i, h, kT, k0, k1, col):
            nb = 0
            for s0 in range(k0, k1, 512):
                w = min(512, k1 - s0)
                ps = rs_.get()
                for dc in range(2):
                    c.mm(ps[:, 0:w], Qg[i % 2][:, h, dc, :], kT[:, dc, s0:s0 + w], start=(dc == 0), stop=(dc == 1))
                c.reduce(mpart[:, nb:nb + 1], ps[:, 0:w], ALU.max)
                nb += 1
            if nb > 1:
                c.reduce(negm[:, col:col + 1], mpart[:, 0:nb], ALU.max)
                c.ts(negm[:, col:col + 1], negm[:, col:col + 1], -1.0)
            else:
                c.ts(negm[:, col:col + 1], mpart[:, 0:1], -1.0)

        cnt = [0]
        rm_ = rs_
        for g in range(4):
            base = 1280
            c.dma(ksT[:], kvT[base + g * 192:base + (g + 1) * 192, :].rearrange("(a p) t -> p a t", p=96))
            c.dma(kwT[:], kvT[2560 + g * 192:2560 + (g + 1) * 192, :].rearrange("(a p) t -> p a t", p=96))
            for (src0, dst) in ((base + 768 + g * 128, vsk), (2560 + 768 + g * 128, vwk)):
                for t4 in range(T // 512):
                    vt_ = vtmp[t4 % 2]
                    c.dma(vt_[:], kvT[src0:src0 + 128, t4 * 512:(t4 + 1) * 512])
                    ps = rm_.get()
                    for j in range(4):
                        c.mm(ps[:, j * 128:(j + 1) * 128], vt_[:, j * 128:(j + 1) * 128], ident_b[:])
                    c.copy(dst[:, t4 * 4:(t4 + 1) * 4, :].rearrange("p a d -> p (a d)"), ps[:, :], eng=('act' if t4 % 2 else 'dve'))
            for i in range(NQ):
                qs = slice(i * 128, (i + 1) * 128)
                Q_ = Qg[i % 2]
                c.dma(Q_[:], S['qT'][g * 768:(g + 1) * 768, qs].rearrange("(h a p) t -> p h a t", a=2, p=96))
                for dc in range(2):
                    c.dma(Qd[i % 2][:, dc, :].rearrange("p (h q) -> p h q", q=128),
                          S['qT'][g * 768:(g + 1) * 768, qs].rearrange("(h a p) t -> p h a t", a=2, p=96)[:, :, dc, :])
                c.dma(gbc[i % 2][:], S['ngT'][g * 12:(g + 1) * 12, qs].partition_broadcast(128))
                nv = 8 * i + 7
                c.memset(Pacc[:], 0.0, eng='pool')
                def cmp_head(h):
                    ps = rs_.get()
                    for dc in range(2):
                        c.mm(ps[:, 0:nv], Q_[:, h, dc, :], kcmpT[:, dc, g, 0:nv], start=(dc == 0), stop=(dc == 1))
                        yield
                    c.reduce(st[:, 4 * h + 0:4 * h + 1], ps[:, 0:nv], ALU.max)
                    yield
                    c.ts(st[:, 4 * h + 1:4 * h + 2], st[:, 4 * h + 0:4 * h + 1], -NSA_SCALE)
                    yield
                    e_ = ee[h]
                    c.act(e_[:, 0:nv], ps[:, 0:nv], AF.Exp, bias=st[:, 4 * h + 1:4 * h + 2], scale=NSA_SCALE)
                    yield
                    lo = max(nv - 8, 0)
                    j0 = 8 - (nv - lo)
                    c.tt(e_[:, lo:nv], e_[:, lo:nv], maskc[:, j0:8], ALU.mult)
                    yield
                    c.reduce(st[:, 4 * h + 2:4 * h + 3], e_[:, 0:nv], ALU.add)
                    yield
                    c.ts(st[:, 4 * h + 2:4 * h + 3], st[:, 4 * h + 2:4 * h + 3], 1e-30, None, ALU.max)
                    yield
                    c.recip(st[:, 4 * h + 3:4 * h + 4], st[:, 4 * h + 2:4 * h + 3])
                    yield
                    c.stt(Pacc[:, 1:1 + nv], e_[:, 0:nv], st[:, 4 * h + 3:4 * h + 4], Pacc[:, 1:1 + nv], ALU.mult, ALU.add)
                    yield
                    p_ = pb[h]
                    c.act(p_[:, 0:nv], e_[:, 0:nv], AF.Copy, scale=st[:, 4 * h + 3:4 * h + 4])
                    yield
                    t_ = pT[h]
                    nblk = (nv + 127) // 128
                    for nb in range(nblk):
                        w = min(128, nv - nb * 128)
                        pst = rm_.get()
                        c.mm(pst[0:w, 0:128], p_[:, nb * 128:nb * 128 + w], ident_b[:])
                        yield
                        c.copy(t_[0:w, nb, :], pst[0:w, 0:128], eng='dve')
                        yield
                    pso = rm_.get()
                    for nb in range(nblk):
                        w = min(128, nv - nb * 128)
                        c.mm(pso[:, 0:128], vcmp[0:w, nb, g, :], t_[0:w, nb, :], start=(nb == 0), stop=(nb == nblk - 1))
                        yield
                    c.tt(yn[:, h, :], pso[:, 0:128], gbc[i % 2][:, h * 3 + 0, :], ALU.mult)
                    yield
                drive([cmp_head(h) for h in range(4)])
                c.tt(imp[:, 0:n_s], Pacc[:, 0:4 * n_s:4], Pacc[:, 1:1 + 4 * n_s:4], ALU.add)
                for j in (2, 3, 4):
                    c.tt(imp[:, 0:n_s], imp[:, 0:n_s], Pacc[:, j:j + 4 * n_s:4], ALU.add)
                c.memset(score[:, 0:n_s], -1e30)
                if i > 0:
                    c.copy(score[:, 0:2 * i], imp[:, 0:2 * i])
                c.memset(score[:, 0:1], 1e6)
                c.memset(score[:, 2 * i:2 * i + 1], 1e6)
                c.memset(score[64:128, 2 * i + 1:2 * i + 2], 1e6)
                if i >= 1:
                    c.memset(score[0:64, 2 * i - 1:2 * i], 1e6)
                c.op('dve', lambda: nc.vector.max(m8[:, 0:8], score[:, 0:n_s]), [score[:, 0:n_s]], [m8[:, 0:8]])
                c.op('dve', lambda: nc.vector.match_replace(sc2[:, 0:n_s], m8[:, 0:8], score[:, 0:n_s], -3e38),
                     [m8[:, 0:8], score[:, 0:n_s]], [sc2[:, 0:n_s]])
                c.op('dve', lambda: nc.vector.max(m8[:, 8:16], sc2[:, 0:n_s]), [sc2[:, 0:n_s]], [m8[:, 8:16]])
                c.ts(sel[:, 0:n_s], score[:, 0:n_s], m8[:, 15:16], None, ALU.is_ge)
                c.ts(sel2[:, 0:n_s], score[:, 0:n_s], -5e29, None, ALU.is_gt)
                c.tt(sel[:, 0:n_s], sel[:, 0:n_s], sel2[:, 0:n_s], ALU.mult)
                c.ts(sel[:, 0:n_s], sel[:, 0:n_s], -1.0, -NEGM, ALU.add, ALU.mult)
                pst = rm_.get()
                c.tr(pst[0:n_s, 0:128], sel[:, 0:n_s], ident[:])
                c.copy(R[0:n_s, :].rearrange("p (h q) -> p h q", q=128), pst[0:n_s, 0:128].unsqueeze(1).broadcast_to([n_s, 4, 128]), eng='act')
                for h in range(4):
                    rowmax(i, h, ksT, 0, 128 * (i + 1), h)
                    rowmax(i, h, kwT, 128 * max(0, i - 4), 128 * (i + 1), 4 + h)
                pst = rm_.get()
                for h in range(4):
                    c.mm(pst[0:1, h * 128:(h + 1) * 128], negm[:, h:h + 1], ident[:])
                c.copy(R2[:], pst[0:1, :], eng='act')
                pst = rm_.get()
                for h in range(4):
                    c.mm(pst[0:1, h * 128:(h + 1) * 128], negm[:, 4 + h:5 + h], ident[:])
                c.copy(R2w[:], pst[0:1, :], eng='act')
                drive([dense_gen(i, g, ksT, vsk, 0, R2, True, 1, banks[4], banks[5], PTs[0:2], rl, ot),
                       dense_gen(i, g, kwT, vwk, max(0, i - 4), R2w, False, 2, banks[6], banks[7], PTs[2:4], rl2, ot2)])
                yb_ = ynb[i % 2]
                c.copy(yb_[:], yn[:], eng='act')
                c.dma(S['ynsT'][g * 512:(g + 1) * 512, qs].rearrange("(h p) q -> p h q", p=128), yb_[:], q='pool')


_NET = None


def kernel(**inputs):
    global _NET
    T = 4096
    if _NET is None:
        _NET = build(T=T, nlayers=DEPTH)
    net = _NET
    cc = host_consts()
    base = {}
    for k in net.inp:
        if k == 'x':
            continue
        if k in cc:
            base[k] = cc[k]
        else:
            base[k] = np.ascontiguousarray(np.asarray(inputs[k], dtype=np.float32))
    x = np.asarray(inputs['x'], dtype=np.float32)
    in_maps = []
    for core in range(8):
        m = dict(base)
        m['x'] = np.ascontiguousarray(x[core // 2])
        in_maps.append(m)
    res = run_bass_kernel_spmd(net.nc, in_maps, core_ids=list(range(8)))
    out = np.empty((4, T, D), np.float32)
    for b in range(4):
        out[b, :T // 2] = res.results[2 * b]['out'][:T // 2]
        out[b, T // 2:] = res.results[2 * b + 1]['out'][T // 2:]
    return out
```

```python
import numpy as np
from contextlib import ExitStack
import concourse.bass as bass
import concourse.mybir as mybir
from concourse.bass_utils import run_bass_kernel_spmd

F32 = mybir.dt.float32
BF16 = mybir.dt.bfloat16
I32 = mybir.dt.int32
U8 = mybir.dt.uint8
AF = mybir.ActivationFunctionType
ALU = mybir.AluOpType
AX = mybir.AxisListType

_DS = {F32: 4, BF16: 2, I32: 4, U8: 1, mybir.dt.uint32: 4, mybir.dt.float32r: 4,
       mybir.dt.uint16: 2, mybir.dt.int16: 2}


def _foot(ap):
    name = ap.tensor.name
    es = _DS.get(ap.dtype, 4)
    dims = list(ap.ap)
    if type(ap.tensor).__name__ == 'DRamTensorHandle':
        lo = ap.offset
        hi = lo + sum((c - 1) * abs(s) for s, c in dims) + 1
        return (name, 0, 1, lo * es, hi * es)
    is_psum = 'PSum' in type(ap.tensor).__name__ or 'Psum' in type(ap.tensor).__name__
    pstep, pcnt = dims[0]
    if pstep == 0:
        pstep = None
    free = dims[1:]
    ext = sum((c - 1) * abs(s) for s, c in free) + 1
    if pstep:
        f0 = ap.offset % pstep
        p0 = ap.offset // pstep
    else:
        tsh = ap.tensor.shape
        ps_ = 1
        for d in tsh[1:]:
            ps_ *= d
        tes = _DS.get(ap.tensor.dtype, 4)
        ps_ = ps_ * tes // es
        f0 = ap.offset % ps_
        p0 = ap.offset // ps_
        pcnt = 1
    if is_psum:
        return (name, (p0 // 32) * 32, ((p0 + pcnt + 31) // 32) * 32, 0, 1 << 40)
    return (name, p0, p0 + pcnt, f0 * es, (f0 + ext) * es)


class Ctx:
    NDMA = 24

    def __init__(self, nc):
        self.nc = nc
        self.es = ExitStack()
        self.eng = {'pe': nc.tensor, 'act': nc.scalar, 'dve': nc.vector, 'pool': nc.gpsimd, 'sp': nc.sync}
        self.sem = {}
        self.cnt = {}
        for e in ['pe', 'act', 'dve', 'pool']:
            self.sem[e] = self.es.enter_context(nc.semaphore('s_' + e))
            self.cnt[e] = 0
        for i in range(self.NDMA):
            k = 'd%d' % i
            self.sem[k] = self.es.enter_context(nc.semaphore('s_' + k))
            self.cnt[k] = 0
        self.dma_rr = 0
        self.known = {e: {} for e in ['pe', 'act', 'dve', 'pool', 'sp']}
        self.rec = {}
        self.ninst = 0

    def sb(self, name, shape, dt=F32, stack=None):
        self.uid = getattr(self, 'uid', 0) + 1
        return (stack or self.es).enter_context(self.nc.sbuf_tensor('%s_%d' % (name, self.uid), list(shape), dt))

    def ps(self, name, shape, dt=F32, stack=None):
        self.uid = getattr(self, 'uid', 0) + 1
        return (stack or self.es).enter_context(self.nc.psum_tensor('%s_%d' % (name, self.uid), list(shape), dt))

    def barrier(self):
        deps = {k: c for k, c in self.cnt.items() if c > 0}
        for e in ['pe', 'act', 'dve', 'pool', 'sp']:
            self._waits(e, dict(deps))

    def _deps(self, reads, writes, me):
        deps = {}
        for ap in reads:
            f = _foot(ap)
            for r in self.rec.get(f[0], ()):
                if r[6] and r[0] < f[2] and f[1] < r[1] and r[2] < f[4] and f[3] < r[3]:
                    if r[5] > deps.get(r[4], 0):
                        deps[r[4]] = r[5]
        for ap in writes:
            f = _foot(ap)
            for r in self.rec.get(f[0], ()):
                if r[0] < f[2] and f[1] < r[1] and r[2] < f[4] and f[3] < r[3]:
                    if r[4] == me and not r[6]:
                        continue
                    if r[5] > deps.get(r[4], 0):
                        deps[r[4]] = r[5]
        if me == 'pe':
            deps.pop('pe', None)
        return deps

    def _record(self, reads, writes, me, cnt):
        for ap in writes:
            f = _foot(ap)
            lst = self.rec.setdefault(f[0], [])
            lst[:] = [r for r in lst if not (f[1] <= r[0] and r[1] <= f[2] and f[3] <= r[2] and r[3] <= f[4])]
            lst.append([f[1], f[2], f[3], f[4], me, cnt, True])
        for ap in reads:
            f = _foot(ap)
            lst = self.rec.setdefault(f[0], [])
            lst[:] = [r for r in lst if not (r[4] == me and not r[6] and f[1] <= r[0] and r[1] <= f[2]
                                             and f[3] <= r[2] and r[3] <= f[4])]
            lst.append([f[1], f[2], f[3], f[4], me, cnt, False])

    def _waits(self, issuer, deps):
        e = self.eng[issuer]
        kn = self.known[issuer]
        for k, c in deps.items():
            if kn.get(k, 0) < c:
                e.wait_ge(self.sem[k], c)
                kn[k] = c

    def op(self, engname, fn, reads, writes):
        xr = [a for a in reads if 'PSum' in type(a.tensor).__name__]
        if xr:
            writes = list(writes) + xr
            for a in xr:
                self._ps_guard_read(a)
        deps = self._deps(reads, writes, engname)
        self._waits(engname, deps)
        inst = fn()
        self.cnt[engname] += 1
        inst.then_inc(self.sem[engname], 1)
        self._record(reads, writes, engname, self.cnt[engname])
        self.ninst += 1
        return inst

    def dma(self, out, in_, q='sp', **kw):
        k = 'd%d' % self.dma_rr
        self.dma_rr = (self.dma_rr + 1) % self.NDMA
        deps = self._deps([in_], [out], k)
        if self.cnt[k] > 0:
            deps[k] = max(deps.get(k, 0), self.cnt[k])
        self._waits(q, deps)
        inst = self.eng[q].dma_start(out=out, in_=in_, **kw)
        self.cnt[k] += 16
        inst.then_inc(self.sem[k], 16)
        self._record([in_], [out], k, self.cnt[k])
        self.ninst += 1
        return inst

    def wait_all(self, issuer='sp'):
        deps = {k: c for k, c in self.cnt.items() if c > 0}
        self._waits(issuer, deps)

    def _r(self, out):
        r32 = self.__dict__.get('r32', ())
        if r32 and out.dtype == F32 and out.tensor.name in r32:
            return out.bitcast(mybir.dt.float32r)
        return out

    def _ps_cols(self, ap):
        dims = list(ap.ap)
        es = _DS.get(ap.dtype, 4)
        pstep, pcnt = dims[0]
        ext = sum((c - 1) * abs(st) for st, c in dims[1:]) + 1
        f0 = ap.offset % pstep if pstep else 0
        p0 = ap.offset // pstep if pstep else 0
        return ap.tensor.name, p0, p0 + pcnt, f0 * es, (f0 + ext) * es

    def _ps_guard_write(self, out, start):
        name, p0, p1, b0, b1 = self._ps_cols(out)
        lst = self.__dict__.setdefault('pw', {}).setdefault(name, [])
        if start:
            for r in lst:
                assert not (r[0] < p1 and p0 < r[1] and r[2] < b1 and b0 < r[3]), \
                    'PSUM overwrite of unread matmul result in %s %s' % (name, (p0, p1, b0, b1, r))
            lst.append([p0, p1, b0, b1])

    def _ps_guard_read(self, ap):
        name, p0, p1, b0, b1 = self._ps_cols(ap)
        lst = self.__dict__.setdefault('pw', {}).get(name)
        if lst:
            lst[:] = [r for r in lst if not (r[0] < p1 and p0 < r[1] and r[2] < b1 and b0 < r[3])]

    def mm(self, out, lhsT, rhs, start=True, stop=True):
        self._ps_guard_write(out, start)
        r32 = self.__dict__.get('r32', ())
        if (r32 and lhsT.dtype == F32 and rhs.dtype == F32 and lhsT.tensor.name in r32 and rhs.tensor.name in r32
                and _foot(out)[1] == 0):
            lhsT = lhsT.bitcast(mybir.dt.float32r)
            rhs = rhs.bitcast(mybir.dt.float32r)
        return self.op('pe', lambda: self.nc.tensor.matmul(out, lhsT, rhs, start=start, stop=stop), [lhsT, rhs], [out])

    def tr(self, out, in_, ident):
        self._ps_guard_write(out, True)
        return self.op('pe', lambda: self.nc.tensor.transpose(out, in_, ident), [in_, ident], [out])

    def act(self, out, in_, func, bias=None, scale=None, accum_out=None, eng='act'):
        out = self._r(out)
        kw = {}
        rd = [in_]
        if bias is not None:
            kw['bias'] = bias
            if not isinstance(bias, (int, float)):
                rd.append(bias)
        if scale is not None:
            kw['scale'] = scale
            if not isinstance(scale, (int, float)):
                rd.append(scale)
        wr = [out]
        if accum_out is not None:
            kw['accum_out'] = accum_out
            wr.append(accum_out)
        return self.op('act', lambda: self.nc.scalar.activation(out, in_, func, **kw), rd, wr)

    def tt(self, out, in0, in1, op, eng='dve'):
        out = self._r(out)
        return self.op(eng, lambda: self.eng[eng].tensor_tensor(out, in0, in1, op), [in0, in1], [out])

    def ts(self, out, in0, s1, s2=None, op0=ALU.mult, op1=None, eng='dve', accum_out=None):
        out = self._r(out)
        rd = [in0]
        if not isinstance(s1, (int, float)):
            rd.append(s1)
        if s2 is not None and not isinstance(s2, (int, float)):
            rd.append(s2)
        kw = {}
        wr = [out]
        if accum_out is not None:
            kw['accum_out'] = accum_out
            wr.append(accum_out)
        if op1 is None:
            return self.op(eng, lambda: self.eng[eng].tensor_scalar(out, in0, s1, None, op0, **kw), rd, wr)
        return self.op(eng, lambda: self.eng[eng].tensor_scalar(out, in0, s1, s2, op0, op1, **kw), rd, wr)

    def stt(self, out, in0, scalar, in1, op0, op1, accum_out=None):
        out = self._r(out)
        rd = [in0, in1]
        if not isinstance(scalar, (int, float)):
            rd.append(scalar)
        wr = [out]
        kw = {}
        if accum_out is not None:
            kw['accum_out'] = accum_out
            wr.append(accum_out)
        return self.op('dve', lambda: self.nc.vector.scalar_tensor_tensor(out, in0, scalar, in1, op0, op1, **kw), rd, wr)

    def copy(self, out, in_, eng='dve'):
        out = self._r(out)
        if eng == 'act':
            return self.op('act', lambda: self.nc.scalar.copy(out, in_), [in_], [out])
        return self.op(eng, lambda: self.eng[eng].tensor_copy(out, in_), [in_], [out])

    def memset(self, ap, v, eng='dve'):
        return self.op(eng, lambda: self.eng[eng].memset(ap, v), [], [ap])

    def reduce(self, out, in_, op=ALU.add, axis=AX.X, eng='dve'):
        return self.op(eng, lambda: self.eng[eng].tensor_reduce(out, in_, axis, op), [in_], [out])

    def recip(self, out, in_):
        return self.op('dve', lambda: self.nc.vector.reciprocal(out, in_), [in_], [out])


D = 2048
DFF = 5632
KT = D // 128
FT = DFF // 128
DEPTH = 2
ALPHA = (2 * DEPTH) ** 0.25
LN_EPS = 1e-5
RW_COLS = 6592
OFF_GM = 6592
OFF_NSA = 10688
OFF_GATE = 17648
C_IN = 23792
TT = 512


class Net:
    def __init__(self, T):
        self.T = T
        nc = bass.Bass("TRN2", target_bir_lowering=False)
        self.nc = nc
        self.c = Ctx(nc)
        self.inp = {}
        self.psn = 0

    def din(self, name, shape, dt=F32):
        t = self.nc.dram_tensor(name, list(shape), dt, kind="ExternalInput").ap()
        self.inp[name] = t
        return t

    def dout(self, name, shape, dt=F32):
        return self.nc.dram_tensor(name, list(shape), dt, kind="ExternalOutput").ap()

    def dscr(self, name, shape, dt=F32):
        return self.nc.dram_tensor(name, list(shape), dt, kind="Internal").ap()


def conv_weight(c, cva, cvb, src2d, dst2d, rows, cols, k):
    CH = 4096
    s = src2d.rearrange("(p r) c -> p (r c)", p=128)
    d = dst2d.rearrange("(p r) c -> p (r c)", p=128)
    n = (rows // 128) * cols
    i = 0
    while i < n:
        m = min(CH, n - i)
        a = cva[k % len(cva)]
        b = cvb[k % len(cvb)]
        c.dma(a[:, 0:m], s[:, i:i + m], q='sp')
        e = ['act', 'dve'][k % 2]
        c.copy(b[:, 0:m], a[:, 0:m], eng=e)
        c.dma(d[:, i:i + m], b[:, 0:m], q='pool')
        i += m
        k += 1
    return k


class PsumPool:
    def __init__(self, c, n=8):
        self.banks = [c.ps("psb%d" % i, [128, 512]) for i in range(n)]
        self.i = 0

    def get(self):
        b = self.banks[self.i % len(self.banks)]
        self.i += 1
        return b


def layer_norm_fm(c, pp, st, z, g, b, ntok, consts):
    zb = st['zb']
    zq = st['zq']
    ones = consts['ones_bf']
    c.copy(zb[:, :, 0:ntok], z[:, :, 0:ntok], eng='act')
    c.act(zq[:, :, 0:ntok], z[:, :, 0:ntok], AF.Square)
    ps1 = pp.get()
    ps2 = pp.get()
    for k in range(KT):
        c.mm(ps1[:, 0:ntok], ones[:, :], zb[:, k, 0:ntok], start=(k == 0), stop=(k == KT - 1))
    for k in range(KT):
        c.mm(ps2[:, 0:ntok], ones[:, :], zq[:, k, 0:ntok], start=(k == 0), stop=(k == KT - 1))
    mean = st['mean']
    rstd = st['rstd']
    tmp = st['tmp512']
    c.act(mean[:, 0:ntok], ps1[:, 0:ntok], AF.Copy, scale=1.0 / D)
    c.tt(tmp[:, 0:ntok], mean[:, 0:ntok], mean[:, 0:ntok], ALU.mult)
    c.stt(rstd[:, 0:ntok], ps2[:, 0:ntok], 1.0 / D, tmp[:, 0:ntok], ALU.mult, ALU.subtract)
    c.ts(rstd[:, 0:ntok], rstd[:, 0:ntok], LN_EPS, None, ALU.add)
    c.act(rstd[:, 0:ntok], rstd[:, 0:ntok], AF.Sqrt)
    c.recip(rstd[:, 0:ntok], rstd[:, 0:ntok])
    for k in range(KT):
        e = 'dve' if k % 2 == 0 else 'pool'
        c.tt(z[:, k, 0:ntok], z[:, k, 0:ntok], mean[:, 0:ntok], ALU.subtract, eng=e)
        c.tt(z[:, k, 0:ntok], z[:, k, 0:ntok], rstd[:, 0:ntok], ALU.mult, eng=e)
        c.ts(z[:, k, 0:ntok], z[:, k, 0:ntok], g[:, k:k + 1], b[:, k:k + 1], ALU.mult, ALU.add, eng='dve')
    c.copy(zb[:, :, 0:ntok], z[:, :, 0:ntok], eng='act')


def ffn_tile(c, pp, st, xs, xb, wg, wu, wd, ntok):
    hT = st['hT']
    FW = 256
    for f0 in range(0, DFF, FW):
        i = (f0 // FW) % 2
        wgs = st['wgs'][i]
        wus = st['wus'][i]
        c.dma(wgs[:], wg[:, f0:f0 + FW].rearrange("(kt p) m -> p kt m", p=128), q='sp')
        c.dma(wus[:], wu[:, f0:f0 + FW].rearrange("(kt p) m -> p kt m", p=128), q='sp')
        for j in range(FW // 128):
            f = f0 // 128 + j
            psg = pp.get()
            psu = pp.get()
            for k in range(KT):
                c.mm(psg[:, 0:ntok], wgs[:, k, j * 128:(j + 1) * 128], xb[:, k, 0:ntok], start=(k == 0), stop=(k == KT - 1))
            for k in range(KT):
                c.mm(psu[:, 0:ntok], wus[:, k, j * 128:(j + 1) * 128], xb[:, k, 0:ntok], start=(k == 0), stop=(k == KT - 1))
            sg = st['sg'][f % 2]
            c.act(sg[:, 0:ntok], psg[:, 0:ntok], AF.Silu)
            c.tt(hT[:, f, 0:ntok], sg[:, 0:ntok], psu[:, 0:ntok], ALU.mult)
    c.ts(xs[:, :, 0:ntok], xs[:, :, 0:ntok], ALPHA, eng='pool')
    for d in range(KT):
        wds = st['wds'][d % 2]
        c.dma(wds[:], wd[:, d * 128:(d + 1) * 128].rearrange("(ft p) m -> p ft m", p=128), q='sp')
        ps = pp.get()
        for f in range(FT):
            c.mm(ps[:, 0:ntok], wds[:, f, :], hT[:, f, 0:ntok], start=(f == 0), stop=(f == FT - 1))
        c.stt(xs[:, d, 0:ntok], ps[:, 0:ntok], 0.5, xs[:, d, 0:ntok], ALU.mult, ALU.add)


def run_slabs(slabs, bufs, load_fn, compute_fn):
    if not slabs:
        return
    load_fn(slabs[0], bufs[0])
    for n, s in enumerate(slabs):
        if n + 1 < len(slabs):
            load_fn(slabs[n + 1], bufs[(n + 1) % len(bufs)])
        compute_fn(s, bufs[n % len(bufs)])


def make_slabs(tiles, maxw=512):
    slabs = []
    cur = None
    for (c0, ms, meta) in tiles:
        if cur is not None and cur[0] + cur[1] == c0 and cur[1] + ms <= maxw:
            cur[1] += ms
            cur[2].append((c0, ms, meta))
        else:
            cur = [c0, ms, [(c0, ms, meta)]]
            slabs.append(cur)
    return slabs


def fm_proj(c, pp, wsl, Wb2d, tiles, xb, ntok, evac, nk=KT, pre=None):
    slabs = make_slabs(tiles)

    def load(s, buf):
        c.dma(buf[:, 0:nk, 0:s[1]], Wb2d[:, s[0]:s[0] + s[1]].rearrange("(kt p) m -> p kt m", p=128), q='sp')

    def comp(s, buf):
        for (c0, ms, meta) in s[2]:
            o = c0 - s[0]
            if pre is not None:
                pre(c0, ms, meta)
            ps = pp.get()
            for k in range(nk):
                c.mm(ps[0:ms, 0:ntok], buf[:, k, o:o + ms], xb[:, k, 0:ntok], start=(k == 0), stop=(k == nk - 1))
            evac(ps, c0, ms, meta)
    run_slabs(slabs, wsl, load, comp)


def ffn_pass(c, pp, T, lng, lnb, consts, xT_src, xT_dst, wg, wu, wd, l, lni):
    with ExitStack() as es:
        st = {}
        st['zb'] = c.sb('zb', [128, KT, TT], BF16, stack=es)
        st['zq'] = c.sb('zq', [128, KT, TT], BF16, stack=es)
        st['mean'] = c.sb('mean', [128, TT], stack=es)
        st['rstd'] = c.sb('rstd', [128, TT], stack=es)
        st['tmp512'] = c.sb('tmp512', [128, TT], stack=es)
        st['xs'] = c.sb('xs', [128, KT, TT], stack=es)
        st['hT'] = c.sb('hT', [128, FT, TT], BF16, stack=es)
        st['wgs'] = [c.sb('wgs%d' % i, [128, KT, 256], BF16, stack=es) for i in range(2)]
        st['wus'] = [c.sb('wus%d' % i, [128, KT, 256], BF16, stack=es) for i in range(2)]
        st['wds'] = [c.sb('wds%d' % i, [128, FT, 128], BF16, stack=es) for i in range(2)]
        st['sg'] = [c.sb('sg%d' % i, [128, TT], stack=es) for i in range(2)]
        for tt in range(T // TT):
            xs = st['xs']
            c.dma(xs[:], xT_src[:, tt * TT:(tt + 1) * TT].rearrange("(k p) t -> p k t", p=128))
            c.copy(st['zb'][:], xs[:], eng='act')
            ffn_tile(c, pp, st, xs, st['zb'], wg, wu, wd, TT)
            layer_norm_fm(c, pp, st, xs, lng[:, l * 3 + lni, :], lnb[:, l * 3 + lni, :], TT, consts)
            c.dma(xT_dst[:, tt * TT:(tt + 1) * TT].rearrange("(k p) t -> p k t", p=128), xs[:], q='pool')


def inproj_pass(c, pp, T, l, W, Wb, S, xT_src):
    win = Wb['w_in'][l]
    tiles = []
    for i in range(48):
        tiles.append((i * 128, 128, ('rw', i)))
    tiles.append((6144, 96, ('rw', 48)))
    tiles.append((6240, 96, ('rw', 49)))
    tiles.append((6336, 128, ('rw', 50)))
    tiles.append((6464, 128, ('rw', 51)))
    for i in range(16):
        tiles.append((OFF_GM + i * 128, 128, ('gmu', i)))
    for i in range(24):
        tiles.append((OFF_NSA + i * 128, 128, ('q', i)))
    for i in range(30):
        tiles.append((OFF_NSA + 3072 + i * 128, 128, ('kv', i)))
    tiles.append((OFF_NSA + 6912, 48, ('ng', 0)))
    for i in range(48):
        tiles.append((OFF_GATE + i * 128, 128, ('gate', i)))
    with ExitStack() as es:
        xs = c.sb('ip_xs', [128, KT, TT], stack=es)
        xb = c.sb('ip_xb', [128, KT, TT], BF16, stack=es)
        wsl = [c.sb('ip_w%d' % i, [128, KT, 512], BF16, stack=es) for i in range(2)]
        wtk = [c.sb('ip_wt%d' % i, [128, KT, 512], BF16, stack=es) for i in range(2)]
        mu = c.sb('ip_mu', [128, 52], stack=es)
        carry = c.sb('ip_carry', [128, 52], stack=es)
        pbuf = [c.sb('ip_pb%d' % i, [128, TT + 1], stack=es) for i in range(2)]
        dbuf = [c.sb('ip_db%d' % i, [128, TT], stack=es) for i in range(2)]
        obuf = [c.sb('ip_ob%d' % i, [128, TT], stack=es) for i in range(3)]
        obb = [c.sb('ip_obb%d' % i, [128, TT], BF16, stack=es) for i in range(3)]
        vbuf = [c.sb('ip_vb%d' % i, [128, 512], stack=es) for i in range(2)]
        c.memset(carry[:], 0.0)
        c.memset(mu[:], 0.0)
        for (c0, ms, meta) in tiles:
            if meta[0] == 'rw':
                c.dma(mu[0:ms, meta[1]:meta[1] + 1], W['rw_mu'][l:l + 1, c0:c0 + ms].rearrange("o m -> m o"), q='sp', allow_slow_non_contiguous=True)
        cnt = [0]
        for tt in range(T // TT):
            ts_ = slice(tt * TT, (tt + 1) * TT)
            c.dma(xs[:], xT_src[:, ts_].rearrange("(k p) t -> p k t", p=128))
            c.copy(xb[:], xs[:], eng='act')

            def evac(ps, c0, ms, meta):
                kind, idx = meta
                n = cnt[0]
                cnt[0] += 1
                if kind == 'rw':
                    pb = pbuf[n % 2]
                    db = dbuf[n % 2]
                    c.copy(pb[0:ms, 0:1], carry[0:ms, idx:idx + 1], eng='dve')
                    c.copy(pb[0:ms, 1:TT + 1], ps[0:ms, 0:TT], eng='act')
                    c.copy(carry[0:ms, idx:idx + 1], pb[0:ms, TT:TT + 1], eng='dve')
                    c.tt(db[0:ms, :], pb[0:ms, 0:TT], pb[0:ms, 1:TT + 1], ALU.subtract)
                    c.stt(db[0:ms, :], db[0:ms, :], mu[0:ms, idx:idx + 1], pb[0:ms, 1:TT + 1], ALU.mult, ALU.add)
                    c.dma(S['pRW'][c0:c0 + ms, ts_], db[0:ms, :], q='pool')
                elif kind == 'gmu':
                    ob = obuf[n % 3]
                    c.act(ob[0:ms, :], ps[0:ms, 0:TT], AF.Gelu)
                    c.dma(S['uT'][idx * 128:idx * 128 + ms, ts_], ob[0:ms, :], q='pool')
                elif kind in ('q', 'kv'):
                    ob = obb[n % 3]
                    c.copy(ob[0:ms, :], ps[0:ms, 0:TT], eng=('act' if n % 2 else 'dve'))
                    dst = S['qT'] if kind == 'q' else S['kvT']
                    c.dma(dst[idx * 128:idx * 128 + ms, ts_], ob[0:ms, :], q='pool')
                elif kind == 'ng':
                    ob = obuf[n % 3]
                    c.act(ob[0:ms, :], ps[0:ms, 0:TT], AF.Sigmoid)
                    c.dma(S['ngT'][0:ms, ts_], ob[0:ms, :], q='pool')
                elif kind == 'gate':
                    ob = obuf[n % 3]
                    c.act(ob[0:ms, :], ps[0:ms, 0:TT], AF.Sigmoid)
                    c.dma(S['gateT'][idx * 128:idx * 128 + ms, ts_], ob[0:ms, :], q='pool')
            fm_proj(c, pp, wsl, win, tiles, xb, TT, evac)
            vslabs = [(OFF_GM + 2048 + j * 512, 512) for j in range(4)]

            def vload(s, buf):
                c.dma(buf[:], win[:, s[0]:s[0] + 512].rearrange("(kt p) m -> p kt m", p=128), q='sp')

            def vcomp(s, buf):
                for tq in range(TT // 128):
                    ps = pp.get()
                    for k in range(KT):
                        c.mm(ps[:, :], xb[:, k, tq * 128:(tq + 1) * 128], buf[:, k, :], start=(k == 0), stop=(k == KT - 1))
                    vb = vbuf[cnt[0] % 2]
                    cnt[0] += 1
                    c.act(vb[:], ps[:], AF.Gelu)
                    j0 = s[0] - OFF_GM - 2048
                    c.dma(S['vtok'][tt * TT + tq * 128:tt * TT + (tq + 1) * 128, j0:j0 + 512], vb[:], q='pool')
            run_slabs(vslabs, wtk, vload, vcomp)


def gmlp_pass(c, pp, T, l, W, S, consts):
    with ExitStack() as es:
        gbc = c.sb('gmt_g', [128, D], stack=es)
        bbc = c.sb('gmt_b', [128, D], stack=es)
        bsb = c.sb('gmt_bs', [128, 16, 128], stack=es)
        wst = c.sb('gm_wsT', [128, 16, 128], BF16, stack=es)
        wraw = [c.sb('gm_wr%d' % i, [128, 128], stack=es) for i in range(2)]
        c.dma(gbc[:], W['gm_ln_g'][l:l + 1, :].partition_broadcast(128).rearrange("p o d -> p (o d)"), q='sp')
        c.dma(bbc[:], W['gm_ln_b'][l:l + 1, :].partition_broadcast(128).rearrange("p o d -> p (o d)"), q='sp')
        c.dma(bsb[:].rearrange("p g t -> p (g t)"), W['gm_bs'][l:l + 1].rearrange("o g t -> o (g t)").partition_broadcast(128).rearrange("p o d -> p (o d)"), q='sp')
        triu = consts['triu']
        for g in range(16):
            wr = wraw[g % 2]
            c.dma(wr[:], W['gm_ws'][l, g])
            ps = pp.get()
            c.tr(ps[:, 0:128], wr[:], consts['ident'][:])
            c.tt(wst[:, g, :], ps[:, 0:128], triu[:], ALU.mult)
        vt = [c.sb('gm_v%d' % i, [128, D], stack=es) for i in range(2)]
        vc = c.sb('gm_vc', [128, D], stack=es)
        vn = c.sb('gm_vn', [128, D], BF16, stack=es)
        junk = c.sb('gm_junk', [128, D], BF16, stack=es)
        stat = c.sb('gm_stat', [128, 8], stack=es)
        ut = [c.sb('gm_u%d' % i, [128, 16, 128], stack=es) for i in range(2)]
        yt = [c.sb('gm_y%d' % i, [128, 16, 128], BF16, stack=es) for i in range(2)]
        tmp = c.sb('gm_tmp', [128, 512], stack=es)
        for ch in range(T // 128):
            v = vt[ch % 2]
            u = ut[ch % 2]
            y = yt[ch % 2]
            c.dma(v[:], S['vtok'][ch * 128:(ch + 1) * 128, :])
            c.dma(u[:], S['uT'][:, ch * 128:(ch + 1) * 128].rearrange("(g p) t -> p g t", p=128))
            c.reduce(stat[:, 0:1], v[:], ALU.add)
            c.ts(stat[:, 1:2], stat[:, 0:1], 1.0 / D)
            c.ts(vc[:], v[:], stat[:, 1:2], None, ALU.subtract)
            c.act(junk[:], vc[:], AF.Square, accum_out=stat[:, 2:3])
            c.ts(stat[:, 3:4], stat[:, 2:3], 1.0 / D, LN_EPS, ALU.mult, ALU.add)
            c.act(stat[:, 3:4], stat[:, 3:4], AF.Sqrt)
            c.recip(stat[:, 4:5], stat[:, 3:4])
            c.stt(vc[:], vc[:], stat[:, 4:5], gbc[:], ALU.mult, ALU.mult)
            c.tt(vn[:], vc[:], bbc[:], ALU.add)
            for g4 in range(4):
                ps = pp.get()
                for j in range(4):
                    g = g4 * 4 + j
                    c.mm(ps[:, j * 128:(j + 1) * 128], vn[:, g * 128:(g + 1) * 128], wst[:, g, :])
                c.tt(tmp[:], ps[:], bsb[:, g4 * 4:(g4 + 1) * 4, :].rearrange("p g t -> p (g t)"), ALU.add)
                c.tt(y[:, g4 * 4:(g4 + 1) * 4, :].rearrange("p g t -> p (g t)"), tmp[:], u[:, g4 * 4:(g4 + 1) * 4, :].rearrange("p g t -> p (g t)"), ALU.mult)
            c.dma(S['ygmT'][:, ch * 128:(ch + 1) * 128].rearrange("(g p) t -> p g t", p=128), y[:], q='pool')


def merge_pass(c, pp, T, l, W, Wb, S, lng, lnb, consts, xT_src, xT_dst, branches):
    with ExitStack() as es:
        st = {}
        st['zb'] = c.sb('zb', [128, KT, TT], BF16, stack=es)
        st['zq'] = c.sb('zq', [128, KT, TT], BF16, stack=es)
        st['mean'] = c.sb('mean', [128, TT], stack=es)
        st['rstd'] = c.sb('rstd', [128, TT], stack=es)
        st['tmp512'] = c.sb('tmp512', [128, TT], stack=es)
        xs = c.sb('xs', [128, KT, TT], stack=es)
        mg = c.sb('mg', [128, KT, TT], stack=es)
        yb = [c.sb('mg_y%d' % i, [128, KT, TT], BF16, stack=es) for i in range(2)]
        wsl = [c.sb('mg_w%d' % i, [128, KT, 512], BF16, stack=es) for i in range(2)]
        gt = [c.sb('mg_g%d' % i, [128, TT], stack=es) for i in range(3)]
        tmp = [c.sb('mg_t%d' % i, [128, TT], stack=es) for i in range(2)]
        ysrc = {0: S['yrwT'], 1: S['ygmT'], 2: S['ynsT']}
        tiles = [(m * 128, 128, m) for m in range(KT)]
        cnt = [0]
        for tt in range(T // TT):
            ts_ = slice(tt * TT, (tt + 1) * TT)
            c.dma(xs[:], xT_src[:, ts_].rearrange("(k p) t -> p k t", p=128))
            first = True
            for bi, i in enumerate(branches):
                y = yb[bi % 2]
                c.dma(y[:], ysrc[i][:, ts_].rearrange("(k p) t -> p k t", p=128))

                gq = []

                def pre(c0, ms, m, i=i, gq=gq):
                    g = gt[cnt[0] % 3]
                    c.dma(g[:], S['gateT'][i * D + m * 128:i * D + (m + 1) * 128, ts_], q='sp')
                    gq.append(g)

                def evac(ps, c0, ms, m, i=i, first=first, gq=gq):
                    n = cnt[0]
                    cnt[0] += 1
                    g = gq.pop(0)
                    if first:
                        c.tt(mg[:, m, :], ps[:, 0:TT], g[:], ALU.mult)
                    else:
                        t_ = tmp[n % 2]
                        c.tt(t_[:], ps[:, 0:TT], g[:], ALU.mult)
                        c.tt(mg[:, m, :], mg[:, m, :], t_[:], ALU.add, eng='pool')
                fm_proj(c, pp, wsl, Wb['w_br'][l, i], tiles, y, TT, evac, pre=pre)
                first = False
            c.copy(st['zb'][:], mg[:], eng='act')

            def evac2(ps, c0, ms, m):
                c.stt(xs[:, m, :], xs[:, m, :], ALPHA, ps[:, 0:TT], ALU.mult, ALU.add)
            fm_proj(c, pp, wsl, Wb['w_o'][l], tiles, st['zb'], TT, evac2)
            layer_norm_fm(c, pp, st, xs, lng[:, l * 3 + 1, :], lnb[:, l * 3 + 1, :], TT, consts)
            c.dma(xT_dst[:, ts_].rearrange("(k p) t -> p k t", p=128), xs[:], q='pool')


WSHAPES = {
    'w_in': [DEPTH, D, C_IN], 'ffn1_wg': [DEPTH, D, DFF], 'ffn1_wu': [DEPTH, D, DFF], 'ffn1_wd': [DEPTH, DFF, D],
    'ffn2_wg': [DEPTH, D, DFF], 'ffn2_wu': [DEPTH, D, DFF], 'ffn2_wd': [DEPTH, DFF, D],
    'w_br': [DEPTH, 3, D, D], 'w_o': [DEPTH, D, D],
    'ln_g': [DEPTH, 3, D], 'ln_b': [DEPTH, 3, D],
    'rw_mu': [DEPTH, RW_COLS], 'rw_w0': [DEPTH, D], 'rw_w2': [DEPTH, 96, D], 'rw_a0': [DEPTH, D], 'rw_a2': [DEPTH, 96, D],
    'rw_g2': [DEPTH, 256, D], 'rw_v0': [DEPTH - 1, D], 'rw_v1': [DEPTH - 1, D, 64], 'rw_v2': [DEPTH - 1, 64, D],
    'rw_k_k': [DEPTH, D], 'rw_k_a': [DEPTH, D], 'rw_r_k': [DEPTH, 32, 64], 'rw_gn_g': [DEPTH, D], 'rw_gn_b': [DEPTH, D],
    'gm_ln_g': [DEPTH, D], 'gm_ln_b': [DEPTH, D], 'gm_ws': [DEPTH, 16, 128, 128], 'gm_bs': [DEPTH, 16, 128],
    'nsa_pos_k': [DEPTH, 32, 192], 'nsa_pos_v': [DEPTH, 32, 128], 'nsa_phi_k1': [DEPTH, 6144, 192], 'nsa_phi_k2': [DEPTH, 192, 192],
    'nsa_phi_v1': [DEPTH, 4096, 128], 'nsa_phi_v2': [DEPTH, 128, 128],
}
BIGW = ['w_in', 'ffn1_wg', 'ffn1_wu', 'ffn1_wd', 'ffn2_wg', 'ffn2_wu', 'ffn2_wd', 'w_br', 'w_o']


def build(T=4096, nlayers=DEPTH, passes=('ffn1', 'inproj', 'gmlp', 'rwkv', 'nsa', 'merge', 'ffn2'), branches=(0, 1, 2),
          use=None, debug=False):
    net = Net(T)
    c = net.c
    nc = net.nc
    x_in = net.din('x', [T, D])
    CONST_SHAPES = {'c_ident': [128, 128], 'c_triu': [128, 128], 'c_bones': [128, 128], 'c_maskqr': [128, 128],
                    'c_masksl': [128, 64], 'c_reset': [128, 512], 'c_ident2': [128, 64], 'c_maskc': [128, 8],
                    'c_esel': [64, 4096], 'c_caus4': [128, 512], 'c_first4': [128, 512]}
    cd = {k: net.din(k, s) for k, s in CONST_SHAPES.items()}
    W = {}
    for k, s in WSHAPES.items():
        if use is None or k in use:
            W[k] = net.din(k, s)
    out = net.dout('out', [T, D])

    mk = net.dout if debug else net.dscr
    xT = [net.dscr('xT%d' % i, [D, T]) for i in range(2)]
    Wb = {}
    for k in BIGW:
        if k in W:
            Wb[k] = net.dscr(k + '_b', WSHAPES[k], BF16)
    S = {
        'pRW': mk('s_pRW', [RW_COLS, T]), 'uT': mk('s_uT', [D, T]), 'vtok': mk('s_vtok', [T, D]),
        'qT': mk('s_qT', [3072, T], BF16), 'kvT': mk('s_kvT', [3840, T], BF16), 'ngT': mk('s_ngT', [48, T]),
        'gateT': mk('s_gateT', [3 * D, T]),
        'yrwT': mk('s_yrwT', [D, T], BF16), 'ygmT': mk('s_ygmT', [D, T], BF16), 'ynsT': mk('s_ynsT', [D, T], BF16),
        'vfirstT': net.dscr('s_vfirstT', [D, T]),
    }
    net.S = S

    ident = c.sb('ident', [128, 128])
    c.dma(ident[:], cd['c_ident'])
    triu = c.sb('triu', [128, 128])
    c.dma(triu[:], cd['c_triu'])
    ones_bf = c.sb('ones_bf', [128, 128], BF16)
    c.memset(ones_bf[:], 1.0)
    consts = {'ident': ident, 'ones_bf': ones_bf, 'triu': triu}
    consts['d_esel'] = cd['c_esel']
    consts['d_caus4'] = cd['c_caus4']
    consts['d_first4'] = cd['c_first4']
    for nm in ['bones', 'maskqr', 'masksl', 'reset', 'ident2', 'maskc']:
        consts[nm] = c.sb('k_' + nm, CONST_SHAPES['c_' + nm])
        c.dma(consts[nm][:], cd['c_' + nm])
    lng = c.sb('lng', [128, DEPTH * 3, KT])
    lnb = c.sb('lnb', [128, DEPTH * 3, KT])
    c.dma(lng[:], W['ln_g'].rearrange("l i (k p) -> p (l i) k", p=128), q='sp', allow_slow_non_contiguous=True)
    c.dma(lnb[:], W['ln_b'].rearrange("l i (k p) -> p (l i) k", p=128), q='sp', allow_slow_non_contiguous=True)
    pp = PsumPool(c)

    with ExitStack() as es:
        cva = [c.sb('cva%d' % i, [128, 4096], F32, stack=es) for i in range(4)]
        cvb = [c.sb('cvb%d' % i, [128, 4096], BF16, stack=es) for i in range(4)]
        k = 0
        for name in Wb:
            for l in range(nlayers):
                s = WSHAPES[name]
                if name == 'w_br':
                    for i in range(3):
                        k = conv_weight(c, cva, cvb, W[name][l, i], Wb[name][l, i], s[2], s[3], k)
                else:
                    k = conv_weight(c, cva, cvb, W[name][l], Wb[name][l], s[1], s[2], k)

    c.barrier()
    with ExitStack() as es:
        xtok = [c.sb('xtok%d' % i, [128, D], F32, stack=es) for i in range(2)]
        xo = [c.sb('xo%d' % i, [128, 4, 128], F32, stack=es) for i in range(2)]
        n = 0
        for t in range(T // 128):
            xt = xtok[t % 2]
            c.dma(xt[:], x_in[t * 128:(t + 1) * 128, :])
            for g4 in range(KT // 4):
                ps = pp.get()
                for j in range(4):
                    k = g4 * 4 + j
                    c.tr(ps[:, j * 128:(j + 1) * 128], xt[:, k * 128:(k + 1) * 128], ident[:])
                o = xo[n % 2]
                n += 1
                c.copy(o[:].rearrange("p a b -> p (a b)"), ps[:], eng=('dve' if n % 2 else 'act'))
                c.dma(xT[0][g4 * 512:(g4 + 1) * 512, t * 128:(t + 1) * 128].rearrange("(a p) t -> p a t", p=128), o[:], q='pool')

    c.barrier()
    cur = 0
    for l in range(nlayers):
        if 'ffn1' in passes:
            ffn_pass(c, pp, T, lng, lnb, consts, xT[cur], xT[1 - cur], Wb['ffn1_wg'][l], Wb['ffn1_wu'][l], Wb['ffn1_wd'][l], l, 0)
            c.barrier()
            cur = 1 - cur
        if 'inproj' in passes:
            inproj_pass(c, pp, T, l, W, Wb, S, xT[cur])
            c.barrier()
        if 'gmlp' in passes:
            gmlp_pass(c, pp, T, l, W, S, consts)
            c.barrier()
        if 'rwkv' in passes:
            rwkv_pass(c, pp, T, l, W, S, consts)
            c.barrier()
        if 'nsa' in passes:
            nsa_pass(c, pp, T, l, W, S, consts)
            c.barrier()
        if 'merge' in passes:
            merge_pass(c, pp, T, l, W, Wb, S, lng, lnb, consts, xT[cur], xT[1 - cur], branches)
            c.barrier()
            cur = 1 - cur
        if 'ffn2' in passes:
            ffn_pass(c, pp, T, lng, lnb, consts, xT[cur], xT[1 - cur], Wb['ffn2_wg'][l], Wb['ffn2_wu'][l], Wb['ffn2_wd'][l], l, 2)
            c.barrier()
            cur = 1 - cur

    c.barrier()
    with ExitStack() as es:
        xf = [c.sb('xf%d' % i, [128, KT, 128], F32, stack=es) for i in range(2)]
        yo = [c.sb('yo%d' % i, [128, D], F32, stack=es) for i in range(2)]
        for t in range(T // 128):
            a = xf[t % 2]
            c.dma(a[:], xT[cur][:, t * 128:(t + 1) * 128].rearrange("(k p) t -> p k t", p=128))
            y = yo[t % 2]
            for g4 in range(KT // 4):
                ps = pp.get()
                for j in range(4):
                    k = g4 * 4 + j
                    c.tr(ps[:, j * 128:(j + 1) * 128], a[:, k, :], ident[:])
                c.copy(y[:, g4 * 512:(g4 + 1) * 512], ps[:], eng=('dve' if g4 % 2 else 'act'))
            c.dma(out[t * 128:(t + 1) * 128, :], y[:], q='pool')
    c.wait_all('sp')
    return net


RW_SBT = 256
RW_NCH = RW_SBT // 64
RW_G = 4
USE_F32R = True
EXPM05 = 0.6065306597126334


def rwkv_pass(c, pp, T, l, W, S, consts, stop=99):
    nc = c.nc
    SBT, NCH, G = RW_SBT, RW_NCH, RW_G
    ident = consts['ident']
    bones = consts['bones']
    mqr = consts['maskqr']
    msl = consts['masksl']
    reset = consts['reset']
    pRW = S['pRW']
    with ExitStack() as es:
        def sb(name, shape, dt=F32):
            return c.sb('rw_' + name, shape, dt, stack=es)
        par = {}
        for nm in ['rw_w0', 'rw_a0', 'rw_k_k', 'rw_k_a', 'rw_gn_g', 'rw_gn_b']:
            par[nm] = sb(nm, [128, 16])
            c.dma(par[nm][:], W[nm][l:l + 1, :].rearrange("o (q p) -> p (o q)", p=128), q='sp', allow_slow_non_contiguous=True)
        par['rw_r_k'] = sb('rk', [128, 16])
        c.dma(par['rw_r_k'][:], W['rw_r_k'][l:l + 1].rearrange("o (q a) n -> (a n) (o q)", a=2), q='sp', allow_slow_non_contiguous=True)
        par['omka'] = sb('omka', [128, 16])
        c.ts(par['omka'][:], par['rw_k_a'][:], -1.0, 1.0, ALU.mult, ALU.add)
        if l > 0:
            par['rw_v0'] = sb('v0', [128, 16])
            c.dma(par['rw_v0'][:], W['rw_v0'][l - 1:l, :].rearrange("o (q p) -> p (o q)", p=128), q='sp', allow_slow_non_contiguous=True)
        w2b = sb('w2b', [96, D], BF16)
        a2b = sb('a2b', [96, D], BF16)
        g2b = sb('g2b', [128, 2, D], BF16)
        if l > 0:
            v1b = sb('v1b', [128, KT, 64], BF16)
            v2b = sb('v2b', [64, D], BF16)
            vvT = sb('vvT', [64, T], BF16)
        with ExitStack() as es2:
            stg = c.sb('rw_stg', [128, 2, D], stack=es2)
            c.dma(stg[0:96, 0, :], W['rw_w2'][l])
            c.copy(w2b[:], stg[0:96, 0, :], eng='act')
            c.dma(stg[0:96, 1, :], W['rw_a2'][l])
            c.copy(a2b[:], stg[0:96, 1, :], eng='act')
            c.dma(stg[:], W['rw_g2'][l].rearrange("(k p) d -> p k d", p=128))
            c.copy(g2b[:], stg[:], eng='act')
            if l > 0:
                c.dma(stg[:, 0, 0:KT * 64].rearrange("p (k e) -> p k e", e=64), W['rw_v1'][l - 1].rearrange("(k p) e -> p k e", p=128))
                c.copy(v1b[:].rearrange("p k e -> p (k e)"), stg[:, 0, 0:KT * 64], eng='act')
                c.dma(stg[0:64, 1, :], W['rw_v2'][l - 1])
                c.copy(v2b[:], stg[0:64, 1, :], eng='act')
                vld = c.sb('rw_vld', [128, KT, 512], stack=es2)
                vlb = c.sb('rw_vlb', [128, KT, 512], BF16, stack=es2)
                for tb in range(T // 512):
                    c.dma(vld[:], pRW[4096:6144, tb * 512:(tb + 1) * 512].rearrange("(k p) t -> p k t", p=128))
                    c.copy(vlb[:], vld[:], eng='act')
                    ps = pp.get()
                    for k in range(KT):
                        c.mm(ps[0:64, :], v1b[:, k, :], vlb[:, k, :], start=(k == 0), stop=(k == KT - 1))
                    c.copy(vvT[:, tb * 512:(tb + 1) * 512], ps[0:64, :], eng='dve')
            c.barrier()
        lw = sb('lw', [96, 2, SBT])
        lg = sb('lg', [128, 2, SBT])
        twl = sb('twl', [96, SBT], BF16)
        alb = sb('alb', [96, SBT], BF16)
        sgl = sb('sgl', [128, 2, SBT], BF16)
        names = ['r', 'k', 'v', 'a', 'ld', 'cs', 'kk', 'kp', 'bv', 't0', 't1', 't2', 'BtT', 'KtT', 'BcT', 'KcT', 'g', 'bonus', 'yT', 'vf', 'sq', 'vr']
        P = [{n: sb('%s%d' % (n, g), [128, SBT]) for n in names} for g in range(G)]
        identr = sb('identr', [128, 128])
        bonesr = sb('bonesr', [128, 128])
        for g in range(G):
            P[g]['AR'] = sb('AR%d' % g, [128, NCH, 128])
            P[g]['QRB'] = sb('QRB%d' % g, [128, NCH, 128])
            P[g]['AKRK'] = sb('AKRK%d' % g, [128, NCH, 128])
            P[g]['Nn'] = sb('Nn%d' % g, [128, NCH, 64])
            P[g]['PwQ'] = sb('PwQ%d' % g, [128, NCH, 128])
            P[g]['PwN'] = sb('PwN%d' % g, [128, NCH, 128])
            P[g]['Tt'] = sb('Tt%d' % g, [128, NCH, 128])
            for n_ in ('PwQ', 'PwN', 'Tt'):
                c.memset(P[g][n_][:], 0.0, eng='pool')
            P[g]['Vtok'] = sb('Vtok%d' % g, [128, NCH, 64])
            P[g]['Bctok'] = sb('Bctok%d' % g, [128, NCH, 64])
            P[g]['Kctok'] = sb('Kctok%d' % g, [128, NCH, 64])
            P[g]['PC'] = sb('PC%d' % g, [128, NCH])
            P[g]['yb'] = sb('yb%d' % g, [128, SBT], BF16)
        St = sb('St', [128, G, 64])
        X0 = sb('X0', [128, G, 64])
        Us = sb('Us', [128, G, 64])
        if USE_F32R:
            c.r32 = set()
            for g in range(G):
                for n in ['AR', 'BtT', 'KtT', 'BcT', 'KcT', 'QRB', 'AKRK', 'Nn', 'PwQ', 'PwN', 'Tt', 'Vtok', 'Bctok', 'Kctok', 'yT', 'sq', 'vr']:
                    c.r32.add(P[g][n].name)
            for t_ in (St, X0, Us, identr, bonesr):
                c.r32.add(t_.name)
        c.copy(identr[:], ident[:], eng='act')
        c.copy(bonesr[:], bones[:], eng='act')
        ident = identr
        bones = bonesr

        def hv(ap):
            return ap.rearrange("p (c t) -> p c t", t=64)

        for pg in range(16 // G):
            c.memset(St[:], 0.0)
            for sbi in range(T // SBT):
                tsl = slice(sbi * SBT, (sbi + 1) * SBT)
                c.dma(lw[:, 0, :], pRW[6144:6240, tsl])
                c.dma(lw[:, 1, :], pRW[6240:6336, tsl])
                c.dma(lg[:], pRW[6336:6592, tsl].rearrange("(k p) t -> p k t", p=128))
                c.act(twl[:], lw[:, 0, :], AF.Tanh)
                c.copy(alb[:], lw[:, 1, :], eng='dve')
                c.act(sgl[:], lg[:], AF.Sigmoid)
                def prep(g):
                    q = pg * G + g
                    t = P[g]
                    rows = slice(q * 128, (q + 1) * 128)
                    c.dma(t['r'][:], pRW[q * 128:(q + 1) * 128, tsl])
                    yield
                    c.dma(t['k'][:], pRW[2048 + q * 128:2048 + (q + 1) * 128, tsl])
                    yield
                    c.dma(t['v'][:], pRW[4096 + q * 128:4096 + (q + 1) * 128, tsl])
                    yield
                    ps = pp.get()
                    c.mm(ps[:, 0:SBT], w2b[:, rows], twl[:])
                    yield
                    c.act(t['ld'][:], ps[:, 0:SBT], AF.Sigmoid, bias=par['rw_w0'][:, q:q + 1])
                    yield
                    c.ts(t['ld'][:], t['ld'][:], -EXPM05)
                    yield
                    c.mm(ps[:, SBT:2 * SBT], a2b[:, rows], alb[:])
                    yield
                    c.act(t['a'][:], ps[:, SBT:2 * SBT], AF.Sigmoid, bias=par['rw_a0'][:, q:q + 1])
                    yield
                    ps = pp.get()
                    for kk_ in range(2):
                        c.mm(ps[:, 0:SBT], g2b[:, kk_, rows], sgl[:, kk_, :], start=(kk_ == 0), stop=(kk_ == 1))
                        yield
                    c.copy(t['g'][:], ps[:, 0:SBT], eng='act')
                    yield
                    if l > 0:
                        c.mm(ps[:, SBT:2 * SBT], v2b[:, rows], vvT[:, tsl])
                        yield
                        c.act(t['t0'][:], ps[:, SBT:2 * SBT], AF.Sigmoid, bias=par['rw_v0'][:, q:q + 1])
                        yield
                        c.dma(t['vf'][:], S['vfirstT'][rows, tsl])
                        yield
                        c.tt(t['t1'][:], t['vf'][:], t['v'][:], ALU.subtract, eng='pool')
                        yield
                        c.tt(t['t1'][:], t['t1'][:], t['t0'][:], ALU.mult, eng='pool')
                        yield
                        c.tt(t['v'][:], t['v'][:], t['t1'][:], ALU.add, eng='pool')
                        yield
                    else:
                        c.dma(S['vfirstT'][rows, tsl], t['v'][:], q='pool')
                        yield
                    if stop <= 1:
                        return
                    c.ts(t['kk'][:], t['k'][:], par['rw_k_k'][:, q:q + 1])
                    yield
                    c.tt(t['sq'][:], t['kk'][:], t['kk'][:], ALU.mult, eng='pool')
                    yield
                    ps = pp.get()
                    c.mm(ps[:, 0:SBT], bones[:], t['sq'][:])
                    yield
                    c.ts(t['t1'][:], ps[:, 0:SBT], 1e-24, None, ALU.max)
                    yield
                    c.act(t['t1'][:], t['t1'][:], AF.Sqrt)
                    yield
                    c.recip(t['t1'][:], t['t1'][:])
                    yield
                    c.tt(t['kk'][:], t['kk'][:], t['t1'][:], ALU.mult)
                    yield
                    c.ts(t['t2'][:], t['a'][:], par['rw_k_a'][:, q:q + 1], par['omka'][:, q:q + 1], ALU.mult, ALU.add)
                    yield
                    c.tt(t['kp'][:], t['k'][:], t['t2'][:], ALU.mult, eng='pool')
                    yield
                    c.tt(t['bv'][:], t['kk'][:], t['a'][:], ALU.mult, eng='pool')
                    yield
                    c.stt(t['sq'][:], t['r'][:], par['rw_r_k'][:, q:q + 1], t['kp'][:], ALU.mult, ALU.mult)
                    yield
                    c.mm(ps[:, SBT:2 * SBT], bones[:], t['sq'][:])
                    yield
                    c.copy(t['vr'][:], t['v'][:], eng='act')
                    yield
                    c.tt(t['bonus'][:], ps[:, SBT:2 * SBT], t['v'][:], ALU.mult)
                    yield
                    if stop <= 2:
                        return
                    c.op('dve', lambda t=t: nc.vector.tensor_tensor_scan(t['cs'][:], reset[:, 0:SBT], t['ld'][:], 0.0, ALU.mult, ALU.add),
                         [reset[:, 0:SBT], t['ld'][:]], [t['cs'][:]])
                    yield
                    c.tt(t['t0'][:], t['cs'][:], t['ld'][:], ALU.subtract, eng='pool')
                    yield
                    c.act(t['t1'][:], t['t0'][:], AF.Exp)
                    yield
                    c.stt(t['AR'][:, :, 0:64], hv(t['kk'][:]), -1.0, hv(t['t1'][:]), ALU.mult, ALU.mult)
                    yield
                    c.act(t['t1'][:], t['cs'][:], AF.Exp)
                    yield
                    c.tt(t['AR'][:, :, 64:128], hv(t['r'][:]), hv(t['t1'][:]), ALU.mult)
                    yield
                    c.act(t['t1'][:], t['cs'][:], AF.Exp, scale=-1.0)
                    yield
                    c.tt(t['BtT'][:], t['bv'][:], t['t1'][:], ALU.mult)
                    yield
                    c.tt(t['KtT'][:], t['kp'][:], t['t1'][:], ALU.mult, eng='pool')
                    yield
                    csv = hv(t['cs'][:])
                    c.tt(hv(t['t0'][:]), csv[:, :, 63:64].broadcast_to([128, NCH, 64]), csv, ALU.subtract)
                    yield
                    c.act(t['t1'][:], t['t0'][:], AF.Exp)
                    yield
                    c.tt(t['BcT'][:], t['bv'][:], t['t1'][:], ALU.mult)
                    yield
                    c.tt(t['KcT'][:], t['kp'][:], t['t1'][:], ALU.mult, eng='pool')
                    yield
                    c.act(t['PC'][:], csv[:, :, 63], AF.Exp)
                    yield
                    if stop <= 3:
                        return
                    ps1 = pp.get()
                    ps2 = pp.get()
                    ps3 = pp.get()
                    ps4 = pp.get()
                    for h in range(2):
                        hs = slice(h * 64, (h + 1) * 64)
                        for ch in range(NCH):
                            cs_ = slice(ch * 64, (ch + 1) * 64)
                            c.mm(ps1[hs, ch * 128:(ch + 1) * 128], t['BtT'][hs, cs_], t['AR'][hs, ch, :])
                            c.mm(ps2[hs, ch * 128:(ch + 1) * 128], t['KtT'][hs, cs_], t['AR'][hs, ch, :])
                            c.mm(ps3[hs, cs_], t['AR'][hs, ch, 0:64], t['BtT'][hs, cs_])
                            c.mm(ps4[hs, cs_], t['vr'][hs, cs_], ident[hs, hs])
                            c.mm(ps4[hs, SBT + ch * 64:SBT + (ch + 1) * 64], t['BcT'][hs, cs_], ident[hs, hs])
                    mq = mqr[:].unsqueeze(1).broadcast_to([128, NCH, 128])
                    c.tt(t['QRB'][:], ps1[:, 0:NCH * 128].rearrange("p (c t) -> p c t", t=128), mq, ALU.mult)
                    c.tt(t['AKRK'][:], ps2[:, 0:NCH * 128].rearrange("p (c t) -> p c t", t=128), mq, ALU.mult)
                    c.tt(t['Nn'][:], hv(ps3[:, 0:SBT]), msl[:].unsqueeze(1).broadcast_to([128, NCH, 64]), ALU.mult)
                    c.copy(t['Vtok'][:], hv(ps4[:, 0:SBT]), eng='act')
                    c.copy(t['Bctok'][:], hv(ps4[:, SBT:2 * SBT]), eng='act')
                    ps5 = pp.get()
                    for h in range(2):
                        hs = slice(h * 64, (h + 1) * 64)
                        for ch in range(NCH):
                            cs_ = slice(ch * 64, (ch + 1) * 64)
                            c.mm(ps5[hs, cs_], t['KcT'][hs, cs_], ident[hs, hs])
                    c.copy(t['Kctok'][:], hv(ps5[:, 0:SBT]), eng='act')
                    if stop <= 4:
                        return
                    yield
                gens = [prep(g) for g in range(G)]
                while gens:
                    nxt = []
                    for gen in gens:
                        try:
                            next(gen)
                            nxt.append(gen)
                        except StopIteration:
                            pass
                    gens = nxt
                if stop > 4:
                    for g in range(G):
                        t = P[g]
                        for h in range(2):
                            hs = slice(h * 64, (h + 1) * 64)
                            c.copy(t['PwQ'][hs, :, hs], t['QRB'][hs, :, 0:64], eng='pool')
                            c.copy(t['PwN'][hs, :, hs], t['Nn'][hs, :, :], eng='pool')
                            c.tt(t['Tt'][hs, :, hs], t['QRB'][hs, :, 0:64], consts['ident2'][hs, :].unsqueeze(1).broadcast_to([64, NCH, 64]), ALU.add)
                    for lev in range(5):
                        bn = {}
                        bq = {}
                        for g in range(G):
                            t = P[g]
                            bn[g] = pp.get()
                            for ch in range(NCH):
                                c.mm(bn[g][:, ch * 128:(ch + 1) * 128], t['PwQ'][:, ch, :], t['PwN'][:, ch, :])
                            if lev < 4:
                                bq[g] = pp.get()
                                for ch in range(NCH):
                                    c.mm(bq[g][:, ch * 128:(ch + 1) * 128], t['PwN'][:, ch, :], t['PwQ'][:, ch, :])
                        for g in range(G):
                            t = P[g]
                            c.copy(t['PwN'][:].rearrange("p c t -> p (c t)"), bn[g][:, :], eng='act')
                            if lev < 4:
                                c.copy(t['PwQ'][:].rearrange("p c t -> p (c t)"), bq[g][:, :], eng='dve')
                        bt = {}
                        for g in range(G):
                            t = P[g]
                            bt[g] = pp.get()
                            for ch in range(NCH):
                                c.mm(bt[g][:, ch * 128:(ch + 1) * 128], t['PwN'][:, ch, :], t['Tt'][:, ch, :])
                        for g in range(G):
                            t = P[g]
                            c.tt(t['Tt'][:].rearrange("p c t -> p (c t)"), t['Tt'][:].rearrange("p c t -> p (c t)"), bt[g][:, :], ALU.add)
                if stop <= 5:
                    continue
                for ch in range(NCH):
                    cs_ = slice(ch * 64, (ch + 1) * 64)
                    psx = pp.get()
                    for g in range(G):
                        t = P[g]
                        for h in range(2):
                            hs = slice(h * 64, (h + 1) * 64)
                            o = psx[hs, g * 64:(g + 1) * 64]
                            c.mm(o, t['AR'][hs, ch, 0:64], St[hs, g, :], start=True, stop=False)
                            c.mm(o, t['AKRK'][hs, ch, 0:64], t['Vtok'][hs, ch, :], start=False, stop=True)
                    c.copy(X0[:].rearrange("p g i -> p (g i)"), psx[:, 0:G * 64], eng='act')
                    psu = pp.get()
                    for g in range(G):
                        t = P[g]
                        c.mm(psu[:, g * 64:(g + 1) * 64], t['Tt'][:, ch, :], X0[:, g, :])
                    c.copy(Us[:].rearrange("p g i -> p (g i)"), psu[:, 0:G * 64], eng='dve')
                    psy = pp.get()
                    for g in range(G):
                        t = P[g]
                        for h in range(2):
                            hs = slice(h * 64, (h + 1) * 64)
                            o = psy[hs, g * 64:(g + 1) * 64]
                            c.mm(o, St[hs, g, :], t['AR'][hs, ch, 64:128], start=True, stop=False)
                            c.mm(o, Us[hs, g, :], t['QRB'][hs, ch, 64:128], start=False, stop=False)
                            c.mm(o, t['Vtok'][hs, ch, :], t['AKRK'][hs, ch, 64:128], start=False, stop=True)
                            o2 = psy[hs, 256 + g * 64:256 + (g + 1) * 64]
                            c.mm(o2, t['Bctok'][hs, ch, :], Us[hs, g, :], start=True, stop=False)
                            c.mm(o2, t['Kctok'][hs, ch, :], t['Vtok'][hs, ch, :], start=False, stop=True)
                    for g in range(G):
                        t = P[g]
                        c.copy(t['yT'][:, cs_], psy[:, g * 64:(g + 1) * 64], eng='act')
                        c.stt(St[:, g, :], St[:, g, :], t['PC'][:, ch:ch + 1], psy[:, 256 + g * 64:256 + (g + 1) * 64], ALU.mult, ALU.add)
                if stop <= 6:
                    continue
                for g in range(G):
                    q = pg * G + g
                    t = P[g]
                    ps = pp.get()
                    c.mm(ps[:, 0:SBT], bones[:], t['yT'][:])
                    c.tt(t['sq'][:], t['yT'][:], t['yT'][:], ALU.mult, eng='pool')
                    c.mm(ps[:, SBT:2 * SBT], bones[:], t['sq'][:])
                    c.act(t['t1'][:], ps[:, 0:SBT], AF.Copy, scale=1.0 / 64)
                    c.tt(t['t2'][:], t['t1'][:], t['t1'][:], ALU.mult)
                    c.stt(t['t2'][:], ps[:, SBT:2 * SBT], 1.0 / 64, t['t2'][:], ALU.mult, ALU.subtract)
                    c.ts(t['t2'][:], t['t2'][:], 64e-5, None, ALU.add)
                    c.act(t['t2'][:], t['t2'][:], AF.Sqrt)
                    c.recip(t['t2'][:], t['t2'][:])
                    c.tt(t['t0'][:], t['yT'][:], t['t1'][:], ALU.subtract)
                    c.tt(t['t0'][:], t['t0'][:], t['t2'][:], ALU.mult)
                    c.ts(t['t0'][:], t['t0'][:], par['rw_gn_g'][:, q:q + 1], par['rw_gn_b'][:, q:q + 1], ALU.mult, ALU.add)
                    c.tt(t['t0'][:], t['t0'][:], t['bonus'][:], ALU.add, eng='pool')
                    c.tt(t['yb'][:], t['t0'][:], t['g'][:], ALU.mult)
                    c.dma(S['yrwT'][q * 128:(q + 1) * 128, tsl], t['yb'][:], q='pool')


def host_consts():
    cc = {}
    cc['c_ident'] = np.eye(128, dtype=np.float32)
    cc['c_triu'] = np.triu(np.ones((128, 128), np.float32))
    bo = np.zeros((128, 128), np.float32)
    bo[:64, :64] = 1
    bo[64:, 64:] = 1
    cc['c_bones'] = bo
    s = np.arange(128)[:, None] % 64
    t = np.arange(64)[None, :]
    cc['c_maskqr'] = np.concatenate([(s < t), (s <= t)], axis=1).astype(np.float32)
    cc['c_masksl'] = (t < s).astype(np.float32)
    r = np.ones((128, 512), np.float32)
    r[:, ::64] = 0
    cc['c_reset'] = r
    cc['c_ident2'] = (s == t).astype(np.float32)
    qq = np.arange(128)[:, None]
    jj = np.arange(8)[None, :]
    cc['c_maskc'] = (qq >= 16 * jj + 15).astype(np.float32)
    cc['c_esel'] = (np.arange(4096)[None, :] // 64 == np.arange(64)[:, None]).astype(np.float32)
    kk = np.arange(128)[:, None]
    q4 = np.tile(np.arange(128), 4)[None, :]
    cc['c_caus4'] = np.where(kk <= q4, 0.0, -30000.0).astype(np.float32)
    cc['c_first4'] = np.where(kk > q4, 0.0, -30000.0).astype(np.float32)
    return cc


NSA_SCALE = 192 ** -0.5
NEGM = -30000.0


def nsa_pass(c, pp, T, l, W, S, consts):
    nc = c.nc
    n_c = T // 16 - 1
    n_s = T // 64
    NQ = T // 128
    ident = consts['ident']
    kvT = S['kvT']
    banks = pp.banks

    class Rot:
        def __init__(self, idx):
            self.idx = idx
            self.i = 0

        def get(self):
            b = banks[self.idx[self.i % len(self.idx)]]
            self.i += 1
            return b
    rs_ = Rot([0, 1, 2, 3])
    rm_ = Rot([6, 7])
    psO = banks[4]
    psL = banks[5]
    with ExitStack() as es:
        def sb(name, shape, dt=F32):
            return c.sb('ns_' + name, shape, dt, stack=es)
        ident_b = sb('identb', [128, 128], BF16)
        c.copy(ident_b[:], ident[:], eng='act')
        ones_b = consts['ones_bf']
        esel = sb('esel', [64, T], BF16)
        caus4 = sb('caus4', [128, 512], BF16)
        first4 = sb('first4', [128, 512], BF16)
        maskc = consts['maskc']
        kcmpT = sb('kcmpT', [96, 2, 4, 256], BF16)
        vcmp = sb('vcmp', [128, 2, 4, 128], BF16)
        c.memset(vcmp[:], 0.0)
        with ExitStack() as es2:
            def sb2(name, shape, dt=F32):
                return c.sb('ns2_' + name, shape, dt, stack=es2)
            stg = sb2('stg', [128, 4096])
            c.dma(stg[0:64, 0:T], consts['d_esel'][:, 0:T])
            c.copy(esel[:], stg[0:64, 0:T], eng='act')
            c.dma(stg[:, 0:512], consts['d_caus4'])
            c.copy(caus4[:], stg[:, 0:512], eng='act')
            c.dma(stg[:, 512:1024], consts['d_first4'])
            c.copy(first4[:], stg[:, 512:1024], eng='act')
            phk1 = sb2('phk1', [96, 64, 192], BF16)
            phv1 = sb2('phv1', [128, 32, 128], BF16)
            phk2 = sb2('phk2', [96, 2, 192], BF16)
            phv2 = sb2('phv2', [128, 128], BF16)
            stk = sb2('stk', [96, 16 * 192])
            for part in range(4):
                c.dma(stk[:].rearrange("p (a e) -> p a e", e=192),
                      W['nsa_phi_k1'][l][part * 1536:(part + 1) * 1536, :].rearrange("(a p) e -> p a e", p=96))
                c.copy(phk1[:, part * 16:(part + 1) * 16, :].rearrange("p a e -> p (a e)"), stk[:], eng=('act' if part % 2 else 'dve'))
            c.dma(stg[:, 0:4096].rearrange("p (a e) -> p a e", e=128), W['nsa_phi_v1'][l].rearrange("(a p) e -> p a e", p=128))
            c.copy(phv1[:].rearrange("p a e -> p (a e)"), stg[:, 0:4096], eng='dve')
            c.dma(stk[:, 0:384].rearrange("p (a e) -> p a e", e=192), W['nsa_phi_k2'][l].rearrange("(a p) e -> p a e", p=96))
            c.copy(phk2[:].rearrange("p a e -> p (a e)"), stk[:, 0:384], eng='act')
            c.dma(stg[:, 0:128], W['nsa_phi_v2'][l])
            c.copy(phv2[:], stg[:, 0:128], eng='act')
            posk = sb2('posk', [96, 2, 32], BF16)
            posv = sb2('posv', [128, 32], BF16)
            for dc in range(2):
                c.dma(stk[:, 400 + dc * 32:432 + dc * 32], W['nsa_pos_k'][l][:, dc * 96:(dc + 1) * 96].rearrange("b p -> p b"), allow_slow_non_contiguous=True)
            c.copy(posk[:].rearrange("p a b -> p (a b)"), stk[:, 400:464], eng='act')
            c.dma(stg[:, 200:232], W['nsa_pos_v'][l].rearrange("b p -> p b"), allow_slow_non_contiguous=True)
            c.copy(posv[:], stg[:, 200:232], eng='act')
            hpk = sb2('hpk', [96, 2])
            hpv = sb2('hpv', [128, 1])
            for et in range(2):
                ps = rm_.get()
                n = 0
                for lq in range(32):
                    for dc in range(2):
                        c.mm(ps[0:96, 0:1], phk1[:, lq * 2 + dc, et * 96:(et + 1) * 96], posk[:, dc, lq:lq + 1], start=(n == 0), stop=(n == 63))
                        n += 1
                c.copy(hpk[:, et:et + 1], ps[0:96, 0:1], eng='dve')
            ps = rm_.get()
            for lq in range(32):
                c.mm(ps[:, 0:1], phv1[:, lq, :], posv[:, lq:lq + 1], start=(lq == 0), stop=(lq == 31))
            c.copy(hpv[:], ps[:, 0:1], eng='dve')
            kc = sb2('kc', [96, 2, T], BF16)
            vc = sb2('vc', [128, T], BF16)
            ghk = sb2('ghk', [96, 2, 256], BF16)
            ghv = sb2('ghv', [128, 256], BF16)
            for g in range(4):
                c.dma(kc[:], kvT[g * 192:(g + 1) * 192, :].rearrange("(a p) t -> p a t", p=96))
                c.dma(vc[:], kvT[768 + g * 128:768 + (g + 1) * 128, :])
                for et in range(2):
                    ps = rm_.get()
                    n = 0
                    for lq in range(32):
                        for dc in range(2):
                            c.mm(ps[0:96, 0:n_c], phk1[:, lq * 2 + dc, et * 96:(et + 1) * 96],
                                 kc[:, dc, lq:lq + 16 * (n_c - 1) + 1:16], start=(n == 0), stop=(n == 63))
                            n += 1
                    c.act(ghk[:, et, 0:n_c], ps[0:96, 0:n_c], AF.Gelu, bias=hpk[:, et:et + 1])
                for e2 in range(2):
                    ps = rm_.get()
                    for ec in range(2):
                        c.mm(ps[0:96, 0:n_c], phk2[:, ec, e2 * 96:(e2 + 1) * 96], ghk[:, ec, 0:n_c], start=(ec == 0), stop=(ec == 1))
                    c.copy(kcmpT[:, e2, g, 0:n_c], ps[0:96, 0:n_c], eng='dve')
                ps = rm_.get()
                for lq in range(32):
                    c.mm(ps[:, 0:n_c], phv1[:, lq, :], vc[:, lq:lq + 16 * (n_c - 1) + 1:16], start=(lq == 0), stop=(lq == 31))
                c.act(ghv[:, 0:n_c], ps[:, 0:n_c], AF.Gelu, bias=hpv[:, 0:1])
                for nb in range((n_c + 127) // 128):
                    w = min(128, n_c - nb * 128)
                    ps = rm_.get()
                    c.mm(ps[0:w, 0:128], ghv[:, nb * 128:nb * 128 + w], phv2[:])
                    c.copy(vcmp[0:w, nb, g, :], ps[0:w, 0:128], eng='dve')
            c.barrier()
        ksT = sb('ksT', [96, 2, T], BF16)
        kwT = sb('kwT', [96, 2, T], BF16)
        vsk = sb('vsk', [128, NQ, 128], BF16)
        vwk = sb('vwk', [128, NQ, 128], BF16)
        vtmp = [sb('vtmp%d' % i, [128, 512], BF16) for i in range(2)]
        Qg = [sb('Qg%d' % i, [96, 4, 2, 128], BF16) for i in range(2)]
        Qd = [sb('Qd%d' % i, [96, 2, 512], BF16) for i in range(2)]
        gbc = [sb('gbc%d' % i, [128, 12, 128]) for i in range(2)]
        yn = sb('yn', [128, 4, 128])
        ynb = [sb('ynb%d' % i, [128, 4, 128], BF16) for i in range(2)]
        Pacc = sb('Pacc', [128, 264])
        ee = [sb('ee%d' % i, [128, 256]) for i in range(4)]
        pb = [sb('pb%d' % i, [128, 256], BF16) for i in range(4)]
        pT = [sb('pT%d' % i, [128, 2, 128], BF16) for i in range(4)]
        st = sb('st', [128, 16])
        imp = sb('imp', [128, 64])
        score = sb('score', [128, 64])
        sc2 = sb('sc2', [128, 64])
        m8 = sb('m8', [128, 16])
        sel = sb('sel', [128, 64])
        sel2 = sb('sel2', [128, 64])
        R = sb('R', [64, 512], BF16)
        R2 = sb('R2', [1, 512], BF16)
        R2w = sb('R2w', [1, 512], BF16)
        negm = sb('negm', [128, 8])
        mpart = sb('mpart', [128, 16])
        PTs = [sb('PTs%d' % i, [128, 512], BF16) for i in range(4)]
        rl = sb('rl', [128, 512])
        ot = sb('ot', [128, 512])
        rl2 = sb('rl2', [128, 512])
        ot2 = sb('ot2', [128, 512])

        def hq(ap):
            return ap.rearrange("p (h q) -> p h q", q=128)

        def dense_gen(i, g, kT, vk, kb0, Rrow, use_sel, gate_j, psO_, psL_, PT_, rl_, ot_):
            Q_ = Qd[i % 2]

            def scores(kb):
                ps = rs_.get()
                mms = [(kT[:, 0, kb * 128:(kb + 1) * 128], Q_[:, 0, :]), (kT[:, 1, kb * 128:(kb + 1) * 128], Q_[:, 1, :]),
                       (ones_b[0:1, :], Rrow[0:1, :])]
                if use_sel:
                    mms.append((esel[0:n_s, kb * 128:(kb + 1) * 128], R[0:n_s, :]))
                if kb == i:
                    mms.append((ident_b[:], caus4[:]))
                if (not use_sel) and i >= 4 and kb == i - 4:
                    mms.append((ident_b[:], first4[:]))
                for n, (a, b) in enumerate(mms):
                    c.mm(ps[:, :], a, b, start=(n == 0), stop=(n == len(mms) - 1))
                return ps
            nxt = scores(kb0)
            yield
            for kb in range(kb0, i + 1):
                ps = nxt
                if kb < i:
                    nxt = scores(kb + 1)
                    yield
                P_ = PT_[kb % 2]
                c.act(P_[:], ps[:, :], AF.Exp, scale=NSA_SCALE)
                yield
                c.mm(psO_[:, :], vk[:, kb, :], P_[:], start=(kb == kb0), stop=(kb == i))
                c.mm(psL_[:, :], ones_b[:], P_[:], start=(kb == kb0), stop=(kb == i))
                yield
            c.ts(rl_[:], psL_[:, :], 1e-30, None, ALU.max)
            yield
            c.recip(rl_[:], rl_[:])
            yield
            c.tt(ot_[:], psO_[:, :], rl_[:], ALU.mult)
            yield
            gv = gbc[i % 2][:].rearrange("p (h j) q -> p h j q", j=3)[:, :, gate_j, :]
            c.tt(hq(ot_[:]), hq(ot_[:]), gv, ALU.mult, eng='pool')
            yield
            c.tt(yn[:], yn[:], hq(ot_[:]), ALU.add, eng='pool')
            yield

        def drive(gens):
            while gens:
                nx = []
                for gen in gens:
                    try:
                        next(gen)
                        nx.append(gen)
                    except StopIteration:
                        pass
                gens = nx

        def rowmax(i, h, kT, k0, k1, col):
            nb = 0
            for s0 in range(k0, k1, 512):
                w = min(512, k1 - s0)
                ps = rs_.get()
                for dc in range(2):
                    c.mm(ps[:, 0:w], Qg[i % 2][:, h, dc, :], kT[:, dc, s0:s0 + w], start=(dc == 0), stop=(dc == 1))
                c.reduce(mpart[:, nb:nb + 1], ps[:, 0:w], ALU.max)
                nb += 1
            if nb > 1:
                c.reduce(negm[:, col:col + 1], mpart[:, 0:nb], ALU.max)
                c.ts(negm[:, col:col + 1], negm[:, col:col + 1], -1.0)
            else:
                c.ts(negm[:, col:col + 1], mpart[:, 0:1], -1.0)

        cnt = [0]
        rm_ = rs_
        for g in range(4):
            base = 1280
            c.dma(ksT[:], kvT[base + g * 192:base + (g + 1) * 192, :].rearrange("(a p) t -> p a t", p=96))
            c.dma(kwT[:], kvT[2560 + g * 192:2560 + (g + 1) * 192, :].rearrange("(a p) t -> p a t", p=96))
            for (src0, dst) in ((base + 768 + g * 128, vsk), (2560 + 768 + g * 128, vwk)):
                for t4 in range(T // 512):
                    vt_ = vtmp[t4 % 2]
                    c.dma(vt_[:], kvT[src0:src0 + 128, t4 * 512:(t4 + 1) * 512])
                    ps = rm_.get()
                    for j in range(4):
                        c.mm(ps[:, j * 128:(j + 1) * 128], vt_[:, j * 128:(j + 1) * 128], ident_b[:])
                    c.copy(dst[:, t4 * 4:(t4 + 1) * 4, :].rearrange("p a d -> p (a d)"), ps[:, :], eng=('act' if t4 % 2 else 'dve'))
            for i in range(NQ):
                qs = slice(i * 128, (i + 1) * 128)
                Q_ = Qg[i % 2]
                c.dma(Q_[:], S['qT'][g * 768:(g + 1) * 768, qs].rearrange("(h a p) t -> p h a t", a=2, p=96))
                for dc in range(2):
                    c.dma(Qd[i % 2][:, dc, :].rearrange("p (h q) -> p h q", q=128),
                          S['qT'][g * 768:(g + 1) * 768, qs].rearrange("(h a p) t -> p h a t", a=2, p=96)[:, :, dc, :])
                c.dma(gbc[i % 2][:], S['ngT'][g * 12:(g + 1) * 12, qs].partition_broadcast(128))
                nv = 8 * i + 7
                c.memset(Pacc[:], 0.0, eng='pool')
                def cmp_head(h):
                    ps = rs_.get()
                    for dc in range(2):
                        c.mm(ps[:, 0:nv], Q_[:, h, dc, :], kcmpT[:, dc, g, 0:nv], start=(dc == 0), stop=(dc == 1))
                        yield
                    c.reduce(st[:, 4 * h + 0:4 * h + 1], ps[:, 0:nv], ALU.max)
                    yield
                    c.ts(st[:, 4 * h + 1:4 * h + 2], st[:, 4 * h + 0:4 * h + 1], -NSA_SCALE)
                    yield
                    e_ = ee[h]
                    c.act(e_[:, 0:nv], ps[:, 0:nv], AF.Exp, bias=st[:, 4 * h + 1:4 * h + 2], scale=NSA_SCALE)
                    yield
                    lo = max(nv - 8, 0)
                    j0 = 8 - (nv - lo)
                    c.tt(e_[:, lo:nv], e_[:, lo:nv], maskc[:, j0:8], ALU.mult)
                    yield
                    c.reduce(st[:, 4 * h + 2:4 * h + 3], e_[:, 0:nv], ALU.add)
                    yield
                    c.ts(st[:, 4 * h + 2:4 * h + 3], st[:, 4 * h + 2:4 * h + 3], 1e-30, None, ALU.max)
                    yield
                    c.recip(st[:, 4 * h + 3:4 * h + 4], st[:, 4 * h + 2:4 * h + 3])
                    yield
                    c.stt(Pacc[:, 1:1 + nv], e_[:, 0:nv], st[:, 4 * h + 3:4 * h + 4], Pacc[:, 1:1 + nv], ALU.mult, ALU.add)
                    yield
                    p_ = pb[h]
                    c.act(p_[:, 0:nv], e_[:, 0:nv], AF.Copy, scale=st[:, 4 * h + 3:4 * h + 4])
                    yield
                    t_ = pT[h]
                    nblk = (nv + 127) // 128
                    for nb in range(nblk):
                        w = min(128, nv - nb * 128)
                        pst = rm_.get()
                        c.mm(pst[0:w, 0:128], p_[:, nb * 128:nb * 128 + w], ident_b[:])
                        yield
                        c.copy(t_[0:w, nb, :], pst[0:w, 0:128], eng='dve')
                        yield
                    pso = rm_.get()
                    for nb in range(nblk):
                        w = min(128, nv - nb * 128)
                        c.mm(pso[:, 0:128], vcmp[0:w, nb, g, :], t_[0:w, nb, :], start=(nb == 0), stop=(nb == nblk - 1))
                        yield
                    c.tt(yn[:, h, :], pso[:, 0:128], gbc[i % 2][:, h * 3 + 0, :], ALU.mult)
                    yield
                drive([cmp_head(h) for h in range(4)])
                c.tt(imp[:, 0:n_s], Pacc[:, 0:4 * n_s:4], Pacc[:, 1:1 + 4 * n_s:4], ALU.add)
                for j in (2, 3, 4):
                    c.tt(imp[:, 0:n_s], imp[:, 0:n_s], Pacc[:, j:j + 4 * n_s:4], ALU.add)
                c.memset(score[:, 0:n_s], -1e30)
                if i > 0:
                    c.copy(score[:, 0:2 * i], imp[:, 0:2 * i])
                c.memset(score[:, 0:1], 1e6)
                c.memset(score[:, 2 * i:2 * i + 1], 1e6)
                c.memset(score[64:128, 2 * i + 1:2 * i + 2], 1e6)
                if i >= 1:
                    c.memset(score[0:64, 2 * i - 1:2 * i], 1e6)
                c.op('dve', lambda: nc.vector.max(m8[:, 0:8], score[:, 0:n_s]), [score[:, 0:n_s]], [m8[:, 0:8]])
                c.op('dve', lambda: nc.vector.match_replace(sc2[:, 0:n_s], m8[:, 0:8], score[:, 0:n_s], -3e38),
                     [m8[:, 0:8], score[:, 0:n_s]], [sc2[:, 0:n_s]])
                c.op('dve', lambda: nc.vector.max(m8[:, 8:16], sc2[:, 0:n_s]), [sc2[:, 0:n_s]], [m8[:, 8:16]])
                c.ts(sel[:, 0:n_s], score[:, 0:n_s], m8[:, 15:16], None, ALU.is_ge)
                c.ts(sel2[:, 0:n_s], score[:, 0:n_s], -5e29, None, ALU.is_gt)
                c.tt(sel[:, 0:n_s], sel[:, 0:n_s], sel2[:, 0:n_s], ALU.mult)
                c.ts(sel[:, 0:n_s], sel[:, 0:n_s], -1.0, -NEGM, ALU.add, ALU.mult)
                pst = rm_.get()
                c.tr(pst[0:n_s, 0:128], sel[:, 0:n_s], ident[:])
                c.copy(R[0:n_s, :].rearrange("p (h q) -> p h q", q=128), pst[0:n_s, 0:128].unsqueeze(1).broadcast_to([n_s, 4, 128]), eng='act')
                for h in range(4):
                    rowmax(i, h, ksT, 0, 128 * (i + 1), h)
                    rowmax(i, h, kwT, 128 * max(0, i - 4), 128 * (i + 1), 4 + h)
                pst = rm_.get()
                for h in range(4):
                    c.mm(pst[0:1, h * 128:(h + 1) * 128], negm[:, h:h + 1], ident[:])
                c.copy(R2[:], pst[0:1, :], eng='act')
                pst = rm_.get()
                for h in range(4):
                    c.mm(pst[0:1, h * 128:(h + 1) * 128], negm[:, 4 + h:5 + h], ident[:])
                c.copy(R2w[:], pst[0:1, :], eng='act')
                drive([dense_gen(i, g, ksT, vsk, 0, R2, True, 1, banks[4], banks[5], PTs[0:2], rl, ot),
                       dense_gen(i, g, kwT, vwk, max(0, i - 4), R2w, False, 2, banks[6], banks[7], PTs[2:4], rl2, ot2)])
                yb_ = ynb[i % 2]
                c.copy(yb_[:], yn[:], eng='act')
                c.dma(S['ynsT'][g * 512:(g + 1) * 512, qs].rearrange("(h p) q -> p h q", p=128), yb_[:], q='pool')


_NET = None


def kernel(**inputs):
    global _NET
    T = 4096
    if _NET is None:
        _NET = build(T=T, nlayers=DEPTH)
    net = _NET
    cc = host_consts()
    base = {}
    for k in net.inp:
        if k == 'x':
            continue
        if k in cc:
            base[k] = cc[k]
        else:
            base[k] = np.ascontiguousarray(np.asarray(inputs[k], dtype=np.float32))
    x = np.asarray(inputs['x'], dtype=np.float32)
    in_maps = []
    for core in range(8):
        m = dict(base)
        m['x'] = np.ascontiguousarray(x[core // 2])
        in_maps.append(m)
    res = run_bass_kernel_spmd(net.nc, in_maps, core_ids=list(range(8)))
    out = np.empty((4, T, D), np.float32)
    for b in range(4):
        out[b, :T // 2] = res.results[2 * b]['out'][:T // 2]
        out[b, T // 2:] = res.results[2 * b + 1]['out'][T // 2:]
    return out
```

```python
import numpy as np
from contextlib import ExitStack
import concourse.bass as bass
import concourse.mybir as mybir
from concourse.bass_utils import run_bass_kernel_spmd

F32 = mybir.dt.float32
BF16 = mybir.dt.bfloat16
I32 = mybir.dt.int32
U8 = mybir.dt.uint8
AF = mybir.ActivationFunctionType
ALU = mybir.AluOpType
AX = mybir.AxisListType

_DS = {F32: 4, BF16: 2, I32: 4, U8: 1, mybir.dt.uint32: 4, mybir.dt.float32r: 4,
       mybir.dt.uint16: 2, mybir.dt.int16: 2}


def _foot(ap):
    name = ap.tensor.name
    es = _DS.get(ap.dtype, 4)
    dims = list(ap.ap)
    if type(ap.tensor).__name__ == 'DRamTensorHandle':
        lo = ap.offset
        hi = lo + sum((c - 1) * abs(s) for s, c in dims) + 1
        return (name, 0, 1, lo * es, hi * es)
    is_psum = 'PSum' in type(ap.tensor).__name__ or 'Psum' in type(ap.tensor).__name__
    pstep, pcnt = dims[0]
    if pstep == 0:
        pstep = None
    free = dims[1:]
    ext = sum((c - 1) * abs(s) for s, c in free) + 1
    if pstep:
        f0 = ap.offset % pstep
        p0 = ap.offset // pstep
    else:
        tsh = ap.tensor.shape
        ps_ = 1
        for d in tsh[1:]:
            ps_ *= d
        tes = _DS.get(ap.tensor.dtype, 4)
        ps_ = ps_ * tes // es
        f0 = ap.offset % ps_
        p0 = ap.offset // ps_
        pcnt = 1
    if is_psum:
        return (name, (p0 // 32) * 32, ((p0 + pcnt + 31) // 32) * 32, 0, 1 << 40)
    return (name, p0, p0 + pcnt, f0 * es, (f0 + ext) * es)


class Ctx:
    NDMA = 24

    def __init__(self, nc):
        self.nc = nc
        self.es = ExitStack()
        self.eng = {'pe': nc.tensor, 'act': nc.scalar, 'dve': nc.vector, 'pool': nc.gpsimd, 'sp': nc.sync}
        self.sem = {}
        self.cnt = {}
        for e in ['pe', 'act', 'dve', 'pool']:
            self.sem[e] = self.es.enter_context(nc.semaphore('s_' + e))
            self.cnt[e] = 0
        for i in range(self.NDMA):
            k = 'd%d' % i
            self.sem[k] = self.es.enter_context(nc.semaphore('s_' + k))
            self.cnt[k] = 0
        self.dma_rr = 0
        self.known = {e: {} for e in ['pe', 'act', 'dve', 'pool', 'sp']}
        self.rec = {}
        self.ninst = 0

    def sb(self, name, shape, dt=F32, stack=None):
        self.uid = getattr(self, 'uid', 0) + 1
        return (stack or self.es).enter_context(self.nc.sbuf_tensor('%s_%d' % (name, self.uid), list(shape), dt))

    def ps(self, name, shape, dt=F32, stack=None):
        self.uid = getattr(self, 'uid', 0) + 1
        return (stack or self.es).enter_context(self.nc.psum_tensor('%s_%d' % (name, self.uid), list(shape), dt))

    def barrier(self):
        deps = {k: c for k, c in self.cnt.items() if c > 0}
        for e in ['pe', 'act', 'dve', 'pool', 'sp']:
            self._waits(e, dict(deps))

    def _deps(self, reads, writes, me):
        deps = {}
        for ap in reads:
            f = _foot(ap)
            for r in self.rec.get(f[0], ()):
                if r[6] and r[0] < f[2] and f[1] < r[1] and r[2] < f[4] and f[3] < r[3]:
                    if r[5] > deps.get(r[4], 0):
                        deps[r[4]] = r[5]
        for ap in writes:
            f = _foot(ap)
            for r in self.rec.get(f[0], ()):
                if r[0] < f[2] and f[1] < r[1] and r[2] < f[4] and f[3] < r[3]:
                    if r[4] == me and not r[6]:
                        continue
                    if r[5] > deps.get(r[4], 0):
                        deps[r[4]] = r[5]
        if me == 'pe':
            deps.pop('pe', None)
        return deps

    def _record(self, reads, writes, me, cnt):
        for ap in writes:
            f = _foot(ap)
            lst = self.rec.setdefault(f[0], [])
            lst[:] = [r for r in lst if not (f[1] <= r[0] and r[1] <= f[2] and f[3] <= r[2] and r[3] <= f[4])]
            lst.append([f[1], f[2], f[3], f[4], me, cnt, True])
        for ap in reads:
            f = _foot(ap)
            lst = self.rec.setdefault(f[0], [])
            lst[:] = [r for r in lst if not (r[4] == me and not r[6] and f[1] <= r[0] and r[1] <= f[2]
                                             and f[3] <= r[2] and r[3] <= f[4])]
            lst.append([f[1], f[2], f[3], f[4], me, cnt, False])

    def _waits(self, issuer, deps):
        e = self.eng[issuer]
        kn = self.known[issuer]
        for k, c in deps.items():
            if kn.get(k, 0) < c:
                e.wait_ge(self.sem[k], c)
                kn[k] = c

    def op(self, engname, fn, reads, writes):
        xr = [a for a in reads if 'PSum' in type(a.tensor).__name__]
        if xr:
            writes = list(writes) + xr
            for a in xr:
                self._ps_guard_read(a)
        deps = self._deps(reads, writes, engname)
        self._waits(engname, deps)
        inst = fn()
        self.cnt[engname] += 1
        inst.then_inc(self.sem[engname], 1)
        self._record(reads, writes, engname, self.cnt[engname])
        self.ninst += 1
        return inst

    def dma(self, out, in_, q='sp', **kw):
        k = 'd%d' % self.dma_rr
        self.dma_rr = (self.dma_rr + 1) % self.NDMA
        deps = self._deps([in_], [out], k)
        if self.cnt[k] > 0:
            deps[k] = max(deps.get(k, 0), self.cnt[k])
        self._waits(q, deps)
        inst = self.eng[q].dma_start(out=out, in_=in_, **kw)
        self.cnt[k] += 16
        inst.then_inc(self.sem[k], 16)
        self._record([in_], [out], k, self.cnt[k])
        self.ninst += 1
        return inst

    def wait_all(self, issuer='sp'):
        deps = {k: c for k, c in self.cnt.items() if c > 0}
        self._waits(issuer, deps)

    def _r(self, out):
        r32 = self.__dict__.get('r32', ())
        if r32 and out.dtype == F32 and out.tensor.name in r32:
            return out.bitcast(mybir.dt.float32r)
        return out

    def _ps_cols(self, ap):
        dims = list(ap.ap)
        es = _DS.get(ap.dtype, 4)
        pstep, pcnt = dims[0]
        ext = sum((c - 1) * abs(st) for st, c in dims[1:]) + 1
        f0 = ap.offset % pstep if pstep else 0
        p0 = ap.offset // pstep if pstep else 0
        return ap.tensor.name, p0, p0 + pcnt, f0 * es, (f0 + ext) * es

    def _ps_guard_write(self, out, start):
        name, p0, p1, b0, b1 = self._ps_cols(out)
        lst = self.__dict__.setdefault('pw', {}).setdefault(name, [])
        if start:
            for r in lst:
                assert not (r[0] < p1 and p0 < r[1] and r[2] < b1 and b0 < r[3]), \
                    'PSUM overwrite of unread matmul result in %s %s' % (name, (p0, p1, b0, b1, r))
            lst.append([p0, p1, b0, b1])

    def _ps_guard_read(self, ap):
        name, p0, p1, b0, b1 = self._ps_cols(ap)
        lst = self.__dict__.setdefault('pw', {}).get(name)
        if lst:
            lst[:] = [r for r in lst if not (r[0] < p1 and p0 < r[1] and r[2] < b1 and b0 < r[3])]

    def mm(self, out, lhsT, rhs, start=True, stop=True):
        self._ps_guard_write(out, start)
        r32 = self.__dict__.get('r32', ())
        if (r32 and lhsT.dtype == F32 and rhs.dtype == F32 and lhsT.tensor.name in r32 and rhs.tensor.name in r32
                and _foot(out)[1] == 0):
            lhsT = lhsT.bitcast(mybir.dt.float32r)
            rhs = rhs.bitcast(mybir.dt.float32r)
        return self.op('pe', lambda: self.nc.tensor.matmul(out, lhsT, rhs, start=start, stop=stop), [lhsT, rhs], [out])

    def tr(self, out, in_, ident):
        self._ps_guard_write(out, True)
        return self.op('pe', lambda: self.nc.tensor.transpose(out, in_, ident), [in_, ident], [out])

    def act(self, out, in_, func, bias=None, scale=None, accum_out=None, eng='act'):
        out = self._r(out)
        kw = {}
        rd = [in_]
        if bias is not None:
            kw['bias'] = bias
            if not isinstance(bias, (int, float)):
                rd.append(bias)
        if scale is not None:
            kw['scale'] = scale
            if not isinstance(scale, (int, float)):
                rd.append(scale)
        wr = [out]
        if accum_out is not None:
            kw['accum_out'] = accum_out
            wr.append(accum_out)
        return self.op('act', lambda: self.nc.scalar.activation(out, in_, func, **kw), rd, wr)

    def tt(self, out, in0, in1, op, eng='dve'):
        out = self._r(out)
        return self.op(eng, lambda: self.eng[eng].tensor_tensor(out, in0, in1, op), [in0, in1], [out])

    def ts(self, out, in0, s1, s2=None, op0=ALU.mult, op1=None, eng='dve', accum_out=None):
        out = self._r(out)
        rd = [in0]
        if not isinstance(s1, (int, float)):
            rd.append(s1)
        if s2 is not None and not isinstance(s2, (int, float)):
            rd.append(s2)
        kw = {}
        wr = [out]
        if accum_out is not None:
            kw['accum_out'] = accum_out
            wr.append(accum_out)
        if op1 is None:
            return self.op(eng, lambda: self.eng[eng].tensor_scalar(out, in0, s1, None, op0, **kw), rd, wr)
        return self.op(eng, lambda: self.eng[eng].tensor_scalar(out, in0, s1, s2, op0, op1, **kw), rd, wr)

    def stt(self, out, in0, scalar, in1, op0, op1, accum_out=None):
        out = self._r(out)
        rd = [in0, in1]
        if not isinstance(scalar, (int, float)):
            rd.append(scalar)
        wr = [out]
        kw = {}
        if accum_out is not None:
            kw['accum_out'] = accum_out
            wr.append(accum_out)
        return self.op('dve', lambda: self.nc.vector.scalar_tensor_tensor(out, in0, scalar, in1, op0, op1, **kw), rd, wr)

    def copy(self, out, in_, eng='dve'):
        out = self._r(out)
        if eng == 'act':
            return self.op('act', lambda: self.nc.scalar.copy(out, in_), [in_], [out])
        return self.op(eng, lambda: self.eng[eng].tensor_copy(out, in_), [in_], [out])

    def memset(self, ap, v, eng='dve'):
        return self.op(eng, lambda: self.eng[eng].memset(ap, v), [], [ap])

    def reduce(self, out, in_, op=ALU.add, axis=AX.X, eng='dve'):
        return self.op(eng, lambda: self.eng[eng].tensor_reduce(out, in_, axis, op), [in_], [out])

    def recip(self, out, in_):
        return self.op('dve', lambda: self.nc.vector.reciprocal(out, in_), [in_], [out])


D = 2048
DFF = 5632
KT = D // 128
FT = DFF // 128
DEPTH = 2
ALPHA = (2 * DEPTH) ** 0.25
LN_EPS = 1e-5
RW_COLS = 6592
OFF_GM = 6592
OFF_NSA = 10688
OFF_GATE = 17648
C_IN = 23792
TT = 512


class Net:
    def __init__(self, T):
        self.T = T
        nc = bass.Bass("TRN2", target_bir_lowering=False)
        self.nc = nc
        self.c = Ctx(nc)
        self.inp = {}
        self.psn = 0

    def din(self, name, shape, dt=F32):
        t = self.nc.dram_tensor(name, list(shape), dt, kind="ExternalInput").ap()
        self.inp[name] = t
        return t

    def dout(self, name, shape, dt=F32):
        return self.nc.dram_tensor(name, list(shape), dt, kind="ExternalOutput").ap()

    def dscr(self, name, shape, dt=F32):
        return self.nc.dram_tensor(name, list(shape), dt, kind="Internal").ap()


def conv_weight(c, cva, cvb, src2d, dst2d, rows, cols, k):
    CH = 4096
    s = src2d.rearrange("(p r) c -> p (r c)", p=128)
    d = dst2d.rearrange("(p r) c -> p (r c)", p=128)
    n = (rows // 128) * cols
    i = 0
    while i < n:
        m = min(CH, n - i)
        a = cva[k % len(cva)]
        b = cvb[k % len(cvb)]
        c.dma(a[:, 0:m], s[:, i:i + m], q='sp')
        e = ['act', 'dve'][k % 2]
        c.copy(b[:, 0:m], a[:, 0:m], eng=e)
        c.dma(d[:, i:i + m], b[:, 0:m], q='pool')
        i += m
        k += 1
    return k


class PsumPool:
    def __init__(self, c, n=8):
        self.banks = [c.ps("psb%d" % i, [128, 512]) for i in range(n)]
        self.i = 0

    def get(self):
        b = self.banks[self.i % len(self.banks)]
        self.i += 1
        return b


def layer_norm_fm(c, pp, st, z, g, b, ntok, consts):
    zb = st['zb']
    zq = st['zq']
    ones = consts['ones_bf']
    c.copy(zb[:, :, 0:ntok], z[:, :, 0:ntok], eng='act')
    c.act(zq[:, :, 0:ntok], z[:, :, 0:ntok], AF.Square)
    ps1 = pp.get()
    ps2 = pp.get()
    for k in range(KT):
        c.mm(ps1[:, 0:ntok], ones[:, :], zb[:, k, 0:ntok], start=(k == 0), stop=(k == KT - 1))
    for k in range(KT):
        c.mm(ps2[:, 0:ntok], ones[:, :], zq[:, k, 0:ntok], start=(k == 0), stop=(k == KT - 1))
    mean = st['mean']
    rstd = st['rstd']
    tmp = st['tmp512']
    c.act(mean[:, 0:ntok], ps1[:, 0:ntok], AF.Copy, scale=1.0 / D)
    c.tt(tmp[:, 0:ntok], mean[:, 0:ntok], mean[:, 0:ntok], ALU.mult)
    c.stt(rstd[:, 0:ntok], ps2[:, 0:ntok], 1.0 / D, tmp[:, 0:ntok], ALU.mult, ALU.subtract)
    c.ts(rstd[:, 0:ntok], rstd[:, 0:ntok], LN_EPS, None, ALU.add)
    c.act(rstd[:, 0:ntok], rstd[:, 0:ntok], AF.Sqrt)
    c.recip(rstd[:, 0:ntok], rstd[:, 0:ntok])
    for k in range(KT):
        e = 'dve' if k % 2 == 0 else 'pool'
        c.tt(z[:, k, 0:ntok], z[:, k, 0:ntok], mean[:, 0:ntok], ALU.subtract, eng=e)
        c.tt(z[:, k, 0:ntok], z[:, k, 0:ntok], rstd[:, 0:ntok], ALU.mult, eng=e)
        c.ts(z[:, k, 0:ntok], z[:, k, 0:ntok], g[:, k:k + 1], b[:, k:k + 1], ALU.mult, ALU.add, eng='dve')
    c.copy(zb[:, :, 0:ntok], z[:, :, 0:ntok], eng='act')


def ffn_tile(c, pp, st, xs, xb, wg, wu, wd, ntok):
    hT = st['hT']
    FW = 256
    for f0 in range(0, DFF, FW):
        i = (f0 // FW) % 2
        wgs = st['wgs'][i]
        wus = st['wus'][i]
        c.dma(wgs[:], wg[:, f0:f0 + FW].rearrange("(kt p) m -> p kt m", p=128), q='sp')
        c.dma(wus[:], wu[:, f0:f0 + FW].rearrange("(kt p) m -> p kt m", p=128), q='sp')
        for j in range(FW // 128):
            f = f0 // 128 + j
            psg = pp.get()
            psu = pp.get()
            for k in range(KT):
                c.mm(psg[:, 0:ntok], wgs[:, k, j * 128:(j + 1) * 128], xb[:, k, 0:ntok], start=(k == 0), stop=(k == KT - 1))
            for k in range(KT):
                c.mm(psu[:, 0:ntok], wus[:, k, j * 128:(j + 1) * 128], xb[:, k, 0:ntok], start=(k == 0), stop=(k == KT - 1))
            sg = st['sg'][f % 2]
            c.act(sg[:, 0:ntok], psg[:, 0:ntok], AF.Silu)
            c.tt(hT[:, f, 0:ntok], sg[:, 0:ntok], psu[:, 0:ntok], ALU.mult)
    c.ts(xs[:, :, 0:ntok], xs[:, :, 0:ntok], ALPHA, eng='pool')
    for d in range(KT):
        wds = st['wds'][d % 2]
        c.dma(wds[:], wd[:, d * 128:(d + 1) * 128].rearrange("(ft p) m -> p ft m", p=128), q='sp')
        ps = pp.get()
        for f in range(FT):
            c.mm(ps[:, 0:ntok], wds[:, f, :], hT[:, f, 0:ntok], start=(f == 0), stop=(f == FT - 1))
        c.stt(xs[:, d, 0:ntok], ps[:, 0:ntok], 0.5, xs[:, d, 0:ntok], ALU.mult, ALU.add)


def run_slabs(slabs, bufs, load_fn, compute_fn):
    if not slabs:
        return
    load_fn(slabs[0], bufs[0])
    for n, s in enumerate(slabs):
        if n + 1 < len(slabs):
            load_fn(slabs[n + 1], bufs[(n + 1) % len(bufs)])
        compute_fn(s, bufs[n % len(bufs)])


def make_slabs(tiles, maxw=512):
    slabs = []
    cur = None
    for (c0, ms, meta) in tiles:
        if cur is not None and cur[0] + cur[1] == c0 and cur[1] + ms <= maxw:
            cur[1] += ms
            cur[2].append((c0, ms, meta))
        else:
            cur = [c0, ms, [(c0, ms, meta)]]
            slabs.append(cur)
    return slabs


def fm_proj(c, pp, wsl, Wb2d, tiles, xb, ntok, evac, nk=KT, pre=None):
    slabs = make_slabs(tiles)

    def load(s, buf):
        c.dma(buf[:, 0:nk, 0:s[1]], Wb2d[:, s[0]:s[0] + s[1]].rearrange("(kt p) m -> p kt m", p=128), q='sp')

    def comp(s, buf):
        for (c0, ms, meta) in s[2]:
            o = c0 - s[0]
            if pre is not None:
                pre(c0, ms, meta)
            ps = pp.get()
            for k in range(nk):
                c.mm(ps[0:ms, 0:ntok], buf[:, k, o:o + ms], xb[:, k, 0:ntok], start=(k == 0), stop=(k == nk - 1))
            evac(ps, c0, ms, meta)
    run_slabs(slabs, wsl, load, comp)


def ffn_pass(c, pp, T, lng, lnb, consts, xT_src, xT_dst, wg, wu, wd, l, lni):
    with ExitStack() as es:
        st = {}
        st['zb'] = c.sb('zb', [128, KT, TT], BF16, stack=es)
        st['zq'] = c.sb('zq', [128, KT, TT], BF16, stack=es)
        st['mean'] = c.sb('mean', [128, TT], stack=es)
        st['rstd'] = c.sb('rstd', [128, TT], stack=es)
        st['tmp512'] = c.sb('tmp512', [128, TT], stack=es)
        st['xs'] = c.sb('xs', [128, KT, TT], stack=es)
        st['hT'] = c.sb('hT', [128, FT, TT], BF16, stack=es)
        st['wgs'] = [c.sb('wgs%d' % i, [128, KT, 256], BF16, stack=es) for i in range(2)]
        st['wus'] = [c.sb('wus%d' % i, [128, KT, 256], BF16, stack=es) for i in range(2)]
        st['wds'] = [c.sb('wds%d' % i, [128, FT, 128], BF16, stack=es) for i in range(2)]
        st['sg'] = [c.sb('sg%d' % i, [128, TT], stack=es) for i in range(2)]
        for tt in range(T // TT):
            xs = st['xs']
            c.dma(xs[:], xT_src[:, tt * TT:(tt + 1) * TT].rearrange("(k p) t -> p k t", p=128))
            c.copy(st['zb'][:], xs[:], eng='act')
            ffn_tile(c, pp, st, xs, st['zb'], wg, wu, wd, TT)
            layer_norm_fm(c, pp, st, xs, lng[:, l * 3 + lni, :], lnb[:, l * 3 + lni, :], TT, consts)
            c.dma(xT_dst[:, tt * TT:(tt + 1) * TT].rearrange("(k p) t -> p k t", p=128), xs[:], q='pool')


def inproj_pass(c, pp, T, l, W, Wb, S, xT_src):
    win = Wb['w_in'][l]
    tiles = []
    for i in range(48):
        tiles.append((i * 128, 128, ('rw', i)))
    tiles.append((6144, 96, ('rw', 48)))
    tiles.append((6240, 96, ('rw', 49)))
    tiles.append((6336, 128, ('rw', 50)))
    tiles.append((6464, 128, ('rw', 51)))
    for i in range(16):
        tiles.append((OFF_GM + i * 128, 128, ('gmu', i)))
    for i in range(24):
        tiles.append((OFF_NSA + i * 128, 128, ('q', i)))
    for i in range(30):
        tiles.append((OFF_NSA + 3072 + i * 128, 128, ('kv', i)))
    tiles.append((OFF_NSA + 6912, 48, ('ng', 0)))
    for i in range(48):
        tiles.append((OFF_GATE + i * 128, 128, ('gate', i)))
    with ExitStack() as es:
        xs = c.sb('ip_xs', [128, KT, TT], stack=es)
        xb = c.sb('ip_xb', [128, KT, TT], BF16, stack=es)
        wsl = [c.sb('ip_w%d' % i, [128, KT, 512], BF16, stack=es) for i in range(2)]
        wtk = [c.sb('ip_wt%d' % i, [128, KT, 512], BF16, stack=es) for i in range(2)]
        mu = c.sb('ip_mu', [128, 52], stack=es)
        carry = c.sb('ip_carry', [128, 52], stack=es)
        pbuf = [c.sb('ip_pb%d' % i, [128, TT + 1], stack=es) for i in range(2)]
        dbuf = [c.sb('ip_db%d' % i, [128, TT], stack=es) for i in range(2)]
        obuf = [c.sb('ip_ob%d' % i, [128, TT], stack=es) for i in range(3)]
        obb = [c.sb('ip_obb%d' % i, [128, TT], BF16, stack=es) for i in range(3)]
        vbuf = [c.sb('ip_vb%d' % i, [128, 512], stack=es) for i in range(2)]
        c.memset(carry[:], 0.0)
        c.memset(mu[:], 0.0)
        for (c0, ms, meta) in tiles:
            if meta[0] == 'rw':
                c.dma(mu[0:ms, meta[1]:meta[1] + 1], W['rw_mu'][l:l + 1, c0:c0 + ms].rearrange("o m -> m o"), q='sp', allow_slow_non_contiguous=True)
        cnt = [0]
        for tt in range(T // TT):
            ts_ = slice(tt * TT, (tt + 1) * TT)
            c.dma(xs[:], xT_src[:, ts_].rearrange("(k p) t -> p k t", p=128))
            c.copy(xb[:], xs[:], eng='act')

            def evac(ps, c0, ms, meta):
                kind, idx = meta
                n = cnt[0]
                cnt[0] += 1
                if kind == 'rw':
                    pb = pbuf[n % 2]
                    db = dbuf[n % 2]
                    c.copy(pb[0:ms, 0:1], carry[0:ms, idx:idx + 1], eng='dve')
                    c.copy(pb[0:ms, 1:TT + 1], ps[0:ms, 0:TT], eng='act')
                    c.copy(carry[0:ms, idx:idx + 1], pb[0:ms, TT:TT + 1], eng='dve')
                    c.tt(db[0:ms, :], pb[0:ms, 0:TT], pb[0:ms, 1:TT + 1], ALU.subtract)
                    c.stt(db[0:ms, :], db[0:ms, :], mu[0:ms, idx:idx + 1], pb[0:ms, 1:TT + 1], ALU.mult, ALU.add)
                    c.dma(S['pRW'][c0:c0 + ms, ts_], db[0:ms, :], q='pool')
                elif kind == 'gmu':
                    ob = obuf[n % 3]
                    c.act(ob[0:ms, :], ps[0:ms, 0:TT], AF.Gelu)
                    c.dma(S['uT'][idx * 128:idx * 128 + ms, ts_], ob[0:ms, :], q='pool')
                elif kind in ('q', 'kv'):
                    ob = obb[n % 3]
                    c.copy(ob[0:ms, :], ps[0:ms, 0:TT], eng=('act' if n % 2 else 'dve'))
                    dst = S['qT'] if kind == 'q' else S['kvT']
                    c.dma(dst[idx * 128:idx * 128 + ms, ts_], ob[0:ms, :], q='pool')
                elif kind == 'ng':
                    ob = obuf[n % 3]
                    c.act(ob[0:ms, :], ps[0:ms, 0:TT], AF.Sigmoid)
                    c.dma(S['ngT'][0:ms, ts_], ob[0:ms, :], q='pool')
                elif kind == 'gate':
                    ob = obuf[n % 3]
                    c.act(ob[0:ms, :], ps[0:ms, 0:TT], AF.Sigmoid)
                    c.dma(S['gateT'][idx * 128:idx * 128 + ms, ts_], ob[0:ms, :], q='pool')
            fm_proj(c, pp, wsl, win, tiles, xb, TT, evac)
            vslabs = [(OFF_GM + 2048 + j * 512, 512) for j in range(4)]

            def vload(s, buf):
                c.dma(buf[:], win[:, s[0]:s[0] + 512].rearrange("(kt p) m -> p kt m", p=128), q='sp')

            def vcomp(s, buf):
                for tq in range(TT // 128):
                    ps = pp.get()
                    for k in range(KT):
                        c.mm(ps[:, :], xb[:, k, tq * 128:(tq + 1) * 128], buf[:, k, :], start=(k == 0), stop=(k == KT - 1))
                    vb = vbuf[cnt[0] % 2]
                    cnt[0] += 1
                    c.act(vb[:], ps[:], AF.Gelu)
                    j0 = s[0] - OFF_GM - 2048
                    c.dma(S['vtok'][tt * TT + tq * 128:tt * TT + (tq + 1) * 128, j0:j0 + 512], vb[:], q='pool')
            run_slabs(vslabs, wtk, vload, vcomp)


def gmlp_pass(c, pp, T, l, W, S, consts):
    with ExitStack() as es:
        gbc = c.sb('gmt_g', [128, D], stack=es)
        bbc = c.sb('gmt_b', [128, D], stack=es)
        bsb = c.sb('gmt_bs', [128, 16, 128], stack=es)
        wst = c.sb('gm_wsT', [128, 16, 128], BF16, stack=es)
        wraw = [c.sb('gm_wr%d' % i, [128, 128], stack=es) for i in range(2)]
        c.dma(gbc[:], W['gm_ln_g'][l:l + 1, :].partition_broadcast(128).rearrange("p o d -> p (o d)"), q='sp')
        c.dma(bbc[:], W['gm_ln_b'][l:l + 1, :].partition_broadcast(128).rearrange("p o d -> p (o d)"), q='sp')
        c.dma(bsb[:].rearrange("p g t -> p (g t)"), W['gm_bs'][l:l + 1].rearrange("o g t -> o (g t)").partition_broadcast(128).rearrange("p o d -> p (o d)"), q='sp')
        triu = consts['triu']
        for g in range(16):
            wr = wraw[g % 2]
            c.dma(wr[:], W['gm_ws'][l, g])
            ps = pp.get()
            c.tr(ps[:, 0:128], wr[:], consts['ident'][:])
            c.tt(wst[:, g, :], ps[:, 0:128], triu[:], ALU.mult)
        vt = [c.sb('gm_v%d' % i, [128, D], stack=es) for i in range(2)]
        vc = c.sb('gm_vc', [128, D], stack=es)
        vn = c.sb('gm_vn', [128, D], BF16, stack=es)
        junk = c.sb('gm_junk', [128, D], BF16, stack=es)
        stat = c.sb('gm_stat', [128, 8], stack=es)
        ut = [c.sb('gm_u%d' % i, [128, 16, 128], stack=es) for i in range(2)]
        yt = [c.sb('gm_y%d' % i, [128, 16, 128], BF16, stack=es) for i in range(2)]
        tmp = c.sb('gm_tmp', [128, 512], stack=es)
        for ch in range(T // 128):
            v = vt[ch % 2]
            u = ut[ch % 2]
            y = yt[ch % 2]
            c.dma(v[:], S['vtok'][ch * 128:(ch + 1) * 128, :])
            c.dma(u[:], S['uT'][:, ch * 128:(ch + 1) * 128].rearrange("(g p) t -> p g t", p=128))
            c.reduce(stat[:, 0:1], v[:], ALU.add)
            c.ts(stat[:, 1:2], stat[:, 0:1], 1.0 / D)
            c.ts(vc[:], v[:], stat[:, 1:2], None, ALU.subtract)
            c.act(junk[:], vc[:], AF.Square, accum_out=stat[:, 2:3])
            c.ts(stat[:, 3:4], stat[:, 2:3], 1.0 / D, LN_EPS, ALU.mult, ALU.add)
            c.act(stat[:, 3:4], stat[:, 3:4], AF.Sqrt)
            c.recip(stat[:, 4:5], stat[:, 3:4])
            c.stt(vc[:], vc[:], stat[:, 4:5], gbc[:], ALU.mult, ALU.mult)
            c.tt(vn[:], vc[:], bbc[:], ALU.add)
            for g4 in range(4):
                ps = pp.get()
                for j in range(4):
                    g = g4 * 4 + j
                    c.mm(ps[:, j * 128:(j + 1) * 128], vn[:, g * 128:(g + 1) * 128], wst[:, g, :])
                c.tt(tmp[:], ps[:], bsb[:, g4 * 4:(g4 + 1) * 4, :].rearrange("p g t -> p (g t)"), ALU.add)
                c.tt(y[:, g4 * 4:(g4 + 1) * 4, :].rearrange("p g t -> p (g t)"), tmp[:], u[:, g4 * 4:(g4 + 1) * 4, :].rearrange("p g t -> p (g t)"), ALU.mult)
            c.dma(S['ygmT'][:, ch * 128:(ch + 1) * 128].rearrange("(g p) t -> p g t", p=128), y[:], q='pool')


def merge_pass(c, pp, T, l, W, Wb, S, lng, lnb, consts, xT_src, xT_dst, branches):
    with ExitStack() as es:
        st = {}
        st['zb'] = c.sb('zb', [128, KT, TT], BF16, stack=es)
        st['zq'] = c.sb('zq', [128, KT, TT], BF16, stack=es)
        st['mean'] = c.sb('mean', [128, TT], stack=es)
        st['rstd'] = c.sb('rstd', [128, TT], stack=es)
        st['tmp512'] = c.sb('tmp512', [128, TT], stack=es)
        xs = c.sb('xs', [128, KT, TT], stack=es)
        mg = c.sb('mg', [128, KT, TT], stack=es)
        yb = [c.sb('mg_y%d' % i, [128, KT, TT], BF16, stack=es) for i in range(2)]
        wsl = [c.sb('mg_w%d' % i, [128, KT, 512], BF16, stack=es) for i in range(2)]
        gt = [c.sb('mg_g%d' % i, [128, TT], stack=es) for i in range(6)]
        tmp = [c.sb('mg_t%d' % i, [128, TT], stack=es) for i in range(2)]
        ysrc = {0: S['yrwT'], 1: S['ygmT'], 2: S['ynsT']}
        tiles = [(m * 128, 128, m) for m in range(KT)]
        cnt = [0]
        gcnt = [0]
        for tt in range(T // TT):
            ts_ = slice(tt * TT, (tt + 1) * TT)
            c.dma(xs[:], xT_src[:, ts_].rearrange("(k p) t -> p k t", p=128))
            first = True
            for bi, i in enumerate(branches):
                y = yb[bi % 2]
                c.dma(y[:], ysrc[i][:, ts_].rearrange("(k p) t -> p k t", p=128))

                gq = []

                def pre(c0, ms, m, i=i, gq=gq):
                    g = gt[gcnt[0] % 6]
                    gcnt[0] += 1
                    c.dma(g[:], S['gateT'][i * D + m * 128:i * D + (m + 1) * 128, ts_], q='act')
                    gq.append(g)

                def evac(ps, c0, ms, m, i=i, first=first, gq=gq):
                    n = cnt[0]
                    cnt[0] += 1
                    g = gq.pop(0)
                    if first:
                        c.tt(mg[:, m, :], ps[:, 0:TT], g[:], ALU.mult)
                    else:
                        t_ = tmp[n % 2]
                        c.tt(t_[:], ps[:, 0:TT], g[:], ALU.mult)
                        c.tt(mg[:, m, :], mg[:, m, :], t_[:], ALU.add, eng='pool')
                fm_proj(c, pp, wsl, Wb['w_br'][l, i], tiles, y, TT, evac, pre=pre)
                first = False
            c.copy(st['zb'][:], mg[:], eng='act')

            def evac2(ps, c0, ms, m):
                c.stt(xs[:, m, :], xs[:, m, :], ALPHA, ps[:, 0:TT], ALU.mult, ALU.add)
            fm_proj(c, pp, wsl, Wb['w_o'][l], tiles, st['zb'], TT, evac2)
            layer_norm_fm(c, pp, st, xs, lng[:, l * 3 + 1, :], lnb[:, l * 3 + 1, :], TT, consts)
            c.dma(xT_dst[:, ts_].rearrange("(k p) t -> p k t", p=128), xs[:], q='pool')


WSHAPES = {
    'w_in': [DEPTH, D, C_IN], 'ffn1_wg': [DEPTH, D, DFF], 'ffn1_wu': [DEPTH, D, DFF], 'ffn1_wd': [DEPTH, DFF, D],
    'ffn2_wg': [DEPTH, D, DFF], 'ffn2_wu': [DEPTH, D, DFF], 'ffn2_wd': [DEPTH, DFF, D],
    'w_br': [DEPTH, 3, D, D], 'w_o': [DEPTH, D, D],
    'ln_g': [DEPTH, 3, D], 'ln_b': [DEPTH, 3, D],
    'rw_mu': [DEPTH, RW_COLS], 'rw_w0': [DEPTH, D], 'rw_w2': [DEPTH, 96, D], 'rw_a0': [DEPTH, D], 'rw_a2': [DEPTH, 96, D],
    'rw_g2': [DEPTH, 256, D], 'rw_v0': [DEPTH - 1, D], 'rw_v1': [DEPTH - 1, D, 64], 'rw_v2': [DEPTH - 1, 64, D],
    'rw_k_k': [DEPTH, D], 'rw_k_a': [DEPTH, D], 'rw_r_k': [DEPTH, 32, 64], 'rw_gn_g': [DEPTH, D], 'rw_gn_b': [DEPTH, D],
    'gm_ln_g': [DEPTH, D], 'gm_ln_b': [DEPTH, D], 'gm_ws': [DEPTH, 16, 128, 128], 'gm_bs': [DEPTH, 16, 128],
    'nsa_pos_k': [DEPTH, 32, 192], 'nsa_pos_v': [DEPTH, 32, 128], 'nsa_phi_k1': [DEPTH, 6144, 192], 'nsa_phi_k2': [DEPTH, 192, 192],
    'nsa_phi_v1': [DEPTH, 4096, 128], 'nsa_phi_v2': [DEPTH, 128, 128],
}
BIGW = ['w_in', 'ffn1_wg', 'ffn1_wu', 'ffn1_wd', 'ffn2_wg', 'ffn2_wu', 'ffn2_wd', 'w_br', 'w_o']


def build(T=4096, nlayers=DEPTH, passes=('ffn1', 'inproj', 'gmlp', 'rwkv', 'nsa', 'merge', 'ffn2'), branches=(0, 1, 2),
          use=None, debug=False):
    net = Net(T)
    c = net.c
    nc = net.nc
    x_in = net.din('x', [T, D])
    CONST_SHAPES = {'c_ident': [128, 128], 'c_triu': [128, 128], 'c_bones': [128, 128], 'c_maskqr': [128, 128],
                    'c_masksl': [128, 64], 'c_reset': [128, 512], 'c_ident2': [128, 64], 'c_maskc': [128, 8],
                    'c_esel': [64, 4096], 'c_caus4': [128, 512], 'c_first4': [128, 512]}
    cd = {k: net.din(k, s) for k, s in CONST_SHAPES.items()}
    W = {}
    for k, s in WSHAPES.items():
        if use is None or k in use:
            W[k] = net.din(k, s)
    out = net.dout('out', [T, D])

    mk = net.dout if debug else net.dscr
    xT = [net.dscr('xT%d' % i, [D, T]) for i in range(2)]
    Wb = {}
    for k in BIGW:
        if k in W:
            Wb[k] = net.dscr(k + '_b', WSHAPES[k], BF16)
    S = {
        'pRW': mk('s_pRW', [RW_COLS, T]), 'uT': mk('s_uT', [D, T]), 'vtok': mk('s_vtok', [T, D]),
        'qT': mk('s_qT', [3072, T], BF16), 'kvT': mk('s_kvT', [3840, T], BF16), 'ngT': mk('s_ngT', [48, T]),
        'gateT': mk('s_gateT', [3 * D, T]),
        'yrwT': mk('s_yrwT', [D, T], BF16), 'ygmT': mk('s_ygmT', [D, T], BF16), 'ynsT': mk('s_ynsT', [D, T], BF16),
        'vfirstT': net.dscr('s_vfirstT', [D, T]),
    }
    net.S = S

    ident = c.sb('ident', [128, 128])
    c.dma(ident[:], cd['c_ident'])
    triu = c.sb('triu', [128, 128])
    c.dma(triu[:], cd['c_triu'])
    ones_bf = c.sb('ones_bf', [128, 128], BF16)
    c.memset(ones_bf[:], 1.0)
    consts = {'ident': ident, 'ones_bf': ones_bf, 'triu': triu}
    consts['d_esel'] = cd['c_esel']
    consts['d_caus4'] = cd['c_caus4']
    consts['d_first4'] = cd['c_first4']
    for nm in ['bones', 'maskqr', 'masksl', 'reset', 'ident2', 'maskc']:
        consts[nm] = c.sb('k_' + nm, CONST_SHAPES['c_' + nm])
        c.dma(consts[nm][:], cd['c_' + nm])
    lng = c.sb('lng', [128, DEPTH * 3, KT])
    lnb = c.sb('lnb', [128, DEPTH * 3, KT])
    c.dma(lng[:], W['ln_g'].rearrange("l i (k p) -> p (l i) k", p=128), q='sp', allow_slow_non_contiguous=True)
    c.dma(lnb[:], W['ln_b'].rearrange("l i (k p) -> p (l i) k", p=128), q='sp', allow_slow_non_contiguous=True)
    pp = PsumPool(c)

    with ExitStack() as es:
        cva = [c.sb('cva%d' % i, [128, 4096], F32, stack=es) for i in range(4)]
        cvb = [c.sb('cvb%d' % i, [128, 4096], BF16, stack=es) for i in range(4)]
        k = 0
        for name in Wb:
            for l in range(nlayers):
                s = WSHAPES[name]
                if name == 'w_br':
                    for i in range(3):
                        k = conv_weight(c, cva, cvb, W[name][l, i], Wb[name][l, i], s[2], s[3], k)
                else:
                    k = conv_weight(c, cva, cvb, W[name][l], Wb[name][l], s[1], s[2], k)

    c.barrier()
    with ExitStack() as es:
        xtok = [c.sb('xtok%d' % i, [128, D], F32, stack=es) for i in range(2)]
        xo = [c.sb('xo%d' % i, [128, 4, 128], F32, stack=es) for i in range(2)]
        n = 0
        for t in range(T // 128):
            xt = xtok[t % 2]
            c.dma(xt[:], x_in[t * 128:(t + 1) * 128, :])
            for g4 in range(KT // 4):
                ps = pp.get()
                for j in range(4):
                    k = g4 * 4 + j
                    c.tr(ps[:, j * 128:(j + 1) * 128], xt[:, k * 128:(k + 1) * 128], ident[:])
                o = xo[n % 2]
                n += 1
                c.copy(o[:].rearrange("p a b -> p (a b)"), ps[:], eng=('dve' if n % 2 else 'act'))
                c.dma(xT[0][g4 * 512:(g4 + 1) * 512, t * 128:(t + 1) * 128].rearrange("(a p) t -> p a t", p=128), o[:], q='pool')

    c.barrier()
    cur = 0
    for l in range(nlayers):
        if 'ffn1' in passes:
            ffn_pass(c, pp, T, lng, lnb, consts, xT[cur], xT[1 - cur], Wb['ffn1_wg'][l], Wb['ffn1_wu'][l], Wb['ffn1_wd'][l], l, 0)
            c.barrier()
            cur = 1 - cur
        if 'inproj' in passes:
            inproj_pass(c, pp, T, l, W, Wb, S, xT[cur])
            c.barrier()
        if 'gmlp' in passes:
            gmlp_pass(c, pp, T, l, W, S, consts)
            c.barrier()
        if 'rwkv' in passes:
            rwkv_pass(c, pp, T, l, W, S, consts)
            c.barrier()
        if 'nsa' in passes:
            nsa_pass(c, pp, T, l, W, S, consts)
            c.barrier()
        if 'merge' in passes:
            merge_pass(c, pp, T, l, W, Wb, S, lng, lnb, consts, xT[cur], xT[1 - cur], branches)
            c.barrier()
            cur = 1 - cur
        if 'ffn2' in passes:
            ffn_pass(c, pp, T, lng, lnb, consts, xT[cur], xT[1 - cur], Wb['ffn2_wg'][l], Wb['ffn2_wu'][l], Wb['ffn2_wd'][l], l, 2)
            c.barrier()
            cur = 1 - cur

    c.barrier()
    with ExitStack() as es:
        xf = [c.sb('xf%d' % i, [128, KT, 128], F32, stack=es) for i in range(2)]
        yo = [c.sb('yo%d' % i, [128, D], F32, stack=es) for i in range(2)]
        for t in range(T // 128):
            a = xf[t % 2]
            c.dma(a[:], xT[cur][:, t * 128:(t + 1) * 128].rearrange("(k p) t -> p k t", p=128))
            y = yo[t % 2]
            for g4 in range(KT // 4):
                ps = pp.get()
                for j in range(4):
                    k = g4 * 4 + j
                    c.tr(ps[:, j * 128:(j + 1) * 128], a[:, k, :], ident[:])
                c.copy(y[:, g4 * 512:(g4 + 1) * 512], ps[:], eng=('dve' if g4 % 2 else 'act'))
            c.dma(out[t * 128:(t + 1) * 128, :], y[:], q='pool')
    c.wait_all('sp')
    return net


RW_SBT = 256
RW_NCH = RW_SBT // 64
RW_G = 4
USE_F32R = True
EXPM05 = 0.6065306597126334


def rwkv_pass(c, pp, T, l, W, S, consts, stop=99):
    nc = c.nc
    SBT, NCH, G = RW_SBT, RW_NCH, RW_G
    ident = consts['ident']
    bones = consts['bones']
    mqr = consts['maskqr']
    msl = consts['masksl']
    reset = consts['reset']
    pRW = S['pRW']
    with ExitStack() as es:
        def sb(name, shape, dt=F32):
            return c.sb('rw_' + name, shape, dt, stack=es)
        par = {}
        for nm in ['rw_w0', 'rw_a0', 'rw_k_k', 'rw_k_a', 'rw_gn_g', 'rw_gn_b']:
            par[nm] = sb(nm, [128, 16])
            c.dma(par[nm][:], W[nm][l:l + 1, :].rearrange("o (q p) -> p (o q)", p=128), q='sp', allow_slow_non_contiguous=True)
        par['rw_r_k'] = sb('rk', [128, 16])
        c.dma(par['rw_r_k'][:], W['rw_r_k'][l:l + 1].rearrange("o (q a) n -> (a n) (o q)", a=2), q='sp', allow_slow_non_contiguous=True)
        par['omka'] = sb('omka', [128, 16])
        c.ts(par['omka'][:], par['rw_k_a'][:], -1.0, 1.0, ALU.mult, ALU.add)
        if l > 0:
            par['rw_v0'] = sb('v0', [128, 16])
            c.dma(par['rw_v0'][:], W['rw_v0'][l - 1:l, :].rearrange("o (q p) -> p (o q)", p=128), q='sp', allow_slow_non_contiguous=True)
        w2b = sb('w2b', [96, D], BF16)
        a2b = sb('a2b', [96, D], BF16)
        g2b = sb('g2b', [128, 2, D], BF16)
        if l > 0:
            v1b = sb('v1b', [128, KT, 64], BF16)
            v2b = sb('v2b', [64, D], BF16)
            vvT = sb('vvT', [64, T], BF16)
        with ExitStack() as es2:
            stg = c.sb('rw_stg', [128, 2, D], stack=es2)
            c.dma(stg[0:96, 0, :], W['rw_w2'][l])
            c.copy(w2b[:], stg[0:96, 0, :], eng='act')
            c.dma(stg[0:96, 1, :], W['rw_a2'][l])
            c.copy(a2b[:], stg[0:96, 1, :], eng='act')
            c.dma(stg[:], W['rw_g2'][l].rearrange("(k p) d -> p k d", p=128))
            c.copy(g2b[:], stg[:], eng='act')
            if l > 0:
                c.dma(stg[:, 0, 0:KT * 64].rearrange("p (k e) -> p k e", e=64), W['rw_v1'][l - 1].rearrange("(k p) e -> p k e", p=128))
                c.copy(v1b[:].rearrange("p k e -> p (k e)"), stg[:, 0, 0:KT * 64], eng='act')
                c.dma(stg[0:64, 1, :], W['rw_v2'][l - 1])
                c.copy(v2b[:], stg[0:64, 1, :], eng='act')
                vld = c.sb('rw_vld', [128, KT, 512], stack=es2)
                vlb = c.sb('rw_vlb', [128, KT, 512], BF16, stack=es2)
                for tb in range(T // 512):
                    c.dma(vld[:], pRW[4096:6144, tb * 512:(tb + 1) * 512].rearrange("(k p) t -> p k t", p=128))
                    c.copy(vlb[:], vld[:], eng='act')
                    ps = pp.get()
                    for k in range(KT):
                        c.mm(ps[0:64, :], v1b[:, k, :], vlb[:, k, :], start=(k == 0), stop=(k == KT - 1))
                    c.copy(vvT[:, tb * 512:(tb + 1) * 512], ps[0:64, :], eng='dve')
            c.barrier()
        lw = sb('lw', [96, 2, SBT])
        lg = sb('lg', [128, 2, SBT])
        twl = sb('twl', [96, SBT], BF16)
        alb = sb('alb', [96, SBT], BF16)
        sgl = sb('sgl', [128, 2, SBT], BF16)
        names = ['r', 'k', 'v', 'a', 'ld', 'cs', 'kk', 'kp', 'bv', 't0', 't1', 't2', 'BtT', 'KtT', 'BcT', 'KcT', 'g', 'bonus', 'yT', 'vf', 'sq', 'vr']
        P = [{n: sb('%s%d' % (n, g), [128, SBT]) for n in names} for g in range(G)]
        identr = sb('identr', [128, 128])
        bonesr = sb('bonesr', [128, 128])
        for g in range(G):
            P[g]['AR'] = sb('AR%d' % g, [128, NCH, 128])
            P[g]['QRB'] = sb('QRB%d' % g, [128, NCH, 128])
            P[g]['AKRK'] = sb('AKRK%d' % g, [128, NCH, 128])
            P[g]['Nn'] = sb('Nn%d' % g, [128, NCH, 64])
            P[g]['PwQ'] = sb('PwQ%d' % g, [128, NCH, 128])
            P[g]['PwN'] = sb('PwN%d' % g, [128, NCH, 128])
            P[g]['Tt'] = sb('Tt%d' % g, [128, NCH, 128])
            for n_ in ('PwQ', 'PwN', 'Tt'):
                c.memset(P[g][n_][:], 0.0, eng='pool')
            P[g]['Vtok'] = sb('Vtok%d' % g, [128, NCH, 64])
            P[g]['Bctok'] = sb('Bctok%d' % g, [128, NCH, 64])
            P[g]['Kctok'] = sb('Kctok%d' % g, [128, NCH, 64])
            P[g]['PC'] = sb('PC%d' % g, [128, NCH])
            P[g]['yb'] = sb('yb%d' % g, [128, SBT], BF16)
        St = sb('St', [128, G, 64])
        X0 = sb('X0', [128, G, 64])
        Us = sb('Us', [128, G, 64])
        if USE_F32R:
            c.r32 = set()
            for g in range(G):
                for n in ['AR', 'BtT', 'KtT', 'BcT', 'KcT', 'QRB', 'AKRK', 'Nn', 'PwQ', 'PwN', 'Tt', 'Vtok', 'Bctok', 'Kctok', 'yT', 'sq', 'vr']:
                    c.r32.add(P[g][n].name)
            for t_ in (St, X0, Us, identr, bonesr):
                c.r32.add(t_.name)
        c.copy(identr[:], ident[:], eng='act')
        c.copy(bonesr[:], bones[:], eng='act')
        ident = identr
        bones = bonesr

        def hv(ap):
            return ap.rearrange("p (c t) -> p c t", t=64)

        for pg in range(16 // G):
            c.memset(St[:], 0.0)
            for sbi in range(T // SBT):
                tsl = slice(sbi * SBT, (sbi + 1) * SBT)
                c.dma(lw[:, 0, :], pRW[6144:6240, tsl])
                c.dma(lw[:, 1, :], pRW[6240:6336, tsl])
                c.dma(lg[:], pRW[6336:6592, tsl].rearrange("(k p) t -> p k t", p=128))
                c.act(twl[:], lw[:, 0, :], AF.Tanh)
                c.copy(alb[:], lw[:, 1, :], eng='dve')
                c.act(sgl[:], lg[:], AF.Sigmoid)
                def prep(g):
                    q = pg * G + g
                    t = P[g]
                    rows = slice(q * 128, (q + 1) * 128)
                    c.dma(t['r'][:], pRW[q * 128:(q + 1) * 128, tsl])
                    yield
                    c.dma(t['k'][:], pRW[2048 + q * 128:2048 + (q + 1) * 128, tsl])
                    yield
                    c.dma(t['v'][:], pRW[4096 + q * 128:4096 + (q + 1) * 128, tsl])
                    yield
                    ps = pp.get()
                    c.mm(ps[:, 0:SBT], w2b[:, rows], twl[:])
                    yield
                    c.act(t['ld'][:], ps[:, 0:SBT], AF.Sigmoid, bias=par['rw_w0'][:, q:q + 1])
                    yield
                    c.ts(t['ld'][:], t['ld'][:], -EXPM05)
                    yield
                    c.mm(ps[:, SBT:2 * SBT], a2b[:, rows], alb[:])
                    yield
                    c.act(t['a'][:], ps[:, SBT:2 * SBT], AF.Sigmoid, bias=par['rw_a0'][:, q:q + 1])
                    yield
                    ps = pp.get()
                    for kk_ in range(2):
                        c.mm(ps[:, 0:SBT], g2b[:, kk_, rows], sgl[:, kk_, :], start=(kk_ == 0), stop=(kk_ == 1))
                        yield
                    c.copy(t['g'][:], ps[:, 0:SBT], eng='act')
                    yield
                    if l > 0:
                        c.mm(ps[:, SBT:2 * SBT], v2b[:, rows], vvT[:, tsl])
                        yield
                        c.act(t['t0'][:], ps[:, SBT:2 * SBT], AF.Sigmoid, bias=par['rw_v0'][:, q:q + 1])
                        yield
                        c.dma(t['vf'][:], S['vfirstT'][rows, tsl])
                        yield
                        c.tt(t['t1'][:], t['vf'][:], t['v'][:], ALU.subtract, eng='pool')
                        yield
                        c.tt(t['t1'][:], t['t1'][:], t['t0'][:], ALU.mult, eng='pool')
                        yield
                        c.tt(t['v'][:], t['v'][:], t['t1'][:], ALU.add, eng='pool')
                        yield
                    else:
                        c.dma(S['vfirstT'][rows, tsl], t['v'][:], q='pool')
                        yield
                    if stop <= 1:
                        return
                    c.ts(t['kk'][:], t['k'][:], par['rw_k_k'][:, q:q + 1])
                    yield
                    c.tt(t['sq'][:], t['kk'][:], t['kk'][:], ALU.mult, eng='pool')
                    yield
                    ps = pp.get()
                    c.mm(ps[:, 0:SBT], bones[:], t['sq'][:])
                    yield
                    c.ts(t['t1'][:], ps[:, 0:SBT], 1e-24, None, ALU.max)
                    yield
                    c.act(t['t1'][:], t['t1'][:], AF.Sqrt)
                    yield
                    c.recip(t['t1'][:], t['t1'][:])
                    yield
                    c.tt(t['kk'][:], t['kk'][:], t['t1'][:], ALU.mult)
                    yield
                    c.ts(t['t2'][:], t['a'][:], par['rw_k_a'][:, q:q + 1], par['omka'][:, q:q + 1], ALU.mult, ALU.add)
                    yield
                    c.tt(t['kp'][:], t['k'][:], t['t2'][:], ALU.mult, eng='pool')
                    yield
                    c.tt(t['bv'][:], t['kk'][:], t['a'][:], ALU.mult, eng='pool')
                    yield
                    c.stt(t['sq'][:], t['r'][:], par['rw_r_k'][:, q:q + 1], t['kp'][:], ALU.mult, ALU.mult)
                    yield
                    c.mm(ps[:, SBT:2 * SBT], bones[:], t['sq'][:])
                    yield
                    c.copy(t['vr'][:], t['v'][:], eng='act')
                    yield
                    c.tt(t['bonus'][:], ps[:, SBT:2 * SBT], t['v'][:], ALU.mult)
                    yield
                    if stop <= 2:
                        return
                    c.op('dve', lambda t=t: nc.vector.tensor_tensor_scan(t['cs'][:], reset[:, 0:SBT], t['ld'][:], 0.0, ALU.mult, ALU.add),
                         [reset[:, 0:SBT], t['ld'][:]], [t['cs'][:]])
                    yield
                    c.tt(t['t0'][:], t['cs'][:], t['ld'][:], ALU.subtract, eng='pool')
                    yield
                    c.act(t['t1'][:], t['t0'][:], AF.Exp)
                    yield
                    c.stt(t['AR'][:, :, 0:64], hv(t['kk'][:]), -1.0, hv(t['t1'][:]), ALU.mult, ALU.mult)
                    yield
                    c.act(t['t1'][:], t['cs'][:], AF.Exp)
                    yield
                    c.tt(t['AR'][:, :, 64:128], hv(t['r'][:]), hv(t['t1'][:]), ALU.mult)
                    yield
                    c.act(t['t1'][:], t['cs'][:], AF.Exp, scale=-1.0)
                    yield
                    c.tt(t['BtT'][:], t['bv'][:], t['t1'][:], ALU.mult)
                    yield
                    c.tt(t['KtT'][:], t['kp'][:], t['t1'][:], ALU.mult, eng='pool')
                    yield
                    csv = hv(t['cs'][:])
                    c.tt(hv(t['t0'][:]), csv[:, :, 63:64].broadcast_to([128, NCH, 64]), csv, ALU.subtract)
                    yield
                    c.act(t['t1'][:], t['t0'][:], AF.Exp)
                    yield
                    c.tt(t['BcT'][:], t['bv'][:], t['t1'][:], ALU.mult)
                    yield
                    c.tt(t['KcT'][:], t['kp'][:], t['t1'][:], ALU.mult, eng='pool')
                    yield
                    c.act(t['PC'][:], csv[:, :, 63], AF.Exp)
                    yield
                    if stop <= 3:
                        return
                    ps1 = pp.get()
                    ps2 = pp.get()
                    ps3 = pp.get()
                    ps4 = pp.get()
                    for h in range(2):
                        hs = slice(h * 64, (h + 1) * 64)
                        for ch in range(NCH):
                            cs_ = slice(ch * 64, (ch + 1) * 64)
                            c.mm(ps1[hs, ch * 128:(ch + 1) * 128], t['BtT'][hs, cs_], t['AR'][hs, ch, :])
                            c.mm(ps2[hs, ch * 128:(ch + 1) * 128], t['KtT'][hs, cs_], t['AR'][hs, ch, :])
                            c.mm(ps3[hs, cs_], t['AR'][hs, ch, 0:64], t['BtT'][hs, cs_])
                            c.mm(ps4[hs, cs_], t['vr'][hs, cs_], ident[hs, hs])
                            c.mm(ps4[hs, SBT + ch * 64:SBT + (ch + 1) * 64], t['BcT'][hs, cs_], ident[hs, hs])
                    mq = mqr[:].unsqueeze(1).broadcast_to([128, NCH, 128])
                    c.tt(t['QRB'][:], ps1[:, 0:NCH * 128].rearrange("p (c t) -> p c t", t=128), mq, ALU.mult)
                    c.tt(t['AKRK'][:], ps2[:, 0:NCH * 128].rearrange("p (c t) -> p c t", t=128), mq, ALU.mult)
                    c.tt(t['Nn'][:], hv(ps3[:, 0:SBT]), msl[:].unsqueeze(1).broadcast_to([128, NCH, 64]), ALU.mult)
                    c.copy(t['Vtok'][:], hv(ps4[:, 0:SBT]), eng='act')
                    c.copy(t['Bctok'][:], hv(ps4[:, SBT:2 * SBT]), eng='act')
                    ps5 = pp.get()
                    for h in range(2):
                        hs = slice(h * 64, (h + 1) * 64)
                        for ch in range(NCH):
                            cs_ = slice(ch * 64, (ch + 1) * 64)
                            c.mm(ps5[hs, cs_], t['KcT'][hs, cs_], ident[hs, hs])
                    c.copy(t['Kctok'][:], hv(ps5[:, 0:SBT]), eng='act')
                    if stop <= 4:
                        return
                    yield
                gens = [prep(g) for g in range(G)]
                while gens:
                    nxt = []
                    for gen in gens:
                        try:
                            next(gen)
                            nxt.append(gen)
                        except StopIteration:
                            pass
                    gens = nxt
                if stop > 4:
                    for g in range(G):
                        t = P[g]
                        for h in range(2):
                            hs = slice(h * 64, (h + 1) * 64)
                            c.copy(t['PwQ'][hs, :, hs], t['QRB'][hs, :, 0:64], eng='pool')
                            c.copy(t['PwN'][hs, :, hs], t['Nn'][hs, :, :], eng='pool')
                            c.tt(t['Tt'][hs, :, hs], t['QRB'][hs, :, 0:64], consts['ident2'][hs, :].unsqueeze(1).broadcast_to([64, NCH, 64]), ALU.add)
                    for lev in range(5):
                        bn = {}
                        bq = {}
                        for g in range(G):
                            t = P[g]
                            bn[g] = pp.get()
                            for ch in range(NCH):
                                c.mm(bn[g][:, ch * 128:(ch + 1) * 128], t['PwQ'][:, ch, :], t['PwN'][:, ch, :])
                            if lev < 4:
                                bq[g] = pp.get()
                                for ch in range(NCH):
                                    c.mm(bq[g][:, ch * 128:(ch + 1) * 128], t['PwN'][:, ch, :], t['PwQ'][:, ch, :])
                        for g in range(G):
                            t = P[g]
                            c.copy(t['PwN'][:].rearrange("p c t -> p (c t)"), bn[g][:, :], eng='act')
                            if lev < 4:
                                c.copy(t['PwQ'][:].rearrange("p c t -> p (c t)"), bq[g][:, :], eng='dve')
                        bt = {}
                        for g in range(G):
                            t = P[g]
                            bt[g] = pp.get()
                            for ch in range(NCH):
                                c.mm(bt[g][:, ch * 128:(ch + 1) * 128], t['PwN'][:, ch, :], t['Tt'][:, ch, :])
                        for g in range(G):
                            t = P[g]
                            c.tt(t['Tt'][:].rearrange("p c t -> p (c t)"), t['Tt'][:].rearrange("p c t -> p (c t)"), bt[g][:, :], ALU.add)
                if stop <= 5:
                    continue
                for ch in range(NCH):
                    cs_ = slice(ch * 64, (ch + 1) * 64)
                    psx = pp.get()
                    for g in range(G):
                        t = P[g]
                        for h in range(2):
                            hs = slice(h * 64, (h + 1) * 64)
                            o = psx[hs, g * 64:(g + 1) * 64]
                            c.mm(o, t['AR'][hs, ch, 0:64], St[hs, g, :], start=True, stop=False)
                            c.mm(o, t['AKRK'][hs, ch, 0:64], t['Vtok'][hs, ch, :], start=False, stop=True)
                    c.copy(X0[:].rearrange("p g i -> p (g i)"), psx[:, 0:G * 64], eng='act')
                    psu = pp.get()
                    for g in range(G):
                        t = P[g]
                        c.mm(psu[:, g * 64:(g + 1) * 64], t['Tt'][:, ch, :], X0[:, g, :])
                    c.copy(Us[:].rearrange("p g i -> p (g i)"), psu[:, 0:G * 64], eng='dve')
                    psy = pp.get()
                    for g in range(G):
                        t = P[g]
                        for h in range(2):
                            hs = slice(h * 64, (h + 1) * 64)
                            o = psy[hs, g * 64:(g + 1) * 64]
                            c.mm(o, St[hs, g, :], t['AR'][hs, ch, 64:128], start=True, stop=False)
                            c.mm(o, Us[hs, g, :], t['QRB'][hs, ch, 64:128], start=False, stop=False)
                            c.mm(o, t['Vtok'][hs, ch, :], t['AKRK'][hs, ch, 64:128], start=False, stop=True)
                            o2 = psy[hs, 256 + g * 64:256 + (g + 1) * 64]
                            c.mm(o2, t['Bctok'][hs, ch, :], Us[hs, g, :], start=True, stop=False)
                            c.mm(o2, t['Kctok'][hs, ch, :], t['Vtok'][hs, ch, :], start=False, stop=True)
                    for g in range(G):
                        t = P[g]
                        c.copy(t['yT'][:, cs_], psy[:, g * 64:(g + 1) * 64], eng='act')
                        c.stt(St[:, g, :], St[:, g, :], t['PC'][:, ch:ch + 1], psy[:, 256 + g * 64:256 + (g + 1) * 64], ALU.mult, ALU.add)
                if stop <= 6:
                    continue
                for g in range(G):
                    q = pg * G + g
                    t = P[g]
                    ps = pp.get()
                    c.mm(ps[:, 0:SBT], bones[:], t['yT'][:])
                    c.tt(t['sq'][:], t['yT'][:], t['yT'][:], ALU.mult, eng='pool')
                    c.mm(ps[:, SBT:2 * SBT], bones[:], t['sq'][:])
                    c.act(t['t1'][:], ps[:, 0:SBT], AF.Copy, scale=1.0 / 64)
                    c.tt(t['t2'][:], t['t1'][:], t['t1'][:], ALU.mult)
                    c.stt(t['t2'][:], ps[:, SBT:2 * SBT], 1.0 / 64, t['t2'][:], ALU.mult, ALU.subtract)
                    c.ts(t['t2'][:], t['t2'][:], 64e-5, None, ALU.add)
                    c.act(t['t2'][:], t['t2'][:], AF.Sqrt)
                    c.recip(t['t2'][:], t['t2'][:])
                    c.tt(t['t0'][:], t['yT'][:], t['t1'][:], ALU.subtract)
                    c.tt(t['t0'][:], t['t0'][:], t['t2'][:], ALU.mult)
                    c.ts(t['t0'][:], t['t0'][:], par['rw_gn_g'][:, q:q + 1], par['rw_gn_b'][:, q:q + 1], ALU.mult, ALU.add)
                    c.tt(t['t0'][:], t['t0'][:], t['bonus'][:], ALU.add, eng='pool')
                    c.tt(t['yb'][:], t['t0'][:], t['g'][:], ALU.mult)
                    c.dma(S['yrwT'][q * 128:(q + 1) * 128, tsl], t['yb'][:], q='pool')


def host_consts():
    cc = {}
    cc['c_ident'] = np.eye(128, dtype=np.float32)
    cc['c_triu'] = np.triu(np.ones((128, 128), np.float32))
    bo = np.zeros((128, 128), np.float32)
    bo[:64, :64] = 1
    bo[64:, 64:] = 1
    cc['c_bones'] = bo
    s = np.arange(128)[:, None] % 64
    t = np.arange(64)[None, :]
    cc['c_maskqr'] = np.concatenate([(s < t), (s <= t)], axis=1).astype(np.float32)
    cc['c_masksl'] = (t < s).astype(np.float32)
    r = np.ones((128, 512), np.float32)
    r[:, ::64] = 0
    cc['c_reset'] = r
    cc['c_ident2'] = (s == t).astype(np.float32)
    qq = np.arange(128)[:, None]
    jj = np.arange(8)[None, :]
    cc['c_maskc'] = (qq >= 16 * jj + 15).astype(np.float32)
    cc['c_esel'] = (np.arange(4096)[None, :] // 64 == np.arange(64)[:, None]).astype(np.float32)
    kk = np.arange(128)[:, None]
    q4 = np.tile(np.arange(128), 4)[None, :]
    cc['c_caus4'] = np.where(kk <= q4, 0.0, -30000.0).astype(np.float32)
    cc['c_first4'] = np.where(kk > q4, 0.0, -30000.0).astype(np.float32)
    return cc


NSA_SCALE = 192 ** -0.5
NEGM = -30000.0


def nsa_pass(c, pp, T, l, W, S, consts):
    nc = c.nc
    n_c = T // 16 - 1
    n_s = T // 64
    NQ = T // 128
    ident = consts['ident']
    kvT = S['kvT']
    banks = pp.banks

    class Rot:
        def __init__(self, idx):
            self.idx = idx
            self.i = 0

        def get(self):
            b = banks[self.idx[self.i % len(self.idx)]]
            self.i += 1
            return b
    rs_ = Rot([0, 1, 2, 3])
    rm_ = Rot([6, 7])
    psO = banks[4]
    psL = banks[5]
    with ExitStack() as es:
        def sb(name, shape, dt=F32):
            return c.sb('ns_' + name, shape, dt, stack=es)
        ident_b = sb('identb', [128, 128], BF16)
        c.copy(ident_b[:], ident[:], eng='act')
        ones_b = consts['ones_bf']
        esel = sb('esel', [64, T], BF16)
        caus4 = sb('caus4', [128, 512], BF16)
        first4 = sb('first4', [128, 512], BF16)
        maskc = consts['maskc']
        kcmpT = sb('kcmpT', [96, 2, 4, 256], BF16)
        vcmp = sb('vcmp', [128, 2, 4, 128], BF16)
        c.memset(vcmp[:], 0.0)
        with ExitStack() as es2:
            def sb2(name, shape, dt=F32):
                return c.sb('ns2_' + name, shape, dt, stack=es2)
            stg = sb2('stg', [128, 4096])
            c.dma(stg[0:64, 0:T], consts['d_esel'][:, 0:T])
            c.copy(esel[:], stg[0:64, 0:T], eng='act')
            c.dma(stg[:, 0:512], consts['d_caus4'])
            c.copy(caus4[:], stg[:, 0:512], eng='act')
            c.dma(stg[:, 512:1024], consts['d_first4'])
            c.copy(first4[:], stg[:, 512:1024], eng='act')
            phk1 = sb2('phk1', [96, 64, 192], BF16)
            phv1 = sb2('phv1', [128, 32, 128], BF16)
            phk2 = sb2('phk2', [96, 2, 192], BF16)
            phv2 = sb2('phv2', [128, 128], BF16)
            stk = sb2('stk', [96, 16 * 192])
            for part in range(4):
                c.dma(stk[:].rearrange("p (a e) -> p a e", e=192),
                      W['nsa_phi_k1'][l][part * 1536:(part + 1) * 1536, :].rearrange("(a p) e -> p a e", p=96))
                c.copy(phk1[:, part * 16:(part + 1) * 16, :].rearrange("p a e -> p (a e)"), stk[:], eng=('act' if part % 2 else 'dve'))
            c.dma(stg[:, 0:4096].rearrange("p (a e) -> p a e", e=128), W['nsa_phi_v1'][l].rearrange("(a p) e -> p a e", p=128))
            c.copy(phv1[:].rearrange("p a e -> p (a e)"), stg[:, 0:4096], eng='dve')
            c.dma(stk[:, 0:384].rearrange("p (a e) -> p a e", e=192), W['nsa_phi_k2'][l].rearrange("(a p) e -> p a e", p=96))
            c.copy(phk2[:].rearrange("p a e -> p (a e)"), stk[:, 0:384], eng='act')
            c.dma(stg[:, 0:128], W['nsa_phi_v2'][l])
            c.copy(phv2[:], stg[:, 0:128], eng='act')
            posk = sb2('posk', [96, 2, 32], BF16)
            posv = sb2('posv', [128, 32], BF16)
            for dc in range(2):
                c.dma(stk[:, 400 + dc * 32:432 + dc * 32], W['nsa_pos_k'][l][:, dc * 96:(dc + 1) * 96].rearrange("b p -> p b"), allow_slow_non_contiguous=True)
            c.copy(posk[:].rearrange("p a b -> p (a b)"), stk[:, 400:464], eng='act')
            c.dma(stg[:, 200:232], W['nsa_pos_v'][l].rearrange("b p -> p b"), allow_slow_non_contiguous=True)
            c.copy(posv[:], stg[:, 200:232], eng='act')
            hpk = sb2('hpk', [96, 2])
            hpv = sb2('hpv', [128, 1])
            for et in range(2):
                ps = rm_.get()
                n = 0
                for lq in range(32):
                    for dc in range(2):
                        c.mm(ps[0:96, 0:1], phk1[:, lq * 2 + dc, et * 96:(et + 1) * 96], posk[:, dc, lq:lq + 1], start=(n == 0), stop=(n == 63))
                        n += 1
                c.copy(hpk[:, et:et + 1], ps[0:96, 0:1], eng='dve')
            ps = rm_.get()
            for lq in range(32):
                c.mm(ps[:, 0:1], phv1[:, lq, :], posv[:, lq:lq + 1], start=(lq == 0), stop=(lq == 31))
            c.copy(hpv[:], ps[:, 0:1], eng='dve')
            kc = sb2('kc', [96, 2, T], BF16)
            vc = sb2('vc', [128, T], BF16)
            ghk = sb2('ghk', [96, 2, 256], BF16)
            ghv = sb2('ghv', [128, 256], BF16)
            for g in range(4):
                c.dma(kc[:], kvT[g * 192:(g + 1) * 192, :].rearrange("(a p) t -> p a t", p=96))
                c.dma(vc[:], kvT[768 + g * 128:768 + (g + 1) * 128, :])
                for et in range(2):
                    ps = rm_.get()
                    n = 0
                    for lq in range(32):
                        for dc in range(2):
                            c.mm(ps[0:96, 0:n_c], phk1[:, lq * 2 + dc, et * 96:(et + 1) * 96],
                                 kc[:, dc, lq:lq + 16 * (n_c - 1) + 1:16], start=(n == 0), stop=(n == 63))
                            n += 1
                    c.act(ghk[:, et, 0:n_c], ps[0:96, 0:n_c], AF.Gelu, bias=hpk[:, et:et + 1])
                for e2 in range(2):
                    ps = rm_.get()
                    for ec in range(2):
                        c.mm(ps[0:96, 0:n_c], phk2[:, ec, e2 * 96:(e2 + 1) * 96], ghk[:, ec, 0:n_c], start=(ec == 0), stop=(ec == 1))
                    c.copy(kcmpT[:, e2, g, 0:n_c], ps[0:96, 0:n_c], eng='dve')
                ps = rm_.get()
                for lq in range(32):
                    c.mm(ps[:, 0:n_c], phv1[:, lq, :], vc[:, lq:lq + 16 * (n_c - 1) + 1:16], start=(lq == 0), stop=(lq == 31))
                c.act(ghv[:, 0:n_c], ps[:, 0:n_c], AF.Gelu, bias=hpv[:, 0:1])
                for nb in range((n_c + 127) // 128):
                    w = min(128, n_c - nb * 128)
                    ps = rm_.get()
                    c.mm(ps[0:w, 0:128], ghv[:, nb * 128:nb * 128 + w], phv2[:])
                    c.copy(vcmp[0:w, nb, g, :], ps[0:w, 0:128], eng='dve')
            c.barrier()
        ksT = sb('ksT', [96, 2, T], BF16)
        kwT = sb('kwT', [96, 2, T], BF16)
        vsk = sb('vsk', [128, NQ, 128], BF16)
        vwk = sb('vwk', [128, NQ, 128], BF16)
        vtmp = [sb('vtmp%d' % i, [128, 512], BF16) for i in range(2)]
        Qg = [sb('Qg%d' % i, [96, 4, 2, 128], BF16) for i in range(2)]
        Qd = [sb('Qd%d' % i, [96, 2, 512], BF16) for i in range(2)]
        gbc = [sb('gbc%d' % i, [128, 12, 128]) for i in range(2)]
        yn = sb('yn', [128, 4, 128])
        ynb = [sb('ynb%d' % i, [128, 4, 128], BF16) for i in range(2)]
        Pacc = sb('Pacc', [128, 264])
        ee = [sb('ee%d' % i, [128, 256]) for i in range(4)]
        pb = [sb('pb%d' % i, [128, 256], BF16) for i in range(4)]
        pT = [sb('pT%d' % i, [128, 2, 128], BF16) for i in range(4)]
        st = sb('st', [128, 16])
        imp = sb('imp', [128, 64])
        score = sb('score', [128, 64])
        sc2 = sb('sc2', [128, 64])
        m8 = sb('m8', [128, 16])
        sel = sb('sel', [128, 64])
        sel2 = sb('sel2', [128, 64])
        R = sb('R', [64, 512], BF16)
        R2 = sb('R2', [1, 512], BF16)
        R2w = sb('R2w', [1, 512], BF16)
        negm = sb('negm', [128, 8])
        mpart = sb('mpart', [128, 16])
        PTs = [sb('PTs%d' % i, [128, 512], BF16) for i in range(4)]
        rl = sb('rl', [128, 512])
        ot = sb('ot', [128, 512])
        rl2 = sb('rl2', [128, 512])
        ot2 = sb('ot2', [128, 512])

        def hq(ap):
            return ap.rearrange("p (h q) -> p h q", q=128)

        def dense_gen(i, g, kT, vk, kb0, Rrow, use_sel, gate_j, psO_, psL_, PT_, rl_, ot_):
            Q_ = Qd[i % 2]

            def scores(kb):
                ps = rs_.get()
                mms = [(kT[:, 0, kb * 128:(kb + 1) * 128], Q_[:, 0, :]), (kT[:, 1, kb * 128:(kb + 1) * 128], Q_[:, 1, :]),
                       (ones_b[0:1, :], Rrow[0:1, :])]
                if use_sel:
                    mms.append((esel[0:n_s, kb * 128:(kb + 1) * 128], R[0:n_s, :]))
                if kb == i:
                    mms.append((ident_b[:], caus4[:]))
                if (not use_sel) and i >= 4 and kb == i - 4:
                    mms.append((ident_b[:], first4[:]))
                for n, (a, b) in enumerate(mms):
                    c.mm(ps[:, :], a, b, start=(n == 0), stop=(n == len(mms) - 1))
                return ps
            nxt = scores(kb0)
            yield
            for kb in range(kb0, i + 1):
                ps = nxt
                if kb < i:
                    nxt = scores(kb + 1)
                    yield
                P_ = PT_[kb % 2]
                c.act(P_[:], ps[:, :], AF.Exp, scale=NSA_SCALE)
                yield
                c.mm(psO_[:, :], vk[:, kb, :], P_[:], start=(kb == kb0), stop=(kb == i))
                c.mm(psL_[:, :], ones_b[:], P_[:], start=(kb == kb0), stop=(kb == i))
                yield
            c.ts(rl_[:], psL_[:, :], 1e-30, None, ALU.max)
            yield
            c.recip(rl_[:], rl_[:])
            yield
            c.tt(ot_[:], psO_[:, :], rl_[:], ALU.mult)
            yield
            gv = gbc[i % 2][:].rearrange("p (h j) q -> p h j q", j=3)[:, :, gate_j, :]
            c.tt(hq(ot_[:]), hq(ot_[:]), gv, ALU.mult, eng='pool')
            yield
            c.tt(yn[:], yn[:], hq(ot_[:]), ALU.add, eng='pool')
            yield

        def drive(gens):
            while gens:
                nx = []
                for gen in gens:
                    try:
                        next(gen)
                        nx.append(gen)
                    except StopIteration:
                        pass
                gens = nx

        def rowmax(i, h, kT, k0, k1, col):
            nb = 0
            for s0 in range(k0, k1, 512):
                w = min(512, k1 - s0)
                ps = rs_.get()
                for dc in range(2):
                    c.mm(ps[:, 0:w], Qg[i % 2][:, h, dc, :], kT[:, dc, s0:s0 + w], start=(dc == 0), stop=(dc == 1))
                c.reduce(mpart[:, nb:nb + 1], ps[:, 0:w], ALU.max)
                nb += 1
            if nb > 1:
                c.reduce(negm[:, col:col + 1], mpart[:, 0:nb], ALU.max)
                c.ts(negm[:, col:col + 1], negm[:, col:col + 1], -1.0)
            else:
                c.ts(negm[:, col:col + 1], mpart[:, 0:1], -1.0)

        cnt = [0]
        rm_ = rs_
        for g in range(4):
            base = 1280
            c.dma(ksT[:], kvT[base + g * 192:base + (g + 1) * 192, :].rearrange("(a p) t -> p a t", p=96))
            c.dma(kwT[:], kvT[2560 + g * 192:2560 + (g + 1) * 192, :].rearrange("(a p) t -> p a t", p=96))
            for (src0, dst) in ((base + 768 + g * 128, vsk), (2560 + 768 + g * 128, vwk)):
                for t4 in range(T // 512):
                    vt_ = vtmp[t4 % 2]
                    c.dma(vt_[:], kvT[src0:src0 + 128, t4 * 512:(t4 + 1) * 512])
                    ps = rm_.get()
                    for j in range(4):
                        c.mm(ps[:, j * 128:(j + 1) * 128], vt_[:, j * 128:(j + 1) * 128], ident_b[:])
                    c.copy(dst[:, t4 * 4:(t4 + 1) * 4, :].rearrange("p a d -> p (a d)"), ps[:, :], eng=('act' if t4 % 2 else 'dve'))
            def qload(i):
                qs_ = slice(i * 128, (i + 1) * 128)
                c.dma(Qg[i % 2][:], S['qT'][g * 768:(g + 1) * 768, qs_].rearrange("(h a p) t -> p h a t", a=2, p=96))
                for dc in range(2):
                    c.dma(Qd[i % 2][:, dc, :].rearrange("p (h q) -> p h q", q=128),
                          S['qT'][g * 768:(g + 1) * 768, qs_].rearrange("(h a p) t -> p h a t", a=2, p=96)[:, :, dc, :])
                c.dma(gbc[i % 2][:], S['ngT'][g * 12:(g + 1) * 12, qs_].partition_broadcast(128))
            qload(0)
            for i in range(NQ):
                qs = slice(i * 128, (i + 1) * 128)
                Q_ = Qg[i % 2]
                if i + 1 < NQ:
                    qload(i + 1)
                nv = 8 * i + 7
                c.memset(Pacc[:], 0.0, eng='pool')
                def cmp_head(h):
                    ps = rs_.get()
                    for dc in range(2):
                        c.mm(ps[:, 0:nv], Q_[:, h, dc, :], kcmpT[:, dc, g, 0:nv], start=(dc == 0), stop=(dc == 1))
                        yield
                    c.reduce(st[:, 4 * h + 0:4 * h + 1], ps[:, 0:nv], ALU.max)
                    yield
                    c.ts(st[:, 4 * h + 1:4 * h + 2], st[:, 4 * h + 0:4 * h + 1], -NSA_SCALE)
                    yield
                    e_ = ee[h]
                    c.act(e_[:, 0:nv], ps[:, 0:nv], AF.Exp, bias=st[:, 4 * h + 1:4 * h + 2], scale=NSA_SCALE)
                    yield
                    lo = max(nv - 8, 0)
                    j0 = 8 - (nv - lo)
                    c.tt(e_[:, lo:nv], e_[:, lo:nv], maskc[:, j0:8], ALU.mult)
                    yield
                    c.reduce(st[:, 4 * h + 2:4 * h + 3], e_[:, 0:nv], ALU.add)
                    yield
                    c.ts(st[:, 4 * h + 2:4 * h + 3], st[:, 4 * h + 2:4 * h + 3], 1e-30, None, ALU.max)
                    yield
                    c.recip(st[:, 4 * h + 3:4 * h + 4], st[:, 4 * h + 2:4 * h + 3])
                    yield
                    c.stt(Pacc[:, 1:1 + nv], e_[:, 0:nv], st[:, 4 * h + 3:4 * h + 4], Pacc[:, 1:1 + nv], ALU.mult, ALU.add)
                    yield
                    p_ = pb[h]
                    c.act(p_[:, 0:nv], e_[:, 0:nv], AF.Copy, scale=st[:, 4 * h + 3:4 * h + 4])
                    yield
                    t_ = pT[h]
                    nblk = (nv + 127) // 128
                    for nb in range(nblk):
                        w = min(128, nv - nb * 128)
                        pst = rm_.get()
                        c.mm(pst[0:w, 0:128], p_[:, nb * 128:nb * 128 + w], ident_b[:])
                        yield
                        c.copy(t_[0:w, nb, :], pst[0:w, 0:128], eng='dve')
                        yield
                    pso = rm_.get()
                    for nb in range(nblk):
                        w = min(128, nv - nb * 128)
                        c.mm(pso[:, 0:128], vcmp[0:w, nb, g, :], t_[0:w, nb, :], start=(nb == 0), stop=(nb == nblk - 1))
                        yield
                    c.tt(yn[:, h, :], pso[:, 0:128], gbc[i % 2][:, h * 3 + 0, :], ALU.mult)
                    yield
                drive([cmp_head(h) for h in range(4)])
                c.tt(imp[:, 0:n_s], Pacc[:, 0:4 * n_s:4], Pacc[:, 1:1 + 4 * n_s:4], ALU.add)
                for j in (2, 3, 4):
                    c.tt(imp[:, 0:n_s], imp[:, 0:n_s], Pacc[:, j:j + 4 * n_s:4], ALU.add)
                c.memset(score[:, 0:n_s], -1e30)
                if i > 0:
                    c.copy(score[:, 0:2 * i], imp[:, 0:2 * i])
                c.memset(score[:, 0:1], 1e6)
                c.memset(score[:, 2 * i:2 * i + 1], 1e6)
                c.memset(score[64:128, 2 * i + 1:2 * i + 2], 1e6)
                if i >= 1:
                    c.memset(score[0:64, 2 * i - 1:2 * i], 1e6)
                c.op('dve', lambda: nc.vector.max(m8[:, 0:8], score[:, 0:n_s]), [score[:, 0:n_s]], [m8[:, 0:8]])
                c.op('dve', lambda: nc.vector.match_replace(sc2[:, 0:n_s], m8[:, 0:8], score[:, 0:n_s], -3e38),
                     [m8[:, 0:8], score[:, 0:n_s]], [sc2[:, 0:n_s]])
                c.op('dve', lambda: nc.vector.max(m8[:, 8:16], sc2[:, 0:n_s]), [sc2[:, 0:n_s]], [m8[:, 8:16]])
                c.ts(sel[:, 0:n_s], score[:, 0:n_s], m8[:, 15:16], None, ALU.is_ge)
                c.ts(sel2[:, 0:n_s], score[:, 0:n_s], -5e29, None, ALU.is_gt)
                c.tt(sel[:, 0:n_s], sel[:, 0:n_s], sel2[:, 0:n_s], ALU.mult)
                c.ts(sel[:, 0:n_s], sel[:, 0:n_s], -1.0, -NEGM, ALU.add, ALU.mult)
                pst = rm_.get()
                c.tr(pst[0:n_s, 0:128], sel[:, 0:n_s], ident[:])
                c.copy(R[0:n_s, :].rearrange("p (h q) -> p h q", q=128), pst[0:n_s, 0:128].unsqueeze(1).broadcast_to([n_s, 4, 128]), eng='act')
                for h in range(4):
                    rowmax(i, h, ksT, 0, 128 * (i + 1), h)
                    rowmax(i, h, kwT, 128 * max(0, i - 4), 128 * (i + 1), 4 + h)
                pst = rm_.get()
                for h in range(4):
                    c.mm(pst[0:1, h * 128:(h + 1) * 128], negm[:, h:h + 1], ident[:])
                c.copy(R2[:], pst[0:1, :], eng='act')
                pst = rm_.get()
                for h in range(4):
                    c.mm(pst[0:1, h * 128:(h + 1) * 128], negm[:, 4 + h:5 + h], ident[:])
                c.copy(R2w[:], pst[0:1, :], eng='act')
                drive([dense_gen(i, g, ksT, vsk, 0, R2, True, 1, banks[4], banks[5], PTs[0:2], rl, ot),
                       dense_gen(i, g, kwT, vwk, max(0, i - 4), R2w, False, 2, banks[6], banks[7], PTs[2:4], rl2, ot2)])
                yb_ = ynb[i % 2]
                c.copy(yb_[:], yn[:], eng='act')
                c.dma(S['ynsT'][g * 512:(g + 1) * 512, qs].rearrange("(h p) q -> p h q", p=128), yb_[:], q='pool')


_NET = None


def kernel(**inputs):
    global _NET
    T = 4096
    if _NET is None:
        _NET = build(T=T, nlayers=DEPTH)
    net = _NET
    cc = host_consts()
    base = {}
    for k in net.inp:
        if k == 'x':
            continue
        if k in cc:
            base[k] = cc[k]
        else:
            base[k] = np.ascontiguousarray(np.asarray(inputs[k], dtype=np.float32))
    x = np.asarray(inputs['x'], dtype=np.float32)
    in_maps = []
    for core in range(8):
        m = dict(base)
        m['x'] = np.ascontiguousarray(x[core // 2])
        in_maps.append(m)
    res = run_bass_kernel_spmd(net.nc, in_maps, core_ids=list(range(8)))
    out = np.empty((4, T, D), np.float32)
    for b in range(4):
        out[b, :T // 2] = res.results[2 * b]['out'][:T // 2]
        out[b, T // 2:] = res.results[2 * b + 1]['out'][T // 2:]
    return out
```
